# Optimizing a Trainium2 kernel written in Bass

```python
import math
import jax
import jax.numpy as jnp
from jax import lax
import numpy as np

D_MODEL = 2048
BATCH = 2
SEQ = 16384
DEPTH = 2

GRID_W = 64
CTX_LEN = 256
NORM_EPS = 1e-6
ROPE_THETA = 10000.0
Q_BLOCK = 128
N_MOD = 6

LRU_WIDTH = 1024
LRU_BLOCKS = 8
LRU_BLOCK = LRU_WIDTH // LRU_BLOCKS
CONV_W = 4
LRU_C = 8.0

DIFF_HEADS = 8
DIFF_HEAD_DIM = 64
DIFF_V_DIM = 2 * DIFF_HEAD_DIM
DIFF_QK = DIFF_HEADS * 2 * DIFF_HEAD_DIM

HGRN_HEADS = 8
HGRN_DK = 128
HGRN_DV = 128
HGRN_WIDTH = HGRN_HEADS * HGRN_DK
CHUNK = 64

GQA_HEADS = 8
GQA_KV_HEADS = 2
GQA_REP = GQA_HEADS // GQA_KV_HEADS
GQA_HEAD_DIM = 128

FFN_HIDDEN = -(-(8 * D_MODEL) // (3 * 256)) * 256

AB_IN = 2 * LRU_WIDTH + 2 * DIFF_QK + DIFF_HEADS * DIFF_V_DIM
AB_OUT = LRU_WIDTH + DIFF_HEADS * DIFF_V_DIM
AB_SPLITS = (LRU_WIDTH, 2 * LRU_WIDTH, 2 * LRU_WIDTH + DIFF_QK, 2 * LRU_WIDTH + 2 * DIFF_QK)
CD_IN = 5 * HGRN_WIDTH + GQA_HEADS * GQA_HEAD_DIM + 2 * GQA_KV_HEADS * GQA_HEAD_DIM
CD_OUT = HGRN_WIDTH + GQA_HEADS * GQA_HEAD_DIM
CD_SPLITS = (HGRN_WIDTH, 2 * HGRN_WIDTH, 3 * HGRN_WIDTH, 4 * HGRN_WIDTH, 5 * HGRN_WIDTH,
             5 * HGRN_WIDTH + GQA_HEADS * GQA_HEAD_DIM,
             5 * HGRN_WIDTH + (GQA_HEADS + GQA_KV_HEADS) * GQA_HEAD_DIM)
N_EVEN = (DEPTH + 1) // 2
N_ODD = DEPTH // 2

kernel_name = 'hybrid_rglru_diffattn_hgrn2_gqa_dit'


def _rms_norm(x, w):
    xf = x.astype(jnp.float32)
    y = xf * lax.rsqrt(jnp.mean(xf * xf, axis=-1, keepdims=True) + NORM_EPS)
    return (y * w.astype(jnp.float32)).astype(x.dtype)


def _modulate(x, shift, scale):
    return x * (1.0 + scale) + shift


def _axial_rope(rows, head_dim):
    row = jnp.repeat(jnp.arange(rows, dtype=jnp.float32), GRID_W)
    col = jnp.tile(jnp.arange(GRID_W, dtype=jnp.float32), rows)
    n_freq = head_dim // 4
    inv = ROPE_THETA ** (-jnp.arange(n_freq, dtype=jnp.float32) / n_freq)
    ang = jnp.concatenate([row[:, None] * inv, col[:, None] * inv], axis=-1)
    return jnp.cos(ang), jnp.sin(ang)


def _apply_rope(x, cos, sin):
    shape = (cos.shape[0],) + (1,) * (x.ndim - 3) + (cos.shape[1],)
    cs = cos.reshape(shape).astype(x.dtype)
    sn = sin.reshape(shape).astype(x.dtype)
    x1, x2 = jnp.split(x, 2, axis=-1)
    return jnp.concatenate([x1 * cs - x2 * sn, x2 * cs + x1 * sn], axis=-1)


def _centred_dwconv(x, w, b):
    pad_l = (CONV_W - 1) // 2
    xp = jnp.pad(x, ((0, 0), (pad_l, CONV_W - 1 - pad_l), (0, 0)))
    T = x.shape[1]
    y = xp[:, 0:T] * w[0]
    for k in range(1, CONV_W):
        y = y + xp[:, k:k + T] * w[k]
    return y + b


def _query_blocks(fn, q):
    B, T = q.shape[:2]
    nb = T // Q_BLOCK
    qb = q.reshape((B, nb, Q_BLOCK) + q.shape[2:]).swapaxes(0, 1)
    ob = lax.map(fn, qb)
    return ob.swapaxes(0, 1).reshape((B, T) + ob.shape[3:])


def _rglru_gates(x, wa, ba, wx, bx, lam):
    B, T, _ = x.shape
    xb = x.reshape(B, T, LRU_BLOCKS, LRU_BLOCK)
    r = jax.nn.sigmoid((jnp.einsum('bthi,hij->bthj', xb, wa).reshape(B, T, LRU_WIDTH) + ba).astype(jnp.float32))
    i = jax.nn.sigmoid((jnp.einsum('bthi,hij->bthj', xb, wx).reshape(B, T, LRU_WIDTH) + bx).astype(jnp.float32))
    log_a = -LRU_C * r * jax.nn.softplus(-lam.astype(jnp.float32))
    a = jnp.exp(log_a)
    b = jnp.sqrt(-jnp.expm1(2.0 * log_a)) * (i * x.astype(jnp.float32))
    return a, b


def _linear_scan(a, b, h0):
    b = b.at[:, 0].add(a[:, 0] * h0)
    def combine(lft, rgt):
        return (lft[0] * rgt[0], rgt[0] * lft[1] + rgt[1])
    _, h = lax.associative_scan(combine, (a, b), axis=1)
    return h


def _rglru_bidir(x_lat, x_ctx, wa, ba, wx, bx, lam):
    outs_l, outs_c = [], []
    for d in range(2):
        flip = d == 1
        xc = x_ctx[:, ::-1] if flip else x_ctx
        xl = x_lat[:, ::-1] if flip else x_lat
        a_c, b_c = _rglru_gates(xc, wa[d], ba[d], wx[d], bx[d], lam[d])
        h_c = _linear_scan(a_c, b_c, jnp.zeros_like(b_c[:, 0]))
        a_l, b_l = _rglru_gates(xl, wa[d], ba[d], wx[d], bx[d], lam[d])
        h_l = _linear_scan(a_l, b_l, h_c[:, -1])
        outs_c.append(h_c[:, ::-1] if flip else h_c)
        outs_l.append(h_l[:, ::-1] if flip else h_l)
    return (outs_l[0] + outs_l[1]).astype(x_lat.dtype), (outs_c[0] + outs_c[1]).astype(x_ctx.dtype)


def _diff_attend(qb, kk, vv, lam):
    s = jnp.einsum('bqhmd,bkhmd->bhmqk', qb, kk).astype(jnp.float32) * (DIFF_HEAD_DIM ** -0.5)
    p = jax.nn.softmax(s, axis=-1)
    pd = p[:, :, 0] - lam * p[:, :, 1]
    return jnp.einsum('bhqk,bkhe->bqhe', pd.astype(vv.dtype), vv)


def _diff_post(o, w, lambda_init):
    B, T = o.shape[:2]
    return (_rms_norm(o, w) * (1.0 - lambda_init)).reshape(B, T, -1)


def _mixer_ab(h_lat, h_ctx, cos, sin, w_in, w_out, conv_w, conv_b, wa, ba, wx, bx, lam_param,
              lq1, lk1, lq2, lk2, subln_w, lambda_init, with_ctx_out):
    def project(h):
        B, T, _ = h.shape
        g, xr, q, k, v = jnp.split(h @ w_in, AB_SPLITS, axis=-1)
        q = q.reshape(B, T, DIFF_HEADS, 2, DIFF_HEAD_DIM)
        k = k.reshape(B, T, DIFF_HEADS, 2, DIFF_HEAD_DIM)
        v = v.reshape(B, T, DIFF_HEADS, DIFF_V_DIM)
        return g, _centred_dwconv(xr, conv_w, conv_b), q, k, v

    g_l, x_l, q_l, k_l, v_l = project(h_lat)
    g_c, x_c, q_c, k_c, v_c = project(h_ctx)
    q_l = _apply_rope(q_l, cos, sin)
    k_l = _apply_rope(k_l, cos, sin)
    r_l, r_c = _rglru_bidir(x_l, x_c, wa, ba, wx, bx, lam_param)
    a_l = r_l * jax.nn.gelu(g_l)
    f32 = jnp.float32
    lam = (jnp.exp(jnp.sum(lq1.astype(f32) * lk1.astype(f32)))
           - jnp.exp(jnp.sum(lq2.astype(f32) * lk2.astype(f32))) + lambda_init)
    k_all = jnp.concatenate([k_l, k_c], axis=1)
    v_all = jnp.concatenate([v_l, v_c], axis=1)
    d_l = _diff_post(_query_blocks(lambda qb: _diff_attend(qb, k_all, v_all, lam), q_l), subln_w, lambda_init)
    y_lat = jnp.concatenate([a_l, d_l], axis=-1) @ w_out
    y_ctx = None
    if with_ctx_out:
        a_c = r_c * jax.nn.gelu(g_c)
        d_c = _diff_post(_diff_attend(q_c, k_c, v_c, lam), subln_w, lambda_init)
        y_ctx = jnp.concatenate([a_c, d_c], axis=-1) @ w_out
    return y_lat, y_ctx


def _gla_chunked(q, k, v, log_f, s0):
    B, T, H, dk = q.shape
    nc = T // CHUNK
    def to_chunks(t):
        return t.reshape(B, nc, CHUNK, H, t.shape[-1]).transpose(1, 0, 3, 2, 4)
    tril = jnp.tril(jnp.ones((CHUNK, CHUNK), dtype=bool))[:, :, None]

    def body(S, inp):
        qc, kc, vc, gc = inp
        b = jnp.cumsum(gc, axis=2)
        o_inter = jnp.einsum('bhtk,bhkv->bhtv', qc * jnp.exp(b), S)
        rel = b[:, :, :, None, :] - b[:, :, None, :, :]
        decay = jnp.exp(jnp.where(tril, rel, -jnp.inf))
        scores = jnp.einsum('bhtk,bhtsk,bhsk->bhts', qc, decay, kc)
        o_intra = jnp.einsum('bhts,bhsv->bhtv', scores, vc)
        b_last = b[:, :, -1:, :]
        S_new = (jnp.exp(b_last)[:, :, 0, :, None] * S
                 + jnp.einsum('bhsk,bhsv->bhkv', kc * jnp.exp(b_last - b), vc))
        return S_new, o_inter + o_intra

    S_fin, o = lax.scan(body, s0, (to_chunks(q), to_chunks(k), to_chunks(v), to_chunks(log_f)))
    return o.transpose(1, 0, 3, 2, 4).reshape(B, T, H, v.shape[-1]), S_fin


def _hgrn2_prep(q, f_raw, i, lb, flip):
    B, T, _ = q.shape
    f = lb + (1.0 - lb) * jax.nn.sigmoid(f_raw.astype(jnp.float32))
    heads = lambda t: t.reshape(B, T, HGRN_HEADS, -1)
    arrs = [heads(q.astype(jnp.float32)), heads(1.0 - f), heads(i.astype(jnp.float32)), heads(jnp.log(f))]
    if flip:
        arrs = [a[:, ::-1] for a in arrs]
    return arrs


def _hgrn2_bidir(q_l, f_l, i_l, q_c, f_c, i_c, lb):
    outs_l, outs_c = [], []
    B = q_c.shape[0]
    for d in range(2):
        flip = d == 1
        s0 = jnp.zeros((B, HGRN_HEADS, HGRN_DK, HGRN_DV), jnp.float32)
        o_c, s_c = _gla_chunked(*_hgrn2_prep(q_c, f_c[d], i_c, lb[d], flip), s0)
        o_l, _ = _gla_chunked(*_hgrn2_prep(q_l, f_l[d], i_l, lb[d], flip), s_c)
        outs_c.append(o_c[:, ::-1] if flip else o_c)
        outs_l.append(o_l[:, ::-1] if flip else o_l)
    return outs_l[0] + outs_l[1], outs_c[0] + outs_c[1]


def _hgrn_post(o, g, w):
    B, T = g.shape[:2]
    return (_rms_norm(o, w).reshape(B, T, -1) * jax.nn.silu(g.astype(jnp.float32))).astype(g.dtype)


def _gqa_attend(qb, kk, vv):
    B, Q = qb.shape[:2]
    qg = qb.reshape(B, Q, GQA_KV_HEADS, GQA_REP, GQA_HEAD_DIM)
    s = jnp.einsum('bqgrd,bkgd->bgrqk', qg, kk).astype(jnp.float32) * (GQA_HEAD_DIM ** -0.5)
    p = jax.nn.softmax(s, axis=-1)
    o = jnp.einsum('bgrqk,bkgd->bqgrd', p.astype(vv.dtype), vv)
    return o.reshape(B, Q, GQA_HEADS * GQA_HEAD_DIM)


def _mixer_cd(h_lat, h_ctx, cos, sin, w_in, w_out, lb, hgrn_norm_w, q_norm_w, k_norm_w, with_ctx_out):
    def project(h):
        B, T, _ = h.shape
        qh, ff, fb, ih, gh, qa, ka, va = jnp.split(h @ w_in, CD_SPLITS, axis=-1)
        qa = _rms_norm(qa.reshape(B, T, GQA_HEADS, GQA_HEAD_DIM), q_norm_w)
        ka = _rms_norm(ka.reshape(B, T, GQA_KV_HEADS, GQA_HEAD_DIM), k_norm_w)
        va = va.reshape(B, T, GQA_KV_HEADS, GQA_HEAD_DIM)
        return jax.nn.silu(qh), (ff, fb), ih, gh, qa, ka, va

    qh_l, f_l, i_l, g_l, qa_l, ka_l, va_l = project(h_lat)
    qh_c, f_c, i_c, g_c, qa_c, ka_c, va_c = project(h_ctx)
    o_l, o_c = _hgrn2_bidir(qh_l, f_l, i_l, qh_c, f_c, i_c, lb)
    c_l = _hgrn_post(o_l, g_l, hgrn_norm_w)
    qa_l = _apply_rope(qa_l, cos, sin)
    ka_l = _apply_rope(ka_l, cos, sin)
    k_all = jnp.concatenate([ka_l, ka_c], axis=1)
    v_all = jnp.concatenate([va_l, va_c], axis=1)
    att_l = _query_blocks(lambda qb: _gqa_attend(qb, k_all, v_all), qa_l)
    y_lat = jnp.concatenate([c_l, att_l], axis=-1) @ w_out
    y_ctx = None
    if with_ctx_out:
        c_c = _hgrn_post(o_c, g_c, hgrn_norm_w)
        att_c = _gqa_attend(qa_c, ka_c, va_c)
        y_ctx = jnp.concatenate([c_c, att_c], axis=-1) @ w_out
    return y_lat, y_ctx


def _swiglu(h, w_gate, w_up, w_down):
    return (jax.nn.silu(h @ w_gate) * (h @ w_up)) @ w_down


def setup_inputs(seed: int = 0) -> dict:
    key = jax.random.key(seed)
    keys = list(jax.random.split(key, 40))
    f32 = jnp.float32

    def dense(shape, fan_in):
        return jax.random.normal(keys.pop(), shape, f32) * fan_in ** -0.5

    def gain(shape):
        return 1.0 + 0.02 * jax.random.normal(keys.pop(), shape, f32)

    def small(shape, s=0.02):
        return s * jax.random.normal(keys.pop(), shape, f32)

    a_base = jax.random.uniform(keys.pop(), (N_EVEN, 2, LRU_WIDTH), f32, 0.9, 0.999) ** (1.0 / LRU_C)
    lru_lambda = jnp.log(a_base) - jnp.log1p(-a_base)

    return {
        'x': jax.random.normal(keys.pop(), (BATCH, SEQ, D_MODEL), f32),
        'c': jax.random.normal(keys.pop(), (BATCH, D_MODEL), f32),
        'ctx': jax.random.normal(keys.pop(), (BATCH, CTX_LEN, D_MODEL), f32),
        'c_ctx': jax.random.normal(keys.pop(), (D_MODEL,), f32),
        'mod_w': dense((DEPTH, D_MODEL, N_MOD * D_MODEL), D_MODEL),
        'mod_b': small((DEPTH, N_MOD * D_MODEL)),
        'norm_mix_w': gain((DEPTH, D_MODEL)),
        'norm_ffn_w': gain((DEPTH, D_MODEL)),
        'ffn_w_gate': dense((DEPTH, D_MODEL, FFN_HIDDEN), D_MODEL),
        'ffn_w_up': dense((DEPTH, D_MODEL, FFN_HIDDEN), D_MODEL),
        'ffn_w_down': dense((DEPTH, FFN_HIDDEN, D_MODEL), FFN_HIDDEN),
        'ab_w_in': dense((N_EVEN, D_MODEL, AB_IN), D_MODEL),
        'ab_w_out': dense((N_EVEN, AB_OUT, D_MODEL), AB_OUT),
        'lru_conv_w': dense((N_EVEN, CONV_W, LRU_WIDTH), CONV_W),
        'lru_conv_b': small((N_EVEN, LRU_WIDTH)),
        'lru_wa': dense((N_EVEN, 2, LRU_BLOCKS, LRU_BLOCK, LRU_BLOCK), LRU_BLOCK),
        'lru_ba': small((N_EVEN, 2, LRU_WIDTH)),
        'lru_wx': dense((N_EVEN, 2, LRU_BLOCKS, LRU_BLOCK, LRU_BLOCK), LRU_BLOCK),
        'lru_bx': small((N_EVEN, 2, LRU_WIDTH)),
        'lru_lambda': lru_lambda,
        'diff_lq1': small((N_EVEN, DIFF_HEAD_DIM), 0.1),
        'diff_lk1': small((N_EVEN, DIFF_HEAD_DIM), 0.1),
        'diff_lq2': small((N_EVEN, DIFF_HEAD_DIM), 0.1),
        'diff_lk2': small((N_EVEN, DIFF_HEAD_DIM), 0.1),
        'diff_subln_w': gain((N_EVEN, DIFF_V_DIM)),
        'cd_w_in': dense((N_ODD, D_MODEL, CD_IN), D_MODEL),
        'cd_w_out': dense((N_ODD, CD_OUT, D_MODEL), CD_OUT),
        'hgrn_lb_logits': small((2, DEPTH, HGRN_WIDTH), 0.5),
        'hgrn_norm_w': gain((N_ODD, HGRN_DV)),
        'gqa_q_norm_w': gain((N_ODD, GQA_HEAD_DIM)),
        'gqa_k_norm_w': gain((N_ODD, GQA_HEAD_DIM)),
        'final_norm_w': gain((D_MODEL,)),
    }


def reference(x, c, ctx, c_ctx, mod_w, mod_b, norm_mix_w, norm_ffn_w, ffn_w_gate, ffn_w_up, ffn_w_down,
              ab_w_in, ab_w_out, lru_conv_w, lru_conv_b, lru_wa, lru_ba, lru_wx, lru_bx, lru_lambda,
              diff_lq1, diff_lk1, diff_lq2, diff_lk2, diff_subln_w,
              cd_w_in, cd_w_out, hgrn_lb_logits, hgrn_norm_w, gqa_q_norm_w, gqa_k_norm_w, final_norm_w):
    ROWS = x.shape[1] // GRID_W
    cos_b, sin_b = _axial_rope(ROWS, DIFF_HEAD_DIM)
    cos_d, sin_d = _axial_rope(ROWS, GQA_HEAD_DIM)
    lb_cum = jnp.cumsum(jax.nn.softmax(hgrn_lb_logits.astype(jnp.float32), axis=1), axis=1)
    silu_c = jax.nn.silu(c)
    silu_cc = jax.nn.silu(c_ctx)

    for l in range(DEPTH):
        last = l == DEPTH - 1
        m_lat = jnp.split((silu_c @ mod_w[l] + mod_b[l])[:, None, :], N_MOD, axis=-1)
        m_ctx = jnp.split((silu_cc @ mod_w[l] + mod_b[l])[None, None, :], N_MOD, axis=-1)
        h_lat = _modulate(_rms_norm(x, norm_mix_w[l]), m_lat[0], m_lat[1])
        h_ctx = _modulate(_rms_norm(ctx, norm_mix_w[l]), m_ctx[0], m_ctx[1])
        if l % 2 == 0:
            e = l // 2
            lambda_init = 0.8 - 0.6 * math.exp(-0.3 * l)
            y_lat, y_ctx = _mixer_ab(h_lat, h_ctx, cos_b, sin_b, ab_w_in[e], ab_w_out[e], lru_conv_w[e], lru_conv_b[e],
                                     lru_wa[e], lru_ba[e], lru_wx[e], lru_bx[e], lru_lambda[e],
                                     diff_lq1[e], diff_lk1[e], diff_lq2[e], diff_lk2[e], diff_subln_w[e],
                                     lambda_init, not last)
        else:
            o = l // 2
            lb = lb_cum[:, l] - lb_cum[:, 0]
            y_lat, y_ctx = _mixer_cd(h_lat, h_ctx, cos_d, sin_d, cd_w_in[o], cd_w_out[o], lb,
                                     hgrn_norm_w[o], gqa_q_norm_w[o], gqa_k_norm_w[o], not last)
        x = x + m_lat[2] * y_lat
        x = x + m_lat[5] * _swiglu(_modulate(_rms_norm(x, norm_ffn_w[l]), m_lat[3], m_lat[4]),
                                   ffn_w_gate[l], ffn_w_up[l], ffn_w_down[l])
        if not last:
            ctx = ctx + m_ctx[2] * y_ctx
            ctx = ctx + m_ctx[5] * _swiglu(_modulate(_rms_norm(ctx, norm_ffn_w[l]), m_ctx[3], m_ctx[4]),
                                           ffn_w_gate[l], ffn_w_up[l], ffn_w_down[l])

    return _rms_norm(x, final_norm_w)
```

```python
import math
from contextlib import ExitStack
import numpy as np
import ml_dtypes
import concourse.bass as bass
import concourse.mybir as mybir
from concourse.bass_utils import run_bass_kernel_spmd

F32 = mybir.dt.float32
BF16 = mybir.dt.bfloat16
AF = mybir.ActivationFunctionType
ALU = mybir.AluOpType
AX = mybir.AxisListType

SEM_LIMIT = 30000

D = 2048
NCH = 16
FF = 5632
NJ = 44
EPS = 1e-6


class T:
    __slots__ = ("t", "w", "r", "dsem", "dcnt", "name", "ap")

    def __init__(self, t, name, ap=None):
        self.t = t
        self.name = name
        self.w = None
        self.r = []
        self.dsem = None
        self.dcnt = 0
        self.ap = ap

    def __getitem__(self, idx):
        if self.ap is not None:
            return self.ap[idx]
        return self.t[idx]


class Prog:
    ENGS = ("pe", "act", "dve", "pool", "sp")

    def __init__(self, nc):
        self.nc = nc
        self.es = ExitStack()
        self.q = {e: [] for e in self.ENGS}
        self.cnt = {e: 0 for e in self.ENGS}
        self.sem = {}
        self.semobj = {}
        self.nsem = 0
        self.waited = {e: {} for e in self.ENGS}
        self.ninst = 0
        self.nm = 0
        self.scopes = []
        self.inherit = {}
        self.free_dsems = []
        for e in ("pe", "act", "dve", "pool"):
            self._new_eng_sem(e)

    def _alloc_sem(self, name):
        self.nsem += 1
        key = "%s_%d" % (name, self.nsem)
        s = self.es.enter_context(self.nc.semaphore(key))
        self.semobj[key] = s
        return key

    def _new_eng_sem(self, e):
        self.sem[e] = self._alloc_sem("s_" + e)
        self.cnt[e] = 0

    def uniq(self, name):
        self.nm += 1
        return "%s_%d" % (name, self.nm)

    def sbuf(self, name, shape, dt):
        name = self.uniq(name)
        st = self.scopes[-1][0] if self.scopes else self.es
        t = st.enter_context(self.nc.sbuf_tensor(name, list(shape), dt))
        tt = T(t, name)
        tt.r = list(self.inherit.items())
        if self.scopes:
            self.scopes[-1][1].append(tt)
        return tt

    def open_scope(self):
        self.scopes.append((ExitStack(), []))

    def close_scope(self):
        st, lst = self.scopes.pop()
        for t in lst:
            deps = list(t.r)
            if t.w is not None:
                deps.append(t.w)
            for k, v in deps:
                if self.inherit.get(k, 0) < v:
                    self.inherit[k] = v
            if t.dsem is not None:
                self.free_dsems.append((t.dsem, t.dcnt))
                t.dsem = None
        st.close()

    def psum(self, name, shape, dt=F32):
        name = self.uniq(name)
        t = self.es.enter_context(self.nc.psum_tensor(name, list(shape), dt))
        return T(t, name)

    def dram(self, name, shape, dt, kind="Internal"):
        t = self.nc.dram_tensor(name, list(shape), dt, kind=kind)
        tt = T(t, name, ap=t.ap())
        if self.scopes and kind == "Internal":
            self.scopes[-1][1].append(tt)
        return tt

    def _deps(self, reads, writes):
        need = {}
        for t in reads:
            if t.w is not None:
                s, v = t.w
                if need.get(s, 0) < v:
                    need[s] = v
        for t in writes:
            if t.w is not None:
                s, v = t.w
                if need.get(s, 0) < v:
                    need[s] = v
            for (s, v) in t.r:
                if need.get(s, 0) < v:
                    need[s] = v
        return need

    def _emit_waits(self, e, need):
        wd = self.waited[e]
        for s, v in need.items():
            if wd.get(s, 0) >= v:
                continue
            wd[s] = v
            so = self.semobj[s]
            self.q[e].append(lambda en, so=so, v=v: en.wait_ge(so, v))
            self.ninst += 1

    def _mark(self, dep, reads, writes):
        for t in writes:
            t.w = dep
            t.r = []
        for t in reads:
            t.r.append(dep)
            if len(t.r) > 32:
                m = {}
                for s, v in t.r:
                    if m.get(s, 0) < v:
                        m[s] = v
                t.r = list(m.items())

    def op(self, e, fn, reads=(), writes=(), **kw):
        need = self._deps(reads, writes)
        if e == "pe":
            need = {k: v for k, v in need.items() if not k.startswith("s_pe")}
        self._emit_waits(e, need)
        if self.cnt[e] >= SEM_LIMIT:
            self._new_eng_sem(e)
        self.cnt[e] += 1
        key = self.sem[e]
        so = self.semobj[key]
        self.q[e].append(lambda en, fn=fn, so=so, kw=kw: getattr(en, fn)(**kw).then_inc(so, 1))
        self.ninst += 1
        dep = (key, self.cnt[e])
        self._mark(dep, reads, writes)
        return dep

    def dma(self, e, out, in_, reads=(), writes=(), **kw):
        need = self._deps(reads, writes)
        self._emit_waits(e, need)
        d = writes[0]
        if d.dsem is None and self.free_dsems:
            d.dsem, d.dcnt = self.free_dsems.pop()
        if d.dsem is None or d.dcnt >= SEM_LIMIT * 16:
            d.dsem = self._alloc_sem("d_" + d.name)
            d.dcnt = 0
        d.dcnt += 16
        so = self.semobj[d.dsem]
        self.q[e].append(
            lambda en, so=so, out=out, in_=in_, kw=kw: en.dma_start(out=out, in_=in_, **kw).then_inc(so, 16))
        self.ninst += 1
        dep = (d.dsem, d.dcnt)
        self._mark(dep, reads, writes)
        return dep

    def coll(self, src_t, src_ap, dst_t, dst_ap, groups):
        need = self._deps([src_t], [dst_t])
        self._emit_waits("pool", need)
        d = dst_t
        if d.dsem is None:
            d.dsem = self._alloc_sem("c_" + d.name)
            d.dcnt = 0
        d.dcnt += 1
        so = self.semobj[d.dsem]
        self.q["pool"].append(lambda en, so=so, a=src_ap, b=dst_ap, g=groups: en.collective_compute(
            "AllGather", ALU.bypass, replica_groups=g, ins=[a], outs=[b]).then_inc(so))
        self.ninst += 1
        dep = (d.dsem, d.dcnt)
        self._mark(dep, [src_t], [dst_t])
        return dep

    def finish(self, out_tiles):
        need = {}
        for t in out_tiles:
            if t.w is not None:
                s, v = t.w
                if need.get(s, 0) < v:
                    need[s] = v
        self._emit_waits("sp", need)
        q = self.q
        with self.nc.Block() as block:
            @block.sync
            def _(en):
                for f in q["sp"]:
                    f(en)

            @block.tensor
            def _(en):
                for f in q["pe"]:
                    f(en)

            @block.scalar
            def _(en):
                for f in q["act"]:
                    f(en)

            @block.vector
            def _(en):
                for f in q["dve"]:
                    f(en)

            @block.gpsimd
            def _(en):
                for f in q["pool"]:
                    f(en)
        self.es.close()


class Ctx:
    def __init__(self, p):
        self.p = p
        self.banks = [p.psum("bank%d" % i, [128, 512]) for i in range(6)]
        self.bank_i = 0
        self.pmisc = p.psum("pmisc", [128, 512])
        self.banks.append(p.psum("bank6", [128, 512]))
        self.ones_bf = p.sbuf("ones_bf", [128, 128], BF16)
        p.op("dve", "memset", ap=self.ones_bf[:], constant=1.0, writes=[self.ones_bf])
        self.epsb = p.sbuf("epsb", [128, 1], F32)
        p.op("dve", "memset", ap=self.epsb[:], constant=EPS, writes=[self.epsb])

    def bank(self):
        b = self.banks[self.bank_i % len(self.banks)]
        self.bank_i += 1
        return b

    def bankbf(self):
        self.bf_i += 1
        return self.bft[self.bf_i % 2]


def mm(p, out_t, out_ap, lhsT_t, lhsT_ap, rhs_t, rhs_ap, start, stop):
    p.op("pe", "matmul", out=out_ap, lhsT=lhsT_ap, rhs=rhs_ap, start=start, stop=stop,
         reads=[lhsT_t, rhs_t], writes=[out_t])


def compute_mod(p, cx, svec_d, modw_d, modb_d, ncomp, mod_sb):
    sv = p.sbuf("sv", [128, NCH, 2], F32)
    sil = p.sbuf("sil", [128, NCH, 2], F32)
    mb = p.sbuf("modb", [128, ncomp, NCH], F32)
    p.dma("sp", sv[:], svec_d[:, :, :], writes=[sv])
    p.dma("sp", mb[:], modb_d[:, :, :], writes=[mb])
    p.op("act", "activation", out=sil[:], in_=sv[:], func=AF.Silu, reads=[sv], writes=[sil])
    wb = [p.sbuf("modwb%d" % i, [128, NCH, 128], F32) for i in range(2)]
    k = 0
    for comp in range(ncomp):
        for oc in range(NCH):
            w = wb[k % 2]
            k += 1
            p.dma("sp", w[:], modw_d[comp, oc], writes=[w])
            ps = cx.bank()
            for c in range(NCH):
                mm(p, ps, ps[:, 0:2], w, w[:, c, :], sil, sil[:, c, :], c == 0, c == NCH - 1)
            p.op("dve", "tensor_scalar", out=mod_sb[:, comp, oc, :], in0=ps[:, 0:2], scalar1=mb[:, comp, oc:oc + 1], scalar2=None,
                op0=ALU.add, reads=[ps, mb], writes=[mod_sb])


def rms_stats(p, cx, x_t, x_ap3, W, sq_t, rstd_t):
    p.op("act", "activation", out=sq_t[:, 0:NCH, :W], in_=x_ap3, func=AF.Square, reads=[x_t], writes=[sq_t])
    ps = cx.bank()
    for c in range(NCH):
        mm(p, ps, ps[:, :W], cx.ones_bf, cx.ones_bf[:], sq_t, sq_t[:, c, :W], c == 0, c == NCH - 1)
    p.op("act", "activation", out=rstd_t[:, :W], in_=ps[:, :W], func=AF.Sqrt, scale=1.0 / D,
                                       bias=cx.epsb[:], reads=[ps, cx.epsb], writes=[rstd_t])
    p.op("dve", "reciprocal", out=rstd_t[:, :W], in_=rstd_t[:, :W], reads=[rstd_t], writes=[rstd_t])


def norm_mod(p, cx, x_t, x3, W, rstd_t, tmp_t, A_t, A_ap, B_t, B_ap, h_t, h3):
    for c in range(NCH):
        tt = tmp_t[c % 2]
        p.op("dve", "tensor_tensor", out=tt[:, :W], in0=x3(c), in1=rstd_t[:, :W], op=ALU.mult,
             reads=[x_t, rstd_t], writes=[tt])
        p.op("act", "activation", out=h3(c), in_=tt[:, :W], func=AF.Identity,
                                                       scale=A_ap(c), bias=B_ap(c),
             reads=[tt, A_t, B_t], writes=[h_t])


def prep_B_weights(p, d, q="pool"):
    wout_b = p.dram(p.uniq("wout_b"), [NCH, 128, NCH, 128], BF16)
    wgu_b = p.dram(p.uniq("wgu_b"), [NJ, 128, 2, NCH, 128], BF16)
    wdn_b = p.dram(p.uniq("wdn_b"), [NCH, 128, NJ, 128], BF16)
    for o in range(0, NCH, 4):
        p.dma(q, wout_b[o:o + 4], d["wout"][o:o + 4], reads=[d["wout"]], writes=[wout_b])
    for j in range(0, NJ, 4):
        p.dma(q, wgu_b[j:j + 4], d["wgu"][j:j + 4], reads=[d["wgu"]], writes=[wgu_b])
    for o in range(0, NCH, 2):
        p.dma(q, wdn_b[o:o + 2], d["wdn"][o:o + 2], reads=[d["wdn"]], writes=[wdn_b])
    d["wcast"] = (wout_b, wgu_b, wdn_b)


def build_B(p, cx, d, tiles, last):
    if "wcast" not in d:
        prep_B_weights(p, d)
    wout_b, wgu_b, wdn_b = d["wcast"]

    mod = p.sbuf("modB", [128, 4, NCH, 2], F32)
    compute_mod(p, cx, d["svec"], d["modw"], d["modb"], 4, mod)
    nw = p.sbuf("nwB", [128, NCH], F32)
    p.dma("sp", nw[:], d["normw"][:, :], writes=[nw])
    A2 = p.sbuf("A2", [128, NCH, 2], F32)
    for col in range(2):
        p.op("dve", "scalar_tensor_tensor", out=A2[:, :, col], in0=mod[:, 2, :, col], scalar=1.0, in1=nw[:], op0=ALU.add, op1=ALU.mult,
            reads=[mod, nw], writes=[A2])
    if last:
        fw = p.sbuf("fwB", [128, NCH], F32)
        p.dma("sp", fw[:], d["finalw"][:, :], writes=[fw])

    TW = 512
    xr_t = [p.sbuf("xresB%d" % i, [128, NCH, TW], F32) for i in range(1)]
    mx_t = [p.sbuf("mixB%d" % i, [128, NCH, TW], BF16) for i in range(1)]
    h_t = p.sbuf("hB", [128, NCH, TW], BF16)
    u_t = p.sbuf("uB", [128, NJ, TW], BF16)
    sq_t = u_t
    rstd_t = p.sbuf("rstdB", [128, TW], F32)
    tmp_t = [p.sbuf("tmpB%d" % i, [128, TW], F32) for i in range(2)]
    sg_t = [p.sbuf("sgB%d" % i, [128, TW], F32) for i in range(2)]
    wo_t = [p.sbuf("woB%d" % i, [128, NCH, 128], BF16) for i in range(2)]
    wg_t = [p.sbuf("wgB%d" % i, [128, 2, NCH, 128], BF16) for i in range(3)]
    wd_t = [p.sbuf("wdB%d" % i, [128, NJ, 128], BF16) for i in range(2)]
    ko = kg = kd = 0

    for ti, (c0, W, mc) in enumerate(tiles):
        x1 = xr_t[0]
        mx = mx_t[0]
        if "ld_x" in d:
            d["ld_x"](x1, c0, W, mc)
        else:
            p.dma("sp", x1[:, :, :W], d["xres"][:, :, c0:c0 + W], reads=[d["xres"]], writes=[x1])
        if "ld_mix" in d:
            d["ld_mix"](mx, c0, W, mc)
        else:
            p.dma("sp", mx[:, :, :W], d["mix"][:, :, c0:c0 + W], reads=[d["mix"]], writes=[mx])
        for o in range(NCH):
            w = wo_t[ko % 2]
            ko += 1
            p.dma("act", w[:], wout_b[o], reads=[wout_b], writes=[w])
            ps = cx.bank()
            for c in range(NCH):
                mm(p, ps, ps[:, :W], w, w[:, c, :], mx, mx[:, c, :W], c == 0, c == NCH - 1)
            p.op("dve", "scalar_tensor_tensor", out=x1[:, o, :W], in0=ps[:, :W], scalar=mod[:, 0, o, mc:mc + 1], in1=x1[:, o, :W],
                op0=ALU.mult, op1=ALU.add, reads=[ps, mod, x1], writes=[x1])
        rms_stats(p, cx, x1, x1[:, :, :W], W, sq_t, rstd_t)
        norm_mod(p, cx, x1, lambda c: x1[:, c, :W], W, rstd_t, tmp_t,
                 A2, lambda c: A2[:, c, mc:mc + 1], mod, lambda c: mod[:, 1, c, mc:mc + 1],
                 h_t, lambda c: h_t[:, c, :W])
        for j in range(NJ):
            w = wg_t[kg % 3]
            kg += 1
            p.dma("sp", w[:], wgu_b[j], reads=[wgu_b], writes=[w])
            pg = cx.bank()
            pu = cx.bank()
            for c in range(NCH):
                mm(p, pg, pg[:, :W], w, w[:, 0, c, :], h_t, h_t[:, c, :W], c == 0, c == NCH - 1)
            for c in range(NCH):
                mm(p, pu, pu[:, :W], w, w[:, 1, c, :], h_t, h_t[:, c, :W], c == 0, c == NCH - 1)
            sg = sg_t[j % 2]
            p.op("act", "activation", out=sg[:, :W], in_=pg[:, :W], func=AF.Silu,
                 reads=[pg], writes=[sg])
            p.op("dve", "tensor_tensor", out=u_t[:, j, :W], in0=pu[:, :W], in1=sg[:, :W],
                                                                  op=ALU.mult, reads=[pu, sg], writes=[u_t])
        for o in range(NCH):
            w = wd_t[kd % 2]
            kd += 1
            p.dma("act", w[:], wdn_b[o], reads=[wdn_b], writes=[w])
            ps = cx.bank()
            for j in range(NJ):
                mm(p, ps, ps[:, :W], w, w[:, j, :], u_t, u_t[:, j, :W], j == 0, j == NJ - 1)
            p.op("dve", "scalar_tensor_tensor", out=x1[:, o, :W], in0=ps[:, :W], scalar=mod[:, 3, o, mc:mc + 1], in1=x1[:, o, :W],
                op0=ALU.mult, op1=ALU.add, reads=[ps, mod, x1], writes=[x1])
        if last:
            rms_stats(p, cx, x1, x1[:, :, :W], W, sq_t, rstd_t)
            for c in range(NCH):
                p.op("dve", "scalar_tensor_tensor", out=x1[:, c, :W], in0=x1[:, c, :W], scalar=fw[:, c:c + 1], in1=rstd_t[:, :W],
                    op0=ALU.mult, op1=ALU.mult, reads=[x1, fw, rstd_t], writes=[x1])
        if "st_x" in d:
            d["st_x"](x1, c0, W, mc)
        else:
            p.dma("pool", d["xout"][:, :, c0:c0 + W], x1[:, :, :W], reads=[x1], writes=[d["xout"]])


def make_B(NT, tiles, last):
    nc = bass.Bass("TRN2", target_bir_lowering=False)
    p = Prog(nc)
    cx = Ctx(p)
    d = {
        "xres": p.dram("xres", [128, NCH, NT], F32, kind="ExternalInput"),
        "mix": p.dram("mix", [128, NCH, NT], BF16, kind="ExternalInput"),
        "svec": p.dram("svec", [128, NCH, 2], F32, kind="ExternalInput"),
        "modw": p.dram("modw", [4, NCH, 128, NCH, 128], F32, kind="ExternalInput"),
        "modb": p.dram("modb", [128, 4, NCH], F32, kind="ExternalInput"),
        "normw": p.dram("normw", [128, NCH], F32, kind="ExternalInput"),
        "wout": p.dram("wout", [NCH, 128, NCH, 128], F32, kind="ExternalInput"),
        "wgu": p.dram("wgu", [NJ, 128, 2, NCH, 128], F32, kind="ExternalInput"),
        "wdn": p.dram("wdn", [NCH, 128, NJ, 128], F32, kind="ExternalInput"),
        "xout": p.dram("xout", [128, NCH, NT], F32, kind="ExternalOutput"),
    }
    if last:
        d["finalw"] = p.dram("finalw", [128, NCH], F32, kind="ExternalInput")
    build_B(p, cx, d, tiles, last)
    p.finish([d["xout"]])
    return nc, p


def fm(v):
    return np.ascontiguousarray(v.reshape(-1, 128).T)


def fm_act(xT):
    F, N = xT.shape
    return np.ascontiguousarray(xT.reshape(F // 128, 128, N).transpose(1, 0, 2))


def blk_lhsT(w, ncols_blk=128):
    K, N = w.shape
    return np.ascontiguousarray(w.reshape(K // 128, 128, N // 128, 128).transpose(2, 1, 0, 3))


def host_B_inputs(l, c_b, c_ctx, mod_w, mod_b, norm_ffn_w, w_out, wg, wu, wd, final_w=None):
    comps = [2, 3, 4, 5]
    modw = np.stack([blk_lhsT(mod_w[l][:, k * D:(k + 1) * D]) for k in comps])
    modb = np.stack([fm(mod_b[l][k * D:(k + 1) * D]) for k in comps], axis=1)
    svec = np.stack([fm(c_b), fm(c_ctx)], axis=2)
    g = blk_lhsT(wg[l])
    u = blk_lhsT(wu[l])
    wgu = np.ascontiguousarray(np.stack([g, u], axis=2))
    out = {
        "svec": np.ascontiguousarray(svec), "modw": modw, "modb": np.ascontiguousarray(modb),
        "normw": fm(norm_ffn_w[l]), "wout": blk_lhsT(w_out), "wgu": wgu, "wdn": blk_lhsT(wd[l]),
    }
    if final_w is not None:
        out["finalw"] = fm(final_w)
    return out


def st_mix(p, d, ch, src_t, src_ap, c0, W, mc):
    if "st_mix" in d:
        d["st_mix"](ch, src_t, src_ap, c0, W, mc)
    else:
        p.dma("pool", d["mixo"][ch, :, c0:c0 + W], src_ap, reads=[src_t], writes=[d["mixo"]])


def load_modA(p, cx, d):
    mod = p.sbuf("modA", [128, 2, NCH, 2], F32)
    compute_mod(p, cx, d["svec"], d["modw"], d["modb"], 2, mod)
    nw = p.sbuf("nwA", [128, NCH], F32)
    p.dma("sp", nw[:], d["normw"][:, :], writes=[nw])
    A1 = p.sbuf("A1", [128, NCH, 2], F32)
    for col in range(2):
        p.op("dve", "scalar_tensor_tensor", out=A1[:, :, col], in0=mod[:, 1, :, col], scalar=1.0, in1=nw[:],
             op0=ALU.add, op1=ALU.mult, reads=[mod, nw], writes=[A1])
    return A1, mod


def gelu_tanh_evac(p, ps, W, out_t, out_ap, t1, t2):
    p.op("act", "activation", out=t1[:, :W], in_=ps[:, :W], func=AF.Square, reads=[ps], writes=[t1])
    p.op("dve", "tensor_scalar", out=t1[:, :W], in0=t1[:, :W], scalar1=0.044715, scalar2=1.0, op0=ALU.mult,
         op1=ALU.add, reads=[t1], writes=[t1])
    p.op("dve", "tensor_tensor", out=t1[:, :W], in0=ps[:, :W], in1=t1[:, :W], op=ALU.mult, reads=[ps, t1],
         writes=[t1])
    p.op("act", "activation", out=t2[:, :W], in_=t1[:, :W], func=AF.Sigmoid, scale=1.5957691216057308,
         reads=[t1], writes=[t2])
    p.op("dve", "tensor_tensor", out=out_ap, in0=ps[:, :W], in1=t2[:, :W], op=ALU.mult, reads=[ps, t2],
         writes=[out_t])


def rope_evac(p, cx, ps, W, qf, rmat, cs_t, cs_ap, sn_ap, t1, t2, out_t, out_ap):
    p.op("act", "activation", out=qf[:, :W], in_=ps[:, :W], func=AF.Copy, reads=[ps], writes=[qf])
    p2 = cx.bank()
    p.op("pe", "matmul", out=p2[:, :W], lhsT=rmat[:], rhs=qf[:, :W], start=True, stop=True, reads=[rmat, qf],
         writes=[p2])
    p.op("dve", "tensor_tensor", out=t1[:, :W], in0=qf[:, :W], in1=cs_ap, op=ALU.mult, reads=[qf, cs_t], writes=[t1])
    p.op("dve", "tensor_tensor", out=t2[:, :W], in0=p2[:, :W], in1=sn_ap, op=ALU.mult, reads=[p2, cs_t], writes=[t2])
    p.op("pool", "tensor_tensor", out=out_ap, in0=t1[:, :W], in1=t2[:, :W], op=ALU.add, reads=[t1, t2],
         writes=[out_t])


def build_A0(p, cx, d, S, CT, lam_init):
    SL = S - CT
    tiles = [(0, CT, 1)] + [(CT + i * 512, 512, 0) for i in range(SL // 512)]
    NBLK = S // 128
    gg_d = p.dram(p.uniq("gg"), [2, 128, S], F32)
    xr_d = p.dram(p.uniq("xr"), [2, 128, S], F32)
    qk_d = p.dram(p.uniq("qk"), [4, 128, S], BF16)
    v_d = p.dram(p.uniq("vt"), [128, NBLK, 256], BF16)
    A1, mod = load_modA(p, cx, d)
    rmat = p.sbuf("rmat", [128, 128], F32)
    p.dma("sp", rmat[:], d["rmat"][:, :], writes=[rmat])
    oneb = p.sbuf("oneb", [128, 1], F32)
    p.op("dve", "memset", ap=oneb[:], constant=1.0, writes=[oneb])
    lqk = p.sbuf("lqk", [128, 4, 64], F32)
    p.dma("sp", lqk[:], d["lqk"][:, :, :], writes=[lqk])
    ltmp = p.sbuf("ltmp", [128, 2, 64], F32)
    lsum = p.sbuf("lsum", [128, 2], F32)
    neg_lam = p.sbuf("neglam", [128, 1], F32)
    for i in range(2):
        p.op("dve", "tensor_tensor", out=ltmp[:, i, :], in0=lqk[:, 2 * i, :], in1=lqk[:, 2 * i + 1, :], op=ALU.mult,
             reads=[lqk], writes=[ltmp])
        p.op("dve", "tensor_reduce", out=lsum[:, i:i + 1], in_=ltmp[:, i, :], axis=AX.X, op=ALU.add, reads=[ltmp],
             writes=[lsum])
    p.op("act", "activation", out=lsum[:], in_=lsum[:], func=AF.Exp, reads=[lsum], writes=[lsum])
    p.op("dve", "tensor_tensor", out=neg_lam[:], in0=lsum[:, 1:2], in1=lsum[:, 0:1], op=ALU.subtract, reads=[lsum],
         writes=[neg_lam])
    p.op("dve", "tensor_scalar", out=neg_lam[:], in0=neg_lam[:], scalar1=-lam_init, scalar2=None, op0=ALU.add,
         reads=[neg_lam], writes=[neg_lam])
    wsub = p.sbuf("wsub", [128, 1], F32)
    p.dma("sp", wsub[:], d["subw"][:, :], writes=[wsub])
    p.op("dve", "tensor_scalar", out=wsub[:], in0=wsub[:], scalar1=1.0 - lam_init, scalar2=None, op0=ALU.mult,
         reads=[wsub], writes=[wsub])
    lba = p.sbuf("lba", [128, 2, 2], F32)
    lbx = p.sbuf("lbx", [128, 2, 2], F32)
    llam = p.sbuf("llam", [128, 2, 2], F32)
    nsl = p.sbuf("nsl", [128, 2, 2], F32)
    cw = p.sbuf("convw", [128, 2, 4], F32)
    cb = p.sbuf("convb", [128, 2], F32)
    p.dma("sp", lba[:], d["lba"][:, :, :], writes=[lba])
    p.dma("sp", lbx[:], d["lbx"][:, :, :], writes=[lbx])
    p.dma("sp", llam[:], d["llam"][:, :, :], writes=[llam])
    p.dma("sp", cw[:], d["convw"][:, :, :], writes=[cw])
    p.dma("sp", cb[:], d["convb"][:, :], writes=[cb])
    ee = p.sbuf("ee", [128, 2, 2], F32)
    pp = p.sbuf("pp", [128, 2, 2], F32)
    p.op("act", "activation", out=ee[:], in_=llam[:], func=AF.Exp, scale=-1.0, reads=[llam], writes=[ee])
    p.op("dve", "tensor_scalar", out=pp[:], in0=ee[:], scalar1=-0.25, scalar2=1.0 / 3.0, op0=ALU.mult, op1=ALU.add,
         reads=[ee], writes=[pp])
    p.op("dve", "tensor_tensor", out=pp[:], in0=pp[:], in1=ee[:], op=ALU.mult, reads=[pp, ee], writes=[pp])
    p.op("dve", "tensor_scalar", out=pp[:], in0=pp[:], scalar1=-0.5, scalar2=None, op0=ALU.add, reads=[pp], writes=[pp])
    p.op("dve", "tensor_tensor", out=pp[:], in0=pp[:], in1=ee[:], op=ALU.mult, reads=[pp, ee], writes=[pp])
    p.op("dve", "tensor_scalar", out=pp[:], in0=pp[:], scalar1=1.0, scalar2=None, op0=ALU.add, reads=[pp], writes=[pp])
    p.op("dve", "tensor_tensor", out=pp[:], in0=pp[:], in1=ee[:], op=ALU.mult, reads=[pp, ee], writes=[pp])
    p.op("dve", "tensor_scalar", out=nsl[:], in0=pp[:], scalar1=-8.0, scalar2=None, op0=ALU.mult, reads=[pp],
         writes=[nsl])
    wg = p.sbuf("lruw", [128, 2, 2, 2, 128], BF16)
    p.dma("pool", wg[:], d["lruw"][:, :, :, :, :], writes=[wg])

    p.open_scope()
    win = p.sbuf("win", [128, NCH, 1280], BF16)
    for c in range(0, NCH, 4):
        p.dma("pool", win[:, c:c + 4, :], d["win"][:, c:c + 4, :], writes=[win])
    xt = p.sbuf("xtA", [128, NCH, 512], F32)
    sq_t = p.sbuf("sqA", [128, NCH, 512], BF16)
    h_t = p.sbuf("hA", [128, NCH, 512], BF16)
    rstd_t = p.sbuf("rstdA", [128, 512], F32)
    tmp_t = [p.sbuf("tmpA%d" % i, [128, 512], F32) for i in range(2)]
    cs_t = p.sbuf("csA", [128, 2, 512], F32)
    stg_f = p.sbuf("stgf", [128, 4, 512], F32)
    stg_b = p.sbuf("stgb", [128, 4, 512], BF16)
    stg_v = p.sbuf("stgv", [128, 4, 256], BF16)
    qf = [p.sbuf("qf%d" % i, [128, 512], F32) for i in range(2)]
    t1 = [p.sbuf("t1A%d" % i, [128, 512], F32) for i in range(2)]
    t2 = [p.sbuf("t2A%d" % i, [128, 512], F32) for i in range(2)]
    for ti, (c0, W, mc) in enumerate(tiles):
        lat = (mc == 0)
        if "ld_xa" in d:
            d["ld_xa"](xt, c0, W, mc)
        else:
            p.dma("sp", xt[:, :, :W], d["xa"][:, :, c0:c0 + W], reads=[d["xa"]], writes=[xt])
        if lat:
            p.dma("sp", cs_t[:, 0, :W], d["cos"][:, c0 - CT:c0 - CT + W], writes=[cs_t])
            p.dma("sp", cs_t[:, 1, :W], d["sin"][:, c0 - CT:c0 - CT + W], writes=[cs_t])
        rms_stats(p, cx, xt, xt[:, :, :W], W, sq_t, rstd_t)
        norm_mod(p, cx, xt, lambda c: xt[:, c, :W], W, rstd_t, tmp_t,
                 A1, lambda c: A1[:, c, mc:mc + 1], mod, lambda c: mod[:, 0, c, mc:mc + 1],
                 h_t, lambda c: h_t[:, c, :W])
        for oc in range(8):
            ps = cx.bank()
            for c in range(NCH):
                mm(p, ps, ps[:, :W], win, win[:, c, oc * 128:(oc + 1) * 128], h_t, h_t[:, c, :W], c == 0, c == NCH - 1)
            if oc < 2:
                gelu_tanh_evac(p, ps, W, stg_f, stg_f[:, oc, :W], t1[oc % 2], t2[oc % 2])
            elif oc < 4:
                p.op("act", "activation", out=stg_f[:, oc, :W], in_=ps[:, :W], func=AF.Copy, reads=[ps],
                     writes=[stg_f])
            elif lat:
                rope_evac(p, cx, ps, W, qf[oc % 2], rmat, cs_t, cs_t[:, 0, :W], cs_t[:, 1, :W], t1[oc % 2],
                          t2[oc % 2], stg_b, stg_b[:, oc - 4, :W])
            else:
                p.op("act", "activation", out=stg_b[:, oc - 4, :W], in_=ps[:, :W], func=AF.Copy, reads=[ps],
                     writes=[stg_b])
        for tb in range(W // 128):
            ps = cx.bank()
            for c in range(NCH):
                mm(p, ps, ps[:, :256], h_t, h_t[:, c, tb * 128:(tb + 1) * 128], win, win[:, c, 1024:1280],
                   c == 0, c == NCH - 1)
            p.op("act", "activation", out=stg_v[:, tb, :], in_=ps[:, :256], func=AF.Copy, reads=[ps], writes=[stg_v])
        p.dma("pool", gg_d[:, :, c0:c0 + W].rearrange("a p w -> p a w"), stg_f[:, 0:2, :W], reads=[stg_f],
              writes=[gg_d])
        p.dma("pool", xr_d[:, :, c0:c0 + W].rearrange("a p w -> p a w"), stg_f[:, 2:4, :W], reads=[stg_f],
              writes=[xr_d])
        p.dma("pool", qk_d[:, :, c0:c0 + W].rearrange("a p w -> p a w"), stg_b[:, :, :W], reads=[stg_b],
              writes=[qk_d])
        p.dma("pool", v_d[:, c0 // 128:(c0 + W) // 128, :], stg_v[:, :W // 128, :], reads=[stg_v], writes=[v_d])
    p.close_scope()

    p.open_scope()
    KT = p.sbuf("KT", [128, S], BF16)
    V = p.sbuf("Vtok", [128, NBLK, 128], BF16)
    QT = [p.sbuf("QT%d" % i, [128, 512], BF16) for i in range(2)]
    E = [p.sbuf("E%d" % i, [128, 512], BF16) for i in range(4)]
    rz = [p.sbuf("rz%d" % i, [128, 512], F32) for i in range(2)]
    oo = [p.sbuf("oo%d" % i, [128, 512], F32) for i in range(2)]
    dd = p.sbuf("dd", [128, 512], F32)
    dsq = p.sbuf("dsq", [128, 512], BF16)
    drs = p.sbuf("drs", [128, 512], F32)
    dout = [p.sbuf("dout%d" % i, [128, 512], BF16) for i in range(2)]
    srot = [cx.banks[4], cx.banks[5], cx.banks[6], cx.pmisc]
    si = 0
    ei = 0
    qi = 0
    for hh in range(2):
        p.dma("sp", KT[:], qk_d[2 + hh], reads=[qk_d], writes=[KT])
        for b0 in range(0, NBLK, 32):
            b1 = min(NBLK, b0 + 32)
            p.dma("sp", V[:, b0:b1, :], v_d[:, b0:b1, hh * 128:(hh + 1) * 128], reads=[v_d], writes=[V])
        for ti, (c0, W, mc) in enumerate(tiles):
            kblocks = list(range(CT // 128)) if mc == 1 else list(range(NBLK))
            Q = QT[qi % 2]
            qi += 1
            p.dma("sp", Q[:, :W], qk_d[hh, :, c0:c0 + W], reads=[qk_d], writes=[Q])
            for m in range(2):
                O = cx.banks[2 * m]
                Z = cx.banks[2 * m + 1]
                for ki, kb in enumerate(kblocks):
                    sp_ = srot[si % 4]
                    si += 1
                    mm(p, sp_, sp_[:, :W], KT, KT[64 * m:64 * m + 64, kb * 128:(kb + 1) * 128], Q,
                       Q[64 * m:64 * m + 64, :W], True, True)
                    Eb = E[ei % 4]
                    ei += 1
                    p.op("act", "activation", out=Eb[:, :W], in_=sp_[:, :W], func=AF.Exp, scale=0.125, reads=[sp_],
                         writes=[Eb])
                    first = ki == 0
                    lastk = ki == len(kblocks) - 1
                    mm(p, O, O[:, :W], V, V[:, kb, :], Eb, Eb[:, :W], first, lastk)
                    mm(p, Z, Z[:, :W], cx.ones_bf, cx.ones_bf[:], Eb, Eb[:, :W], first, lastk)
                p.op("dve", "reciprocal", out=rz[m][:, :W], in_=Z[:, :W], reads=[Z], writes=[rz[m]])
                p.op("dve", "tensor_tensor", out=oo[m][:, :W], in0=O[:, :W], in1=rz[m][:, :W], op=ALU.mult,
                     reads=[O, rz[m]], writes=[oo[m]])
            p.op("dve", "scalar_tensor_tensor", out=dd[:, :W], in0=oo[1][:, :W], scalar=neg_lam[:], in1=oo[0][:, :W],
                 op0=ALU.mult, op1=ALU.add, reads=[oo[0], oo[1], neg_lam], writes=[dd])
            p.op("act", "activation", out=dsq[:, :W], in_=dd[:, :W], func=AF.Square, reads=[dd], writes=[dsq])
            ps = cx.banks[2]
            mm(p, ps, ps[:, :W], cx.ones_bf, cx.ones_bf[:], dsq, dsq[:, :W], True, True)
            p.op("act", "activation", out=drs[:, :W], in_=ps[:, :W], func=AF.Sqrt, scale=1.0 / 128, bias=cx.epsb[:],
                 reads=[ps, cx.epsb], writes=[drs])
            p.op("dve", "reciprocal", out=drs[:, :W], in_=drs[:, :W], reads=[drs], writes=[drs])
            do = dout[ti % 2]
            p.op("dve", "scalar_tensor_tensor", out=do[:, :W], in0=dd[:, :W], scalar=wsub[:], in1=drs[:, :W],
                 op0=ALU.mult, op1=ALU.mult, reads=[dd, wsub, drs], writes=[do])
            st_mix(p, d, 2 + hh, do, do[:, :W], c0, W, mc)
    p.close_scope()

    for bi in range(2):
        p.open_scope()
        xc = p.sbuf("xc", [128, S], F32)
        xcb = p.sbuf("xcb", [128, S], BF16)
        p.open_scope()
        xp = p.sbuf("xp", [128, S + 6], F32)
        p.op("pool", "memset", ap=xp[:], constant=0.0, writes=[xp])
        p.dma("sp", xp[:, 1:1 + CT], xr_d[bi, :, 0:CT], reads=[xr_d], writes=[xp])
        p.dma("sp", xp[:, CT + 4:CT + 4 + SL], xr_d[bi, :, CT:S], reads=[xr_d], writes=[xp])
        for (o0, po, n) in ((0, 1, CT), (CT, CT + 4, SL)):
            for n0 in range(0, n, 4096):
                n1 = min(n, n0 + 4096)
                L = n1 - n0
                p.op("dve", "tensor_scalar", out=xc[:, o0 + n0:o0 + n1], in0=xp[:, po + n0 - 1:po + n0 - 1 + L],
                     scalar1=cw[:, bi, 0:1], scalar2=cb[:, bi:bi + 1], op0=ALU.mult, op1=ALU.add,
                     reads=[xp, cw, cb], writes=[xc])
                for k in range(1, 4):
                    p.op("dve", "scalar_tensor_tensor", out=xc[:, o0 + n0:o0 + n1],
                         in0=xp[:, po + n0 - 1 + k:po + n0 - 1 + k + L], scalar=cw[:, bi, k:k + 1],
                         in1=xc[:, o0 + n0:o0 + n1], op0=ALU.mult, op1=ALU.add, reads=[xp, cw, xc], writes=[xc])
        p.close_scope()
        for n0 in range(0, S, 4096):
            n1 = min(S, n0 + 4096)
            p.op("act", "activation", out=xcb[:, n0:n1], in_=xc[:, n0:n1], func=AF.Copy, reads=[xc], writes=[xcb])
        p.open_scope()
        racc = p.sbuf("racc", [128, S], F32)
        gr = p.sbuf("gr", [128, 512], F32)
        ga = p.sbuf("ga", [128, 512], F32)
        gi = p.sbuf("gi", [128, 512], F32)
        gs = p.sbuf("gs", [128, 512], F32)
        gb = p.sbuf("gb", [128, 512], F32)
        hb = [p.sbuf("hb%d" % i, [128, 512], F32) for i in range(2)]
        ggt = p.sbuf("ggt", [128, 512], F32)
        hsum = p.sbuf("hsum", [128, 512], F32)
        ao = [p.sbuf("ao%d" % i, [128, 512], BF16) for i in range(2)]
        for dr in range(2):
            order = list(range(len(tiles)))
            if dr == 1:
                order = [0] + order[:0:-1]
            prev = None
            for n, ti in enumerate(order):
                c0, W, mc = tiles[ti]
                pa = cx.bank()
                px = cx.bank()
                mm(p, pa, pa[:, :W], wg, wg[:, 0, dr, bi, :], xcb, xcb[:, c0:c0 + W], True, True)
                mm(p, px, px[:, :W], wg, wg[:, 1, dr, bi, :], xcb, xcb[:, c0:c0 + W], True, True)
                p.op("act", "activation", out=gr[:, :W], in_=pa[:, :W], func=AF.Sigmoid, bias=lba[:, dr, bi:bi + 1],
                     reads=[pa, lba], writes=[gr])
                p.op("act", "activation", out=gi[:, :W], in_=px[:, :W], func=AF.Sigmoid, bias=lbx[:, dr, bi:bi + 1],
                     reads=[px, lbx], writes=[gi])
                p.op("act", "activation", out=ga[:, :W], in_=gr[:, :W], func=AF.Exp, scale=nsl[:, dr, bi:bi + 1],
                     reads=[gr, nsl], writes=[ga])
                p.op("dve", "tensor_scalar", out=ga[:, :W], in0=ga[:, :W], scalar1=1.0, scalar2=None, op0=ALU.min,
                     reads=[ga], writes=[ga])
                p.op("dve", "tensor_tensor", out=gs[:, :W], in0=ga[:, :W], in1=ga[:, :W], op=ALU.mult, reads=[ga],
                     writes=[gs])
                p.op("act", "activation", out=gs[:, :W], in_=gs[:, :W], func=AF.Sqrt, scale=-1.0, bias=oneb[:],
                     reads=[gs, oneb], writes=[gs])
                p.op("dve", "tensor_tensor", out=gb[:, :W], in0=gs[:, :W], in1=gi[:, :W], op=ALU.mult, reads=[gs, gi],
                     writes=[gb])
                p.op("dve", "tensor_tensor", out=gb[:, :W], in0=gb[:, :W], in1=xc[:, c0:c0 + W], op=ALU.mult,
                     reads=[gb, xc], writes=[gb])
                if dr == 0:
                    init = 0.0 if n == 0 else racc[:, c0 - 1:c0]
                    p.op("dve", "tensor_tensor_scan", out=racc[:, c0:c0 + W], data0=ga[:, :W], data1=gb[:, :W],
                         initial=init, op0=ALU.mult, op1=ALU.add, reads=[ga, gb, racc], writes=[racc])
                else:
                    hcur = hb[n % 2]
                    if n == 0:
                        init = 0.0
                        rd = [ga, gb]
                    else:
                        init = prev[0][:, 0:1]
                        rd = [ga, gb, prev[0]]
                    p.op("dve", "tensor_tensor_scan", out=hcur[:, 0:W][:, ::-1],
                         data0=ga[:, 0:W][:, ::-1], data1=gb[:, 0:W][:, ::-1], initial=init, op0=ALU.mult,
                         op1=ALU.add, reads=rd, writes=[hcur])
                    prev = (hcur, W)
                    p.dma("sp", ggt[:, :W], gg_d[bi, :, c0:c0 + W], reads=[gg_d], writes=[ggt])
                    a_o = ao[n % 2]
                    p.op("pool", "tensor_tensor", out=hsum[:, :W], in0=hcur[:, :W], in1=racc[:, c0:c0 + W], op=ALU.add,
                         reads=[hcur, racc], writes=[hsum])
                    p.op("pool", "tensor_tensor", out=a_o[:, :W], in0=hsum[:, :W], in1=ggt[:, :W], op=ALU.mult,
                         reads=[hsum, ggt], writes=[a_o])
                    st_mix(p, d, bi, a_o, a_o[:, :W], c0, W, mc)
        p.close_scope()
        p.close_scope()


def a0_dram(p, S, SL, pre="", fused=False):
    dd = {
        "xa": p.dram(pre + "xa", [128, NCH, S], F32, kind="ExternalInput"),
        "svec": p.dram(pre + "svec", [128, NCH, 2], F32, kind="ExternalInput"),
        "modw": p.dram(pre + "modw", [2, NCH, 128, NCH, 128], F32, kind="ExternalInput"),
        "modb": p.dram(pre + "modb", [128, 2, NCH], F32, kind="ExternalInput"),
        "normw": p.dram(pre + "normw", [128, NCH], F32, kind="ExternalInput"),
        "win": p.dram(pre + "win", [128, NCH, 1280], F32, kind="ExternalInput"),
        "rmat": p.dram(pre + "rmat", [128, 128], F32, kind="ExternalInput"),
        "lqk": p.dram(pre + "lqk", [128, 4, 64], F32, kind="ExternalInput"),
        "subw": p.dram(pre + "subw", [128, 1], F32, kind="ExternalInput"),
        "lba": p.dram(pre + "lba", [128, 2, 2], F32, kind="ExternalInput"),
        "lbx": p.dram(pre + "lbx", [128, 2, 2], F32, kind="ExternalInput"),
        "llam": p.dram(pre + "llam", [128, 2, 2], F32, kind="ExternalInput"),
        "convw": p.dram(pre + "convw", [128, 2, 4], F32, kind="ExternalInput"),
        "convb": p.dram(pre + "convb", [128, 2], F32, kind="ExternalInput"),
        "lruw": p.dram(pre + "lruw", [128, 2, 2, 2, 128], F32, kind="ExternalInput"),
        "cos": p.dram(pre + "cos", [128, SL], F32, kind="ExternalInput"),
        "sin": p.dram(pre + "sin", [128, SL], F32, kind="ExternalInput"),
    }
    dd["mixo"] = p.dram(pre + "mixo", [4, 128, S], BF16, kind="Internal" if fused else "ExternalOutput")
    return dd


def make_A0(S, CT):
    nc = bass.Bass("TRN2", target_bir_lowering=False)
    p = Prog(nc)
    cx = Ctx(p)
    d = a0_dram(p, S, S - CT)
    build_A0(p, cx, d, S, CT, 0.8 - 0.6 * math.exp(0.0))
    p.finish([d["mixo"]])
    return nc, p


def rope_tables(SL, head_dim, grid_w=64, theta=10000.0):
    rows = SL // grid_w
    row = np.repeat(np.arange(rows, dtype=np.float32), grid_w)
    col = np.tile(np.arange(grid_w, dtype=np.float32), rows)
    n_freq = head_dim // 4
    inv = (np.float32(theta) ** (-np.arange(n_freq, dtype=np.float32) / np.float32(n_freq))).astype(np.float32)
    ang = np.concatenate([row[:, None] * inv, col[:, None] * inv], axis=-1).astype(np.float32)
    cos = np.cos(ang).astype(np.float32).T
    sin = np.sin(ang).astype(np.float32).T
    reps = 128 // (head_dim // 2)
    return np.ascontiguousarray(np.tile(cos, (reps, 1))), np.ascontiguousarray(np.tile(sin, (reps, 1)))


def rot_matrix(head_dim):
    R = np.zeros((128, 128), np.float32)
    hh = head_dim // 2
    for m in range(128):
        if (m % head_dim) < hh:
            R[m + hh, m] = -1.0
        else:
            R[m - hh, m] = 1.0
    return R


def host_A0_inputs(j, c_b, c_ctx, mod_w, mod_b, norm_mix_w, ab_w_in, conv_w, conv_b, wa, ba, wx, bx, lam,
                   lq1, lk1, lq2, lk2, subw, SL):
    l = 0
    comps = [0, 1]
    modw = np.stack([blk_lhsT(mod_w[l][:, k * D:(k + 1) * D]) for k in comps])
    modb = np.stack([fm(mod_b[l][k * D:(k + 1) * D]) for k in comps], axis=1)
    svec = np.stack([fm(c_b), fm(c_ctx)], axis=2)
    W = ab_w_in[0]
    cols = np.concatenate([np.arange(256 * j, 256 * j + 256), 1024 + np.arange(256 * j, 256 * j + 256),
                           2048 + np.arange(256 * j, 256 * j + 256), 3072 + np.arange(256 * j, 256 * j + 256),
                           4096 + np.arange(256 * j, 256 * j + 256)])
    win = np.ascontiguousarray(W[:, cols].reshape(NCH, 128, 1280).transpose(1, 0, 2))
    blks = [2 * j, 2 * j + 1]

    def pvec(v):
        return np.ascontiguousarray(np.stack([v[:, b * 128:(b + 1) * 128] for b in blks], axis=2).transpose(1, 0, 2))
    lruw = np.stack([np.stack([np.stack([w[0][dr, b] for b in blks], 0) for dr in range(2)], 0) for w in (wa, wx)], 0)
    lruw = np.ascontiguousarray(lruw.transpose(3, 0, 1, 2, 4))
    cos, sin = rope_tables(SL, 64)
    return {
        "svec": np.ascontiguousarray(svec), "modw": modw, "modb": np.ascontiguousarray(modb),
        "normw": fm(norm_mix_w[l]), "win": win, "rmat": rot_matrix(64),
        "lqk": np.ascontiguousarray(np.broadcast_to(np.stack([lq1[0], lk1[0], lq2[0], lk2[0]])[None], (128, 4, 64))),
        "subw": np.ascontiguousarray(subw[0].reshape(128, 1)),
        "lba": pvec(ba[0]), "lbx": pvec(bx[0]), "llam": pvec(lam[0]),
        "convw": np.ascontiguousarray(np.stack([conv_w[0][:, b * 128:(b + 1) * 128] for b in blks], 0).transpose(2, 0, 1)),
        "convb": np.ascontiguousarray(np.stack([conv_b[0][b * 128:(b + 1) * 128] for b in blks], 1)),
        "lruw": lruw, "cos": cos, "sin": sin,
    }


def build_A1(p, cx, d, S, CT, stage=9):
    SL = S - CT
    tiles = [(0, CT, 1)] + [(CT + i * 512, 512, 0) for i in range(SL // 512)]
    NBLK = S // 128
    NF = 1408
    qh_d = p.dram(p.uniq("qh"), [2, 128, S], F32)
    f_d = p.dram(p.uniq("fd"), [2, 2, 128, S], F32)
    sg_d = p.dram(p.uniq("sg"), [2, 128, S], F32)
    qk_d = p.dram(p.uniq("qk1"), [3, 128, S], BF16)
    vt_d = p.dram(p.uniq("vt1"), [128, NBLK, 384], BF16)
    o_d = p.dram(p.uniq("od"), [2, 128, S], F32)

    A1, mod = load_modA(p, cx, d)
    rmat = p.sbuf("rmat", [128, 128], F32)
    p.dma("sp", rmat[:], d["rmat"][:, :], writes=[rmat])
    ident = p.sbuf("ident", [128, 128], BF16)
    p.dma("pool", ident[:], d["ident"][:, :], writes=[ident])
    nws = p.sbuf("nws", [128, 3], F32)
    p.dma("sp", nws[:], d["nws"][:, :], writes=[nws])
    lbl = p.sbuf("lbl", [128, 2, 2, 2], F32)
    p.dma("sp", lbl[:], d["lbl"][:, :, :, :], writes=[lbl])
    lb = p.sbuf("lb", [128, 2, 2], F32)
    oml = p.sbuf("oml", [128, 2, 2], F32)
    p.op("dve", "tensor_tensor", out=lb[:], in0=lbl[:, :, 1, :], in1=lbl[:, :, 0, :], op=ALU.subtract, reads=[lbl],
         writes=[lb])
    p.op("act", "activation", out=lb[:], in_=lb[:], func=AF.Sigmoid, reads=[lb], writes=[lb])
    p.op("dve", "tensor_scalar", out=oml[:], in0=lb[:], scalar1=-1.0, scalar2=1.0, op0=ALU.mult, op1=ALU.add,
         reads=[lb], writes=[oml])
    maskF = p.sbuf("maskF", [128, 512], F32)
    maskB = p.sbuf("maskB", [128, 512], F32)
    p.op("pool", "memset", ap=maskF[:], constant=1.0, writes=[maskF])
    p.op("pool", "memset", ap=maskB[:], constant=1.0, writes=[maskB])
    for ci in range(8):
        p.op("pool", "memset", ap=maskF[:, ci * 64:ci * 64 + 1], constant=0.0, writes=[maskF])
        p.op("pool", "memset", ap=maskB[:, ci * 64 + 63:ci * 64 + 64], constant=0.0, writes=[maskB])
    tri = p.sbuf("tri", [64, 2, 64], F32)
    p.dma("sp", tri[:], d["tri"][:, :, :], writes=[tri])

    p.open_scope()
    win = p.sbuf("win1", [128, NCH, 1792], BF16)
    for c in range(0, NCH, 4):
        p.dma("pool", win[:, c:c + 4, :], d["win"][:, c:c + 4, :], writes=[win])
    xt = p.sbuf("xtA", [128, NCH, 512], F32)
    sq_t = p.sbuf("sqA", [128, NCH, 512], BF16)
    h_t = p.sbuf("hA", [128, NCH, 512], BF16)
    rstd_t = p.sbuf("rstdA", [128, 512], F32)
    tmp_t = [p.sbuf("tmpA%d" % i, [128, 512], F32) for i in range(2)]
    cs_t = p.sbuf("csA", [128, 2, 512], F32)
    stg_q = p.sbuf("stgq", [128, 2, 512], F32)
    stg_f = p.sbuf("stgf1", [128, 4, 512], F32)
    stg_g = p.sbuf("stgg", [128, 2, 512], F32)
    stg_b = p.sbuf("stgb1", [128, 3, 512], BF16)
    stg_v = p.sbuf("stgv1", [128, 4, 384], BF16)
    qf = [p.sbuf("qf%d" % i, [128, 512], F32) for i in range(2)]
    qn = [p.sbuf("qn%d" % i, [128, 512], F32) for i in range(2)]
    qsq = [p.sbuf("qsq%d" % i, [128, 512], BF16) for i in range(2)]
    qrs = [p.sbuf("qrs%d" % i, [128, 512], F32) for i in range(2)]
    t1 = [p.sbuf("t1A%d" % i, [128, 512], F32) for i in range(2)]
    t2 = [p.sbuf("t2A%d" % i, [128, 512], F32) for i in range(2)]
    for ti, (c0, W, mc) in enumerate(tiles):
        lat = (mc == 0)
        if "ld_xa" in d:
            d["ld_xa"](xt, c0, W, mc)
        else:
            p.dma("sp", xt[:, :, :W], d["xa"][:, :, c0:c0 + W], reads=[d["xa"]], writes=[xt])
        if lat:
            p.dma("sp", cs_t[:, 0, :W], d["cos"][:, c0 - CT:c0 - CT + W], writes=[cs_t])
            p.dma("sp", cs_t[:, 1, :W], d["sin"][:, c0 - CT:c0 - CT + W], writes=[cs_t])
        rms_stats(p, cx, xt, xt[:, :, :W], W, sq_t, rstd_t)
        norm_mod(p, cx, xt, lambda c: xt[:, c, :W], W, rstd_t, tmp_t,
                 A1, lambda c: A1[:, c, mc:mc + 1], mod, lambda c: mod[:, 0, c, mc:mc + 1],
                 h_t, lambda c: h_t[:, c, :W])
        for oc in range(11):
            if oc in (8, 9) and not lat:
                continue
            ps = cx.bank()
            for c in range(NCH):
                mm(p, ps, ps[:, :W], win, win[:, c, oc * 128:(oc + 1) * 128], h_t, h_t[:, c, :W], c == 0, c == NCH - 1)
            if oc < 2:
                p.op("act", "activation", out=stg_q[:, oc, :W], in_=ps[:, :W], func=AF.Silu, reads=[ps], writes=[stg_q])
            elif oc < 6:
                dr, hh = (oc - 2) // 2, (oc - 2) % 2
                tt = t1[oc % 2]
                p.op("act", "activation", out=tt[:, :W], in_=ps[:, :W], func=AF.Sigmoid, reads=[ps], writes=[tt])
                p.op("dve", "tensor_scalar", out=stg_f[:, oc - 2, :W], in0=tt[:, :W], scalar1=oml[:, dr, hh:hh + 1],
                     scalar2=lb[:, dr, hh:hh + 1], op0=ALU.mult, op1=ALU.add, reads=[tt, oml, lb], writes=[stg_f])
            elif oc < 8:
                p.op("act", "activation", out=stg_g[:, oc - 6, :W], in_=ps[:, :W], func=AF.Silu, reads=[ps],
                     writes=[stg_g])
            else:
                k = oc % 2
                wcol = 1 if oc < 10 else 2
                p.op("act", "activation", out=qf[k][:, :W], in_=ps[:, :W], func=AF.Copy, reads=[ps], writes=[qf[k]])
                p.op("act", "activation", out=qsq[k][:, :W], in_=ps[:, :W], func=AF.Square, reads=[ps], writes=[qsq[k]])
                p2 = cx.bank()
                mm(p, p2, p2[:, :W], cx.ones_bf, cx.ones_bf[:], qsq[k], qsq[k][:, :W], True, True)
                p.op("act", "activation", out=qrs[k][:, :W], in_=p2[:, :W], func=AF.Sqrt, scale=1.0 / 128,
                     bias=cx.epsb[:], reads=[p2, cx.epsb], writes=[qrs[k]])
                p.op("dve", "reciprocal", out=qrs[k][:, :W], in_=qrs[k][:, :W], reads=[qrs[k]], writes=[qrs[k]])
                if lat:
                    p.op("dve", "scalar_tensor_tensor", out=qn[k][:, :W], in0=qf[k][:, :W], scalar=nws[:, wcol:wcol + 1],
                         in1=qrs[k][:, :W], op0=ALU.mult, op1=ALU.mult, reads=[qf[k], nws, qrs[k]], writes=[qn[k]])
                    p3 = cx.bank()
                    p.op("pe", "matmul", out=p3[:, :W], lhsT=rmat[:], rhs=qn[k][:, :W], start=True, stop=True,
                         reads=[rmat, qn[k]], writes=[p3])
                    p.op("dve", "tensor_tensor", out=t1[k][:, :W], in0=qn[k][:, :W], in1=cs_t[:, 0, :W], op=ALU.mult,
                         reads=[qn[k], cs_t], writes=[t1[k]])
                    p.op("dve", "tensor_tensor", out=t2[k][:, :W], in0=p3[:, :W], in1=cs_t[:, 1, :W], op=ALU.mult,
                         reads=[p3, cs_t], writes=[t2[k]])
                    p.op("pool", "tensor_tensor", out=stg_b[:, oc - 8, :W], in0=t1[k][:, :W], in1=t2[k][:, :W],
                         op=ALU.add, reads=[t1[k], t2[k]], writes=[stg_b])
                else:
                    p.op("dve", "scalar_tensor_tensor", out=stg_b[:, oc - 8, :W], in0=qf[k][:, :W],
                         scalar=nws[:, wcol:wcol + 1], in1=qrs[k][:, :W], op0=ALU.mult, op1=ALU.mult,
                         reads=[qf[k], nws, qrs[k]], writes=[stg_b])
        for tb in range(W // 128):
            ps = cx.bank()
            for c in range(NCH):
                mm(p, ps, ps[:, :384], h_t, h_t[:, c, tb * 128:(tb + 1) * 128], win, win[:, c, NF:NF + 384],
                   c == 0, c == NCH - 1)
            p.op("act", "activation", out=stg_v[:, tb, :], in_=ps[:, :384], func=AF.Copy, reads=[ps], writes=[stg_v])
        p.dma("pool", qh_d[:, :, c0:c0 + W].rearrange("a p w -> p a w"), stg_q[:, :, :W], reads=[stg_q], writes=[qh_d])
        p.dma("pool", f_d[:, :, :, c0:c0 + W].rearrange("a b p w -> p (a b) w"), stg_f[:, :, :W], reads=[stg_f],
              writes=[f_d])
        p.dma("pool", sg_d[:, :, c0:c0 + W].rearrange("a p w -> p a w"), stg_g[:, :, :W], reads=[stg_g], writes=[sg_d])
        if lat:
            p.dma("pool", qk_d[:, :, c0:c0 + W].rearrange("a p w -> p a w"), stg_b[:, :, :W], reads=[stg_b],
                  writes=[qk_d])
        else:
            p.dma("pool", qk_d[2, :, c0:c0 + W], stg_b[:, 2, :W], reads=[stg_b], writes=[qk_d])
        p.dma("pool", vt_d[:, c0 // 128:(c0 + W) // 128, :], stg_v[:, :W // 128, :], reads=[stg_v], writes=[vt_d])
    p.close_scope()

    if stage < 2:
        return
    p.open_scope()
    KT = p.sbuf("KT1", [128, S], BF16)
    V = p.sbuf("Vtok1", [128, NBLK, 128], BF16)
    QT = [p.sbuf("QT1%d" % i, [128, 512], BF16) for i in range(2)]
    E = [p.sbuf("E1%d" % i, [128, 512], BF16) for i in range(4)]
    rz = p.sbuf("rz1", [128, 512], F32)
    ao = [p.sbuf("ao1%d" % i, [128, 512], BF16) for i in range(2)]
    srot = [cx.banks[4], cx.banks[5], cx.banks[6], cx.pmisc]
    si = ei = qi = 0
    p.dma("sp", KT[:], qk_d[2], reads=[qk_d], writes=[KT])
    for b0 in range(0, NBLK, 32):
        b1 = min(NBLK, b0 + 32)
        p.dma("sp", V[:, b0:b1, :], vt_d[:, b0:b1, 256:384], reads=[vt_d], writes=[V])
    sc = 128 ** -0.5
    for hh in range(2):
        for ti, (c0, W, mc) in enumerate(tiles):
            if mc == 1:
                continue
            Q = QT[qi % 2]
            O = cx.banks[2 * (qi % 2)]
            Z = cx.banks[2 * (qi % 2) + 1]
            qi += 1
            p.dma("sp", Q[:, :W], qk_d[hh, :, c0:c0 + W], reads=[qk_d], writes=[Q])
            for kb in range(NBLK):
                sp_ = srot[si % 4]
                si += 1
                mm(p, sp_, sp_[:, :W], KT, KT[:, kb * 128:(kb + 1) * 128], Q, Q[:, :W], True, True)
                Eb = E[ei % 4]
                ei += 1
                p.op("act", "activation", out=Eb[:, :W], in_=sp_[:, :W], func=AF.Exp, scale=sc, reads=[sp_],
                     writes=[Eb])
                mm(p, O, O[:, :W], V, V[:, kb, :], Eb, Eb[:, :W], kb == 0, kb == NBLK - 1)
                mm(p, Z, Z[:, :W], cx.ones_bf, cx.ones_bf[:], Eb, Eb[:, :W], kb == 0, kb == NBLK - 1)
            p.op("dve", "reciprocal", out=rz[:, :W], in_=Z[:, :W], reads=[Z], writes=[rz])
            a_o = ao[qi % 2]
            p.op("dve", "tensor_tensor", out=a_o[:, :W], in0=O[:, :W], in1=rz[:, :W], op=ALU.mult, reads=[O, rz],
                 writes=[a_o])
            st_mix(p, d, 2 + hh, a_o, a_o[:, :W], c0, W, mc)
    p.close_scope()

    if stage < 3:
        return
    p.open_scope()
    ft = p.sbuf("ft", [128, 512], F32)
    qt = p.sbuf("qt", [128, 512], F32)
    lf = p.sbuf("lf", [128, 512], F32)
    lfm = p.sbuf("lfm", [128, 514], F32)
    kk = p.sbuf("kk", [128, 512], F32)
    bb = p.sbuf("bb", [128, 512], F32)
    cc = p.sbuf("cc", [128, 512], F32)
    eb = p.sbuf("eb", [128, 512], F32)
    enb = p.sbuf("enb", [128, 512], F32)
    dd_ = p.sbuf("ddh", [128, 512], F32)
    ed = p.sbuf("ed", [128, 512], F32)
    ec = p.sbuf("ec", [128, 512], F32)
    qe = p.sbuf("qe", [128, 512], BF16)
    ke = p.sbuf("ke", [128, 512], BF16)
    kd = p.sbuf("kd", [128, 512], BF16)
    kdT = p.sbuf("kdT", [64, 8, 128], BF16)
    Vc = p.sbuf("Vc", [64, 8, 128], BF16)
    scm = [[p.sbuf("scm%d_%d" % (dr_, i), [64, 64], BF16) for i in range(2)] for dr_ in range(2)]
    for dr_ in range(2):
        for i in range(2):
            p.op("pool", "memset", ap=scm[dr_][i][:], constant=0.0, writes=[scm[dr_][i]])
    S32 = p.sbuf("S32", [128, 128], F32)
    Sbf = p.sbuf("Sbf", [128, 128], BF16)
    ot = p.sbuf("ot", [128, 512], F32)
    of = p.sbuf("of", [128, 512], F32)
    osq = p.sbuf("osq", [128, 512], BF16)
    ors = p.sbuf("ors", [128, 512], F32)
    sgt = p.sbuf("sgt", [128, 512], F32)
    oy = p.sbuf("oy", [128, 512], F32)
    ob = [p.sbuf("ob%d" % i, [128, 512], BF16) for i in range(2)]
    p.op("pool", "memset", ap=lfm[:], constant=0.0, writes=[lfm])
    for hh in range(2):
        for dr in range(2):
            order = list(range(len(tiles)))
            if dr == 1:
                order = [0] + order[:0:-1]
            p.op("dve", "memset", ap=S32[:], constant=0.0, writes=[S32])
            p.op("dve", "memset", ap=Sbf[:], constant=0.0, writes=[Sbf])
            mF, mB = (maskF, maskB) if dr == 0 else (maskB, maskF)
            for n, ti in enumerate(order):
                c0, W, mc = tiles[ti]
                nch = W // 64
                p.dma("sp", ft[:, :W], f_d[dr, hh, :, c0:c0 + W], reads=[f_d], writes=[ft])
                p.dma("sp", qt[:, :W], qh_d[hh, :, c0:c0 + W], reads=[qh_d], writes=[qt])
                for half in range(2):
                    p.dma("sp", Vc[:, half:nch:2, :], vt_d[half * 64:(half + 1) * 64, c0 // 128:(c0 + W) // 128,
                                                           hh * 128:(hh + 1) * 128], reads=[vt_d], writes=[Vc])
                p.op("act", "activation", out=lf[:, :W], in_=ft[:, :W], func=AF.Ln, reads=[ft], writes=[lf])
                p.op("dve", "tensor_scalar", out=kk[:, :W], in0=ft[:, :W], scalar1=-1.0, scalar2=1.0, op0=ALU.mult,
                     op1=ALU.add, reads=[ft], writes=[kk])
                p.op("dve", "tensor_tensor", out=lfm[:, 1:W + 1], in0=lf[:, :W], in1=mF[:, :W], op=ALU.mult,
                     reads=[lf, mF], writes=[lfm])
                if dr == 0:
                    p.op("dve", "tensor_tensor_scan", out=bb[:, 0:W], data0=mF[:, 0:W], data1=lf[:, 0:W], initial=0.0,
                         op0=ALU.mult, op1=ALU.add, reads=[mF, lf], writes=[bb])
                    p.op("dve", "tensor_tensor_scan", out=cc[:, 0:W][:, ::-1], data0=mB[:, 0:W][:, ::-1],
                         data1=lfm[:, 2:W + 2][:, ::-1], initial=0.0, op0=ALU.mult, op1=ALU.add, reads=[mB, lfm],
                         writes=[cc])
                else:
                    p.op("dve", "tensor_tensor_scan", out=bb[:, 0:W][:, ::-1], data0=mF[:, 0:W][:, ::-1],
                         data1=lf[:, 0:W][:, ::-1], initial=0.0, op0=ALU.mult, op1=ALU.add, reads=[mF, lf], writes=[bb])
                    p.op("dve", "tensor_tensor_scan", out=cc[:, 0:W], data0=mB[:, 0:W], data1=lfm[:, 0:W], initial=0.0,
                         op0=ALU.mult, op1=ALU.add, reads=[mB, lfm], writes=[cc])
                p.op("act", "activation", out=eb[:, :W], in_=bb[:, :W], func=AF.Exp, reads=[bb], writes=[eb])
                for ci in range(nch):
                    apos = ci * 64 + (31 if dr == 0 else 32)
                    p.op("dve", "tensor_scalar", out=dd_[:, ci * 64:ci * 64 + 64], in0=bb[:, ci * 64:ci * 64 + 64],
                         scalar1=bb[:, apos:apos + 1], scalar2=None, op0=ALU.subtract, reads=[bb], writes=[dd_])
                p.op("act", "activation", out=ed[:, :W], in_=dd_[:, :W], func=AF.Exp, reads=[dd_], writes=[ed])
                p.op("act", "activation", out=enb[:, :W], in_=dd_[:, :W], func=AF.Exp, scale=-1.0, reads=[dd_],
                     writes=[enb])
                p.op("act", "activation", out=ec[:, :W], in_=cc[:, :W], func=AF.Exp, reads=[cc], writes=[ec])
                p.op("dve", "tensor_tensor", out=qe[:, :W], in0=qt[:, :W], in1=ed[:, :W], op=ALU.mult, reads=[qt, ed],
                     writes=[qe])
                p.op("pool", "tensor_tensor", out=ke[:, :W], in0=kk[:, :W], in1=enb[:, :W], op=ALU.mult, reads=[kk, enb],
                     writes=[ke])
                p.op("pool", "tensor_tensor", out=kd[:, :W], in0=kk[:, :W], in1=ec[:, :W], op=ALU.mult, reads=[kk, ec],
                     writes=[kd])
                for ci in range(nch):
                    pt = cx.bank()
                    mm(p, pt, pt[:64, :128], kd, kd[:, ci * 64:(ci + 1) * 64], ident, ident[:], True, True)
                    p.op("act", "activation", out=kdT[:, ci, :], in_=pt[:64, :128], func=AF.Copy, reads=[pt],
                         writes=[kdT])
                corder = list(range(nch)) if dr == 0 else list(range(nch - 1, -1, -1))
                if stage < 4:
                    continue
                for k_, ci in enumerate(corder):
                    cs = slice(ci * 64, ci * 64 + 64)
                    lastpos = ci * 64 + (63 if dr == 0 else 0)
                    f0 = 0 if dr == 0 else 32
                    s0 = 32 - f0
                    apos = ci * 64 + (31 if dr == 0 else 32)
                    ps1 = cx.bank()
                    mm(p, ps1, ps1[:64, s0:s0 + 32], ke, ke[:, cs], qe, qe[:, ci * 64 + s0:ci * 64 + s0 + 32], True, True)
                    pf = cx.bank()
                    mm(p, pf, pf[f0:f0 + 32, f0:f0 + 32], ke, ke[:, ci * 64 + f0:ci * 64 + f0 + 32], qe,
                       qe[:, ci * 64 + f0:ci * 64 + f0 + 32], True, True)
                    sm = scm[dr][k_ % 2]
                    p.op("dve", "tensor_tensor", out=sm[:, s0:s0 + 32], in0=ps1[:64, s0:s0 + 32], in1=tri[:, dr, s0:s0 + 32],
                         op=ALU.mult, reads=[ps1, tri], writes=[sm])
                    p.op("dve", "tensor_tensor", out=sm[f0:f0 + 32, f0:f0 + 32], in0=pf[f0:f0 + 32, f0:f0 + 32],
                         in1=tri[f0:f0 + 32, dr, f0:f0 + 32], op=ALU.mult, reads=[pf, tri], writes=[sm])
                    p.op("pool", "tensor_scalar", out=Sbf[:], in0=S32[:], scalar1=eb[:, apos:apos + 1], scalar2=None,
                         op0=ALU.mult, reads=[S32, eb], writes=[Sbf])
                    ps2 = cx.bank()
                    mm(p, ps2, ps2[:, :64], Sbf, Sbf[:], qe, qe[:, cs], True, False)
                    mm(p, ps2, ps2[:, :64], Vc, Vc[:, ci, :], sm, sm[:, :], False, True)
                    p.op("act", "activation", out=ot[:, cs], in_=ps2[:, :64], func=AF.Copy, reads=[ps2], writes=[ot])
                    ps3 = cx.bank()
                    mm(p, ps3, ps3[:, :128], kdT, kdT[:, ci, :], Vc, Vc[:, ci, :], True, True)
                    p.op("dve", "scalar_tensor_tensor", out=S32[:], in0=S32[:], scalar=eb[:, lastpos:lastpos + 1],
                         in1=ps3[:, :128], op0=ALU.mult, op1=ALU.add, reads=[S32, eb, ps3], writes=[S32])
                if dr == 0:
                    p.dma("pool", o_d[hh, :, c0:c0 + W], ot[:, :W], reads=[ot], writes=[o_d])
                else:
                    p.dma("sp", of[:, :W], o_d[hh, :, c0:c0 + W], reads=[o_d], writes=[of])
                    p.dma("sp", sgt[:, :W], sg_d[hh, :, c0:c0 + W], reads=[sg_d], writes=[sgt])
                    p.op("dve", "tensor_tensor", out=of[:, :W], in0=of[:, :W], in1=ot[:, :W], op=ALU.add, reads=[of, ot],
                         writes=[of])
                    p.op("act", "activation", out=osq[:, :W], in_=of[:, :W], func=AF.Square, reads=[of], writes=[osq])
                    ps4 = cx.bank()
                    mm(p, ps4, ps4[:, :W], cx.ones_bf, cx.ones_bf[:], osq, osq[:, :W], True, True)
                    p.op("act", "activation", out=ors[:, :W], in_=ps4[:, :W], func=AF.Sqrt, scale=1.0 / 128,
                         bias=cx.epsb[:], reads=[ps4, cx.epsb], writes=[ors])
                    p.op("dve", "reciprocal", out=ors[:, :W], in_=ors[:, :W], reads=[ors], writes=[ors])
                    p.op("dve", "scalar_tensor_tensor", out=oy[:, :W], in0=of[:, :W], scalar=nws[:, 0:1], in1=ors[:, :W],
                         op0=ALU.mult, op1=ALU.mult, reads=[of, nws, ors], writes=[oy])
                    o_b = ob[n % 2]
                    p.op("pool", "tensor_tensor", out=o_b[:, :W], in0=oy[:, :W], in1=sgt[:, :W], op=ALU.mult,
                         reads=[oy, sgt], writes=[o_b])
                    st_mix(p, d, hh, o_b, o_b[:, :W], c0, W, mc)
    p.close_scope()


def a1_dram(p, S, SL, pre="", fused=False):
    dd = {
        "svec": p.dram(pre + "svec", [128, NCH, 2], F32, kind="ExternalInput"),
        "modw": p.dram(pre + "modw", [2, NCH, 128, NCH, 128], F32, kind="ExternalInput"),
        "modb": p.dram(pre + "modb", [128, 2, NCH], F32, kind="ExternalInput"),
        "normw": p.dram(pre + "normw", [128, NCH], F32, kind="ExternalInput"),
        "win": p.dram(pre + "win", [128, NCH, 1792], F32, kind="ExternalInput"),
        "rmat": p.dram(pre + "rmat", [128, 128], F32, kind="ExternalInput"),
        "ident": p.dram(pre + "ident", [128, 128], F32, kind="ExternalInput"),
        "nws": p.dram(pre + "nws", [128, 3], F32, kind="ExternalInput"),
        "lbl": p.dram(pre + "lbl", [128, 2, 2, 2], F32, kind="ExternalInput"),
        "tri": p.dram(pre + "tri", [64, 2, 64], F32, kind="ExternalInput"),
        "cos": p.dram(pre + "cos", [128, SL], F32, kind="ExternalInput"),
        "sin": p.dram(pre + "sin", [128, SL], F32, kind="ExternalInput"),
    }
    dd["mixo"] = p.dram(pre + "mixo", [4, 128, S], BF16, kind="Internal" if fused else "ExternalOutput")
    if not fused:
        dd["xa"] = p.dram(pre + "xa", [128, NCH, S], F32, kind="ExternalInput")
    return dd


def make_A1(S, CT, stage=9):
    nc = bass.Bass("TRN2", target_bir_lowering=False)
    p = Prog(nc)
    cx = Ctx(p)
    d = a1_dram(p, S, S - CT)
    build_A1(p, cx, d, S, CT, stage)
    p.finish([d["mixo"]])
    return nc, p


def host_A1_inputs(j, c_b, c_ctx, mod_w, mod_b, norm_mix_w, cd_w_in, lb_logits, hgrn_norm_w, q_norm_w, k_norm_w, SL):
    l = 1
    comps = [0, 1]
    modw = np.stack([blk_lhsT(mod_w[l][:, k * D:(k + 1) * D]) for k in comps])
    modb = np.stack([fm(mod_b[l][k * D:(k + 1) * D]) for k in comps], axis=1)
    svec = np.stack([fm(c_b), fm(c_ctx)], axis=2)
    W = cd_w_in[0]
    r = np.arange(256 * j, 256 * j + 256)
    g = j // 2
    cols = np.concatenate([r, 1024 + r, 2048 + r, 4096 + r, 5120 + r, 6144 + np.arange(128 * g, 128 * g + 128),
                           3072 + r, 6400 + np.arange(128 * g, 128 * g + 128)])
    win = np.ascontiguousarray(W[:, cols].reshape(NCH, 128, 1792).transpose(1, 0, 2))
    heads = [2 * j, 2 * j + 1]
    lbl = np.stack([lb_logits[:, :, h * 128:(h + 1) * 128] for h in heads], axis=3)
    lbl = np.ascontiguousarray(lbl.transpose(2, 0, 1, 3))
    cos, sin = rope_tables(SL, 128)
    s_ = np.arange(64)[:, None]
    t_ = np.arange(64)[None, :]
    tri = np.stack([(s_ <= t_), (s_ >= t_)], axis=1).astype(np.float32)
    return {
        "svec": np.ascontiguousarray(svec), "modw": modw, "modb": np.ascontiguousarray(modb),
        "normw": fm(norm_mix_w[l]), "win": win, "rmat": rot_matrix(128), "ident": np.eye(128, dtype=np.float32),
        "nws": np.ascontiguousarray(np.stack([hgrn_norm_w[0], q_norm_w[0], k_norm_w[0]], axis=1)),
        "lbl": lbl, "tri": np.ascontiguousarray(tri), "cos": cos, "sin": sin,
    }


SEQ = 16384
CTX = 256
NCORE = 8
_CACHE = {}


def _prog(key, fn):
    if key not in _CACHE:
        _CACHE[key] = fn()
    return _CACHE[key]


def _dbg(tag, arrs):
    import os, sys
    if not os.environ.get("KDEBUG"):
        return
    for i, a in enumerate(arrs):
        a = np.asarray(a, dtype=np.float32)
        bad = ~np.isfinite(a)
        msg = "[kdebug] %s core %d nonfinite=%d rms=%.4g" % (tag, i, int(bad.sum()), float(np.sqrt(np.mean(np.where(bad, 0, a) ** 2))))
        if bad.any():
            idx = np.argwhere(bad)
            msg += " first=%s chunks=%s" % (idx[0].tolist(), sorted(set(idx[:, 0].tolist()))[:8])
        print(msg, file=sys.stderr, flush=True)


def _from_fm(a):
    P, C, N = a.shape
    return a.transpose(2, 1, 0).reshape(N, C * P)


def kernel_unfused(x, c, ctx, c_ctx, mod_w, mod_b, norm_mix_w, norm_ffn_w, ffn_w_gate, ffn_w_up, ffn_w_down,
           ab_w_in, ab_w_out, lru_conv_w, lru_conv_b, lru_wa, lru_ba, lru_wx, lru_bx, lru_lambda,
           diff_lq1, diff_lk1, diff_lq2, diff_lk2, diff_subln_w,
           cd_w_in, cd_w_out, hgrn_lb_logits, hgrn_norm_w, gqa_q_norm_w, gqa_k_norm_w, final_norm_w):
    f = lambda a: np.asarray(a, dtype=np.float32)
    (x, c, ctx, c_ctx, mod_w, mod_b, norm_mix_w, norm_ffn_w, ffn_w_gate, ffn_w_up, ffn_w_down, ab_w_in, ab_w_out,
     lru_conv_w, lru_conv_b, lru_wa, lru_ba, lru_wx, lru_bx, lru_lambda, diff_lq1, diff_lk1, diff_lq2, diff_lk2,
     diff_subln_w, cd_w_in, cd_w_out, hgrn_lb_logits, hgrn_norm_w, gqa_q_norm_w, gqa_k_norm_w, final_norm_w) = map(f, (
        x, c, ctx, c_ctx, mod_w, mod_b, norm_mix_w, norm_ffn_w, ffn_w_gate, ffn_w_up, ffn_w_down, ab_w_in, ab_w_out,
        lru_conv_w, lru_conv_b, lru_wa, lru_ba, lru_wx, lru_bx, lru_lambda, diff_lq1, diff_lk1, diff_lq2, diff_lk2,
        diff_subln_w, cd_w_in, cd_w_out, hgrn_lb_logits, hgrn_norm_w, gqa_q_norm_w, gqa_k_norm_w, final_norm_w))
    B = x.shape[0]
    SL = x.shape[1]
    CT = ctx.shape[1]
    S = CT + SL
    QL = SL // 4
    QC = CT // 4
    cores = list(range(NCORE))

    def mix_assemble(res, with_ctx):
        outs = []
        for core in cores:
            b, q = core // 4, core % 4
            full = np.empty((NCH, 128, S), dtype=ml_dtypes.bfloat16)
            for j in range(4):
                o = res[b * 4 + j]["mixo"]
                full[2 * j] = o[0]
                full[2 * j + 1] = o[1]
                full[8 + 2 * j] = o[2]
                full[8 + 2 * j + 1] = o[3]
            parts = [full[:, :, CT + q * QL:CT + (q + 1) * QL]]
            if with_ctx:
                parts.append(full[:, :, q * QC:(q + 1) * QC])
            outs.append(np.ascontiguousarray(np.concatenate(parts, axis=2).transpose(1, 0, 2)))
        return outs

    nc, _ = _prog(("A0", S, CT), lambda: make_A0(S, CT))
    xa = [fm_act(np.concatenate([ctx[b], x[b]], 0).T) for b in range(B)]
    maps = []
    for core in cores:
        b, j = core // 4, core % 4
        h = host_A0_inputs(j, c[b], c_ctx, mod_w, mod_b, norm_mix_w, ab_w_in, lru_conv_w, lru_conv_b, lru_wa, lru_ba,
                           lru_wx, lru_bx, lru_lambda, diff_lq1, diff_lk1, diff_lq2, diff_lk2, diff_subln_w, SL)
        h["xa"] = xa[b]
        maps.append(h)
    resA0 = run_bass_kernel_spmd(nc, maps, core_ids=cores).results
    del maps
    _dbg("A0 mixo", [r["mixo"] for r in resA0])
    NT0 = QL + QC
    tiles0 = [(i * 512, 512, 0) for i in range(QL // 512)] + [(QL, QC, 1)]
    nc, _ = _prog(("B", NT0, 0), lambda: make_B(NT0, tiles0, False))
    mixs = mix_assemble(resA0, True)
    del resA0
    wB = [host_B_inputs(0, c[b], c_ctx, mod_w, mod_b, norm_ffn_w, ab_w_out[0], ffn_w_gate, ffn_w_up, ffn_w_down)
          for b in range(B)]
    maps = []
    for core in cores:
        b, q = core // 4, core % 4
        h = dict(wB[b])
        xr = np.concatenate([x[b][q * QL:(q + 1) * QL], ctx[b][q * QC:(q + 1) * QC]], 0)
        h["xres"] = fm_act(xr.T)
        h["mix"] = mixs[core]
        maps.append(h)
    resB0 = run_bass_kernel_spmd(nc, maps, core_ids=cores).results
    del maps, wB, mixs
    _dbg("B0 xout", [r["xout"] for r in resB0])
    nc, _ = _prog(("A1", S, CT), lambda: make_A1(S, CT))
    x1q = [resB0[core]["xout"] for core in cores]
    del resB0
    xa = []
    for b in range(B):
        lat = np.concatenate([x1q[b * 4 + q][:, :, :QL] for q in range(4)], axis=2)
        cc = np.concatenate([x1q[b * 4 + q][:, :, QL:] for q in range(4)], axis=2)
        xa.append(np.ascontiguousarray(np.concatenate([cc, lat], axis=2)))
    maps = []
    for core in cores:
        b, j = core // 4, core % 4
        h = host_A1_inputs(j, c[b], c_ctx, mod_w, mod_b, norm_mix_w, cd_w_in, hgrn_lb_logits, hgrn_norm_w,
                           gqa_q_norm_w, gqa_k_norm_w, SL)
        h["xa"] = xa[b]
        maps.append(h)
    resA1 = run_bass_kernel_spmd(nc, maps, core_ids=cores).results
    del maps, xa
    _dbg("A1 mixo", [r["mixo"][:, :, CT:] for r in resA1])
    tiles1 = [(i * 512, 512, 0) for i in range(QL // 512)]
    nc, _ = _prog(("B", QL, 1), lambda: make_B(QL, tiles1, True))
    mixs = mix_assemble(resA1, False)
    del resA1
    wB = [host_B_inputs(1, c[b], c_ctx, mod_w, mod_b, norm_ffn_w, cd_w_out[0], ffn_w_gate, ffn_w_up, ffn_w_down,
                        final_norm_w) for b in range(B)]
    maps = []
    for core in cores:
        b, q = core // 4, core % 4
        h = dict(wB[b])
        h["xres"] = np.ascontiguousarray(x1q[core][:, :, :QL])
        h["mix"] = mixs[core]
        maps.append(h)
    resB1 = run_bass_kernel_spmd(nc, maps, core_ids=cores).results
    out = np.empty((B, SL, D), dtype=np.float32)
    for core in cores:
        b, q = core // 4, core % 4
        out[b, q * QL:(q + 1) * QL] = _from_fm(resB1[core]["xout"])
    return out


GROUPS = [[0, 1, 2, 3], [4, 5, 6, 7]]


def b_dram(p, pre, last):
    d = {
        "svec": p.dram(pre + "svec", [128, NCH, 2], F32, kind="ExternalInput"),
        "modw": p.dram(pre + "modw", [4, NCH, 128, NCH, 128], F32, kind="ExternalInput"),
        "modb": p.dram(pre + "modb", [128, 4, NCH], F32, kind="ExternalInput"),
        "normw": p.dram(pre + "normw", [128, NCH], F32, kind="ExternalInput"),
        "wout": p.dram(pre + "wout", [NCH, 128, NCH, 128], F32, kind="ExternalInput"),
        "wgu": p.dram(pre + "wgu", [NJ, 128, 2, NCH, 128], F32, kind="ExternalInput"),
        "wdn": p.dram(pre + "wdn", [NCH, 128, NJ, 128], F32, kind="ExternalInput"),
    }
    if last:
        d["finalw"] = p.dram(pre + "finalw", [128, NCH], F32, kind="ExternalInput")
    return d


def fc_src(fc):
    if fc < 8:
        return fc % 2, fc // 2
    return 2 + (fc - 8) % 2, (fc - 8) // 2


def make_fused(S, CT):
    SL = S - CT
    QL = SL // 4
    QC = CT // 4
    HL = QL // 2
    nc = bass.Bass("TRN2", target_bir_lowering=False)
    p = Prog(nc)
    cx = Ctx(p)
    oh_d = p.dram("oh", [128, 4], F32, kind="ExternalInput")
    oh = p.sbuf("oh", [128, 4], F32)
    p.dma("sp", oh[:], oh_d[:, :], writes=[oh])

    def mix_loader(Gl, Gc):
        def ld(mx, c0, W, mc):
            FG = 2
            for grp in range(NCH // FG):
                buf = ld.buf
                for qq in range(4):
                    for f_ in range(FG):
                        ch, rk = fc_src(grp * FG + f_)
                        if mc == 0:
                            src = Gl[ch, qq, rk * 128:(rk + 1) * 128, c0:c0 + W]
                            p.dma("sp", buf[:, qq, f_, :W], src, reads=[Gl], writes=[buf])
                        else:
                            src = Gc[ch, rk * 128:(rk + 1) * 128, qq * QC:(qq + 1) * QC]
                            p.dma("sp", buf[:, qq, f_, :W], src, reads=[Gc], writes=[buf])
                dst = mx[:, grp * FG:(grp + 1) * FG, :W]
                p.op("dve", "tensor_scalar", out=dst, in0=buf[:, 0, :, :W], scalar1=oh[:, 0:1], scalar2=None,
                     op0=ALU.mult, reads=[buf, oh], writes=[mx])
                for qq in range(1, 4):
                    p.op("dve", "scalar_tensor_tensor", out=dst, in0=buf[:, qq, :, :W], scalar=oh[:, qq:qq + 1], in1=dst,
                         op0=ALU.mult, op1=ALU.add, reads=[buf, oh, mx], writes=[mx])
        return ld

    dB0 = b_dram(p, "b0_", False)
    dB1 = b_dram(p, "b1_", True)
    prep_B_weights(p, dB0)
    prep_B_weights(p, dB1)

    def mix_store(mixc, mixl):
        def st(ch, src_t, src_ap, c0, W, mc):
            if mc == 1:
                p.dma("pool", mixc[ch, :, c0:c0 + W], src_ap, reads=[src_t], writes=[mixc])
            else:
                col = c0 - CT
                p.dma("pool", mixl[ch, col // QL, :, col % QL:col % QL + W], src_ap, reads=[src_t], writes=[mixl])
        return st

    dA0 = a0_dram(p, S, SL, pre="a0_", fused=True)
    m0c = p.dram("m0c", [4, 128, CT], BF16)
    m0l = p.dram("m0l", [4, 4, 128, QL], BF16)
    dA0["st_mix"] = mix_store(m0c, m0l)
    p.open_scope()
    build_A0(p, cx, dA0, S, CT, 0.8 - 0.6 * math.exp(0.0))
    p.close_scope()
    G1l = p.dram("G1l", [4, 4, 512, QL], BF16)
    G1c = p.dram("G1c", [4, 512, CT], BF16)
    for ch in range(4):
        p.coll(m0c, m0c[ch].opt(), G1c, G1c[ch].opt(), GROUPS)
        for qq in range(4):
            p.coll(m0l, m0l[ch, qq].opt(), G1l, G1l[ch, qq].opt(), GROUPS)
    x1_d = p.dram("x1_d", [NCH, 2, 128, HL], F32)
    xc1_d = p.dram("xc1_d", [128, NCH, QC], F32)
    dB0["xres"] = p.dram("b0_xres", [128, NCH, QL + QC], F32, kind="ExternalInput")
    p.open_scope()
    ld0 = mix_loader(G1l, G1c)
    ld0.buf = p.sbuf("selbuf", [128, 4, 2, 512], BF16)
    dB0["ld_mix"] = ld0

    def st0(x1, c0, W, mc):
        if mc == 0:
            p.dma("pool", x1_d[:, c0 // HL, :, c0 % HL:c0 % HL + W].rearrange("c p w -> p c w"), x1[:, :, :W],
                  reads=[x1], writes=[x1_d])
        else:
            p.dma("pool", xc1_d[:, :, :], x1[:, :, :W], reads=[x1], writes=[xc1_d])
    dB0["st_x"] = st0
    tiles0 = [(i * 512, 512, 0) for i in range(QL // 512)] + [(QL, QC, 1)]
    build_B(p, cx, dB0, tiles0, False)
    p.close_scope()
    G2l = p.dram("G2l", [NCH, 2, 512, HL], F32)
    G2c = p.dram("G2c", [512, NCH * QC], F32)
    for c in range(NCH):
        for hf in range(2):
            p.coll(x1_d, x1_d[c, hf].opt(), G2l, G2l[c, hf].opt(), GROUPS)
    p.coll(xc1_d, xc1_d[:, :, :].rearrange("p c w -> p (c w)").opt(), G2c, G2c[:, :].opt(), GROUPS)
    dA1 = a1_dram(p, S, SL, pre="a1_", fused=True)

    def ld_xa1(xt, c0, W, mc):
        if mc == 1:
            for r in range(4):
                p.dma("sp", xt[:, :, r * QC:(r + 1) * QC],
                      G2c[r * 128:(r + 1) * 128, :].rearrange("p (c w) -> p c w", c=NCH), reads=[G2c], writes=[xt])
        else:
            col = c0 - CT
            r, within = col // QL, col % QL
            hf, off = within // HL, within % HL
            for c in range(NCH):
                p.dma("sp", xt[:, c, :W], G2l[c, hf, r * 128:(r + 1) * 128, off:off + W], reads=[G2l], writes=[xt])
    dA1["ld_xa"] = ld_xa1
    m1c = p.dram("m1c", [4, 128, CT], BF16)
    m1l = p.dram("m1l", [4, 4, 128, QL], BF16)
    dA1["st_mix"] = mix_store(m1c, m1l)
    p.open_scope()
    build_A1(p, cx, dA1, S, CT)
    p.close_scope()
    G3l = p.dram("G3l", [4, 4, 512, QL], BF16)
    for ch in range(4):
        for qq in range(4):
            p.coll(m1l, m1l[ch, qq].opt(), G3l, G3l[ch, qq].opt(), GROUPS)
    dB1["xout"] = p.dram("b1_xout", [128, NCH, QL], F32, kind="ExternalOutput")
    p.open_scope()
    ld1 = mix_loader(G3l, None)
    ld1.buf = p.sbuf("selbuf1", [128, 4, 2, 512], BF16)
    dB1["ld_mix"] = ld1

    def ldx1(x1, c0, W, mc):
        p.dma("sp", x1[:, :, :W], x1_d[:, c0 // HL, :, c0 % HL:c0 % HL + W].rearrange("c p w -> p c w"),
              reads=[x1_d], writes=[x1])
    dB1["ld_x"] = ldx1
    tiles1 = [(i * 512, 512, 0) for i in range(QL // 512)]
    build_B(p, cx, dB1, tiles1, True)
    p.close_scope()
    p.finish([dB1["xout"]])
    return nc, p


def kernel(x, c, ctx, c_ctx, mod_w, mod_b, norm_mix_w, norm_ffn_w, ffn_w_gate, ffn_w_up, ffn_w_down,
           ab_w_in, ab_w_out, lru_conv_w, lru_conv_b, lru_wa, lru_ba, lru_wx, lru_bx, lru_lambda,
           diff_lq1, diff_lk1, diff_lq2, diff_lk2, diff_subln_w,
           cd_w_in, cd_w_out, hgrn_lb_logits, hgrn_norm_w, gqa_q_norm_w, gqa_k_norm_w, final_norm_w):
    f = lambda a: np.asarray(a, dtype=np.float32)
    (x, c, ctx, c_ctx, mod_w, mod_b, norm_mix_w, norm_ffn_w, ffn_w_gate, ffn_w_up, ffn_w_down, ab_w_in, ab_w_out,
     lru_conv_w, lru_conv_b, lru_wa, lru_ba, lru_wx, lru_bx, lru_lambda, diff_lq1, diff_lk1, diff_lq2, diff_lk2,
     diff_subln_w, cd_w_in, cd_w_out, hgrn_lb_logits, hgrn_norm_w, gqa_q_norm_w, gqa_k_norm_w, final_norm_w) = map(f, (
        x, c, ctx, c_ctx, mod_w, mod_b, norm_mix_w, norm_ffn_w, ffn_w_gate, ffn_w_up, ffn_w_down, ab_w_in, ab_w_out,
        lru_conv_w, lru_conv_b, lru_wa, lru_ba, lru_wx, lru_bx, lru_lambda, diff_lq1, diff_lk1, diff_lq2, diff_lk2,
        diff_subln_w, cd_w_in, cd_w_out, hgrn_lb_logits, hgrn_norm_w, gqa_q_norm_w, gqa_k_norm_w, final_norm_w))
    B, SL = x.shape[0], x.shape[1]
    CT = ctx.shape[1]
    S = CT + SL
    QL, QC = SL // 4, CT // 4
    cores = list(range(NCORE))
    nc, _ = _prog(("fused", S, CT), lambda: make_fused(S, CT))
    xa = [fm_act(np.concatenate([ctx[b], x[b]], 0).T) for b in range(B)]
    wB0 = [host_B_inputs(0, c[b], c_ctx, mod_w, mod_b, norm_ffn_w, ab_w_out[0], ffn_w_gate, ffn_w_up, ffn_w_down)
           for b in range(B)]
    wB1 = [host_B_inputs(1, c[b], c_ctx, mod_w, mod_b, norm_ffn_w, cd_w_out[0], ffn_w_gate, ffn_w_up, ffn_w_down,
                         final_norm_w) for b in range(B)]
    maps = []
    for core in cores:
        b, j = core // 4, core % 4
        m = {}
        hA0 = host_A0_inputs(j, c[b], c_ctx, mod_w, mod_b, norm_mix_w, ab_w_in, lru_conv_w, lru_conv_b, lru_wa,
                             lru_ba, lru_wx, lru_bx, lru_lambda, diff_lq1, diff_lk1, diff_lq2, diff_lk2,
                             diff_subln_w, SL)
        hA0["xa"] = xa[b]
        for k, v in hA0.items():
            m["a0_" + k] = v
        for k, v in wB0[b].items():
            m["b0_" + k] = v
        xr = np.concatenate([x[b][j * QL:(j + 1) * QL], ctx[b][j * QC:(j + 1) * QC]], 0)
        m["b0_xres"] = fm_act(xr.T)
        hA1 = host_A1_inputs(j, c[b], c_ctx, mod_w, mod_b, norm_mix_w, cd_w_in, hgrn_lb_logits, hgrn_norm_w,
                             gqa_q_norm_w, gqa_k_norm_w, SL)
        for k, v in hA1.items():
            m["a1_" + k] = v
        for k, v in wB1[b].items():
            m["b1_" + k] = v
        ohv = np.zeros((128, 4), np.float32)
        ohv[:, j] = 1.0
        m["oh"] = ohv
        maps.append(m)
    res = run_bass_kernel_spmd(nc, maps, core_ids=cores).results
    out = np.empty((B, SL, D), dtype=np.float32)
    for core in cores:
        b, q = core // 4, core % 4
        out[b, q * QL:(q + 1) * QL] = _from_fm(res[core]["b1_xout"])
    return out
```

```python
import math
from contextlib import ExitStack
import numpy as np
import ml_dtypes
import concourse.bass as bass
import concourse.mybir as mybir
from concourse.bass_utils import run_bass_kernel_spmd

F32 = mybir.dt.float32
BF16 = mybir.dt.bfloat16
AF = mybir.ActivationFunctionType
ALU = mybir.AluOpType
AX = mybir.AxisListType

SEM_LIMIT = 30000
LAG = 2

D = 2048
NCH = 16
FF = 5632
NJ = 44
EPS = 1e-6


class T:
    __slots__ = ("t", "w", "r", "dsem", "dcnt", "name", "ap")

    def __init__(self, t, name, ap=None):
        self.t = t
        self.name = name
        self.w = None
        self.r = []
        self.dsem = None
        self.dcnt = 0
        self.ap = ap

    def __getitem__(self, idx):
        if self.ap is not None:
            return self.ap[idx]
        return self.t[idx]


class Prog:
    ENGS = ("pe", "act", "dve", "pool", "sp")

    def __init__(self, nc):
        self.nc = nc
        self.es = ExitStack()
        self.q = {e: [] for e in self.ENGS}
        self.cnt = {e: 0 for e in self.ENGS}
        self.sem = {}
        self.semobj = {}
        self.nsem = 0
        self.waited = {e: {} for e in self.ENGS}
        self.ninst = 0
        self.nm = 0
        self.scopes = []
        self.inherit = {}
        self.free_dsems = []
        for e in ("pe", "act", "dve", "pool"):
            self._new_eng_sem(e)

    def _alloc_sem(self, name):
        self.nsem += 1
        key = "%s_%d" % (name, self.nsem)
        s = self.es.enter_context(self.nc.semaphore(key))
        self.semobj[key] = s
        return key

    def _new_eng_sem(self, e):
        self.sem[e] = self._alloc_sem("s_" + e)
        self.cnt[e] = 0

    def uniq(self, name):
        self.nm += 1
        return "%s_%d" % (name, self.nm)

    def sbuf(self, name, shape, dt):
        name = self.uniq(name)
        st = self.scopes[-1][0] if self.scopes else self.es
        t = st.enter_context(self.nc.sbuf_tensor(name, list(shape), dt))
        tt = T(t, name)
        tt.r = list(self.inherit.items())
        if self.scopes:
            self.scopes[-1][1].append(tt)
        return tt

    def open_scope(self):
        self.scopes.append((ExitStack(), []))

    def close_scope(self):
        st, lst = self.scopes.pop()
        for t in lst:
            deps = list(t.r)
            if t.w is not None:
                deps.append(t.w)
            for k, v in deps:
                if self.inherit.get(k, 0) < v:
                    self.inherit[k] = v
            if t.dsem is not None:
                self.free_dsems.append((t.dsem, t.dcnt))
                t.dsem = None
        st.close()

    def psum(self, name, shape, dt=F32):
        name = self.uniq(name)
        t = self.es.enter_context(self.nc.psum_tensor(name, list(shape), dt))
        return T(t, name)

    def dram(self, name, shape, dt, kind="Internal"):
        t = self.nc.dram_tensor(name, list(shape), dt, kind=kind)
        tt = T(t, name, ap=t.ap())
        if self.scopes and kind == "Internal":
            self.scopes[-1][1].append(tt)
        return tt

    def _deps(self, reads, writes):
        need = {}
        for t in reads:
            if t.w is not None:
                s, v = t.w
                if need.get(s, 0) < v:
                    need[s] = v
        for t in writes:
            if t.w is not None:
                s, v = t.w
                if need.get(s, 0) < v:
                    need[s] = v
            for (s, v) in t.r:
                if need.get(s, 0) < v:
                    need[s] = v
        return need

    def _emit_waits(self, e, need):
        wd = self.waited[e]
        for s, v in need.items():
            if wd.get(s, 0) >= v:
                continue
            wd[s] = v
            so = self.semobj[s]
            self.q[e].append(lambda en, so=so, v=v: en.wait_ge(so, v))
            self.ninst += 1

    def _mark(self, dep, reads, writes):
        for t in writes:
            t.w = dep
            t.r = []
        for t in reads:
            t.r.append(dep)
            if len(t.r) > 32:
                m = {}
                for s, v in t.r:
                    if m.get(s, 0) < v:
                        m[s] = v
                t.r = list(m.items())

    def op(self, e, fn, reads=(), writes=(), **kw):
        need = self._deps(reads, writes)
        if e == "pe":
            need = {k: v for k, v in need.items() if not k.startswith("s_pe")}
        self._emit_waits(e, need)
        if self.cnt[e] >= SEM_LIMIT:
            self._new_eng_sem(e)
        self.cnt[e] += 1
        key = self.sem[e]
        so = self.semobj[key]
        self.q[e].append(lambda en, fn=fn, so=so, kw=kw: getattr(en, fn)(**kw).then_inc(so, 1))
        self.ninst += 1
        dep = (key, self.cnt[e])
        self._mark(dep, reads, writes)
        return dep

    def dma(self, e, out, in_, reads=(), writes=(), **kw):
        need = self._deps(reads, writes)
        self._emit_waits(e, need)
        d = writes[0]
        if d.dsem is None and self.free_dsems:
            d.dsem, d.dcnt = self.free_dsems.pop()
        if d.dsem is None or d.dcnt >= SEM_LIMIT * 16:
            d.dsem = self._alloc_sem("d_" + d.name)
            d.dcnt = 0
        d.dcnt += 16
        so = self.semobj[d.dsem]
        self.q[e].append(
            lambda en, so=so, out=out, in_=in_, kw=kw: en.dma_start(out=out, in_=in_, **kw).then_inc(so, 16))
        self.ninst += 1
        dep = (d.dsem, d.dcnt)
        self._mark(dep, reads, writes)
        return dep

    def coll(self, src_t, src_ap, dst_t, dst_ap, groups):
        need = self._deps([src_t], [dst_t])
        self._emit_waits("pool", need)
        d = dst_t
        if d.dsem is None:
            d.dsem = self._alloc_sem("c_" + d.name)
            d.dcnt = 0
        d.dcnt += 1
        so = self.semobj[d.dsem]
        self.q["pool"].append(lambda en, so=so, a=src_ap, b=dst_ap, g=groups: en.collective_compute(
            "AllGather", ALU.bypass, replica_groups=g, ins=[a], outs=[b]).then_inc(so))
        self.ninst += 1
        dep = (d.dsem, d.dcnt)
        self._mark(dep, [src_t], [dst_t])
        return dep

    def finish(self, out_tiles):
        need = {}
        for t in out_tiles:
            if t.w is not None:
                s, v = t.w
                if need.get(s, 0) < v:
                    need[s] = v
        self._emit_waits("sp", need)
        q = self.q
        with self.nc.Block() as block:
            @block.sync
            def _(en):
                for f in q["sp"]:
                    f(en)

            @block.tensor
            def _(en):
                for f in q["pe"]:
                    f(en)

            @block.scalar
            def _(en):
                for f in q["act"]:
                    f(en)

            @block.vector
            def _(en):
                for f in q["dve"]:
                    f(en)

            @block.gpsimd
            def _(en):
                for f in q["pool"]:
                    f(en)
        self.es.close()


class Ctx:
    def __init__(self, p):
        self.p = p
        self.banks = [p.psum("bank%d" % i, [128, 512]) for i in range(6)]
        self.bank_i = 0
        self.pmisc = p.psum("pmisc", [128, 512])
        self.banks.append(p.psum("bank6", [128, 512]))
        self.ones_bf = p.sbuf("ones_bf", [128, 128], BF16)
        p.op("dve", "memset", ap=self.ones_bf[:], constant=1.0, writes=[self.ones_bf])
        self.epsb = p.sbuf("epsb", [128, 1], F32)
        p.op("dve", "memset", ap=self.epsb[:], constant=EPS, writes=[self.epsb])

    def bank(self):
        b = self.banks[self.bank_i % len(self.banks)]
        self.bank_i += 1
        return b

    def bankbf(self):
        self.bf_i += 1
        return self.bft[self.bf_i % 2]


def mm(p, out_t, out_ap, lhsT_t, lhsT_ap, rhs_t, rhs_ap, start, stop):
    p.op("pe", "matmul", out=out_ap, lhsT=lhsT_ap, rhs=rhs_ap, start=start, stop=stop,
         reads=[lhsT_t, rhs_t], writes=[out_t])


def compute_mod(p, cx, svec_d, modw_d, modb_d, ncomp, mod_sb):
    sv = p.sbuf("sv", [128, NCH, 2], F32)
    sil = p.sbuf("sil", [128, NCH, 2], F32)
    mb = p.sbuf("modb", [128, ncomp, NCH], F32)
    p.dma("sp", sv[:], svec_d[:, :, :], writes=[sv])
    p.dma("sp", mb[:], modb_d[:, :, :], writes=[mb])
    p.op("act", "activation", out=sil[:], in_=sv[:], func=AF.Silu, reads=[sv], writes=[sil])
    wb = [p.sbuf("modwb%d" % i, [128, NCH, 128], F32) for i in range(2)]
    k = 0
    for comp in range(ncomp):
        for oc in range(NCH):
            w = wb[k % 2]
            k += 1
            p.dma("sp", w[:], modw_d[comp, oc], writes=[w])
            ps = cx.bank()
            for c in range(NCH):
                mm(p, ps, ps[:, 0:2], w, w[:, c, :], sil, sil[:, c, :], c == 0, c == NCH - 1)
            p.op("dve", "tensor_scalar", out=mod_sb[:, comp, oc, :], in0=ps[:, 0:2], scalar1=mb[:, comp, oc:oc + 1], scalar2=None,
                op0=ALU.add, reads=[ps, mb], writes=[mod_sb])


def rms_stats(p, cx, x_t, x_ap3, W, sq_t, rstd_t):
    p.op("act", "activation", out=sq_t[:, 0:NCH, :W], in_=x_ap3, func=AF.Square, reads=[x_t], writes=[sq_t])
    ps = cx.bank()
    for c in range(NCH):
        mm(p, ps, ps[:, :W], cx.ones_bf, cx.ones_bf[:], sq_t, sq_t[:, c, :W], c == 0, c == NCH - 1)
    p.op("act", "activation", out=rstd_t[:, :W], in_=ps[:, :W], func=AF.Sqrt, scale=1.0 / D,
                                       bias=cx.epsb[:], reads=[ps, cx.epsb], writes=[rstd_t])
    p.op("dve", "reciprocal", out=rstd_t[:, :W], in_=rstd_t[:, :W], reads=[rstd_t], writes=[rstd_t])


def norm_mod(p, cx, x_t, x3, W, rstd_t, tmp_t, A_t, A_ap, B_t, B_ap, h_t, h3):
    for c in range(NCH):
        tt = tmp_t[c % 2]
        p.op("dve", "tensor_tensor", out=tt[:, :W], in0=x3(c), in1=rstd_t[:, :W], op=ALU.mult,
             reads=[x_t, rstd_t], writes=[tt])
        p.op("act", "activation", out=h3(c), in_=tt[:, :W], func=AF.Identity,
                                                       scale=A_ap(c), bias=B_ap(c),
             reads=[tt, A_t, B_t], writes=[h_t])


def prep_B_weights(p, d, q="pool"):
    wout_b = p.dram(p.uniq("wout_b"), [NCH, 128, NCH, 128], BF16)
    wgu_b = p.dram(p.uniq("wgu_b"), [NJ, 128, 2, NCH, 128], BF16)
    wdn_b = p.dram(p.uniq("wdn_b"), [NCH, 128, NJ, 128], BF16)
    for o in range(0, NCH, 4):
        p.dma(q, wout_b[o:o + 4], d["wout"][o:o + 4], reads=[d["wout"]], writes=[wout_b])
    for j in range(0, NJ, 4):
        p.dma(q, wgu_b[j:j + 4], d["wgu"][j:j + 4], reads=[d["wgu"]], writes=[wgu_b])
    for o in range(0, NCH, 2):
        p.dma(q, wdn_b[o:o + 2], d["wdn"][o:o + 2], reads=[d["wdn"]], writes=[wdn_b])
    d["wcast"] = (wout_b, wgu_b, wdn_b)


def build_B(p, cx, d, tiles, last):
    if "wcast" not in d:
        prep_B_weights(p, d)
    wout_b, wgu_b, wdn_b = d["wcast"]

    mod = p.sbuf("modB", [128, 4, NCH, 2], F32)
    compute_mod(p, cx, d["svec"], d["modw"], d["modb"], 4, mod)
    nw = p.sbuf("nwB", [128, NCH], F32)
    p.dma("sp", nw[:], d["normw"][:, :], writes=[nw])
    A2 = p.sbuf("A2", [128, NCH, 2], F32)
    for col in range(2):
        p.op("dve", "scalar_tensor_tensor", out=A2[:, :, col], in0=mod[:, 2, :, col], scalar=1.0, in1=nw[:], op0=ALU.add, op1=ALU.mult,
            reads=[mod, nw], writes=[A2])
    if last:
        fw = p.sbuf("fwB", [128, NCH], F32)
        p.dma("sp", fw[:], d["finalw"][:, :], writes=[fw])

    TW = 512
    xr_t = [p.sbuf("xresB%d" % i, [128, NCH, TW], F32) for i in range(1)]
    mx_t = [p.sbuf("mixB%d" % i, [128, NCH, TW], BF16) for i in range(1)]
    h_t = p.sbuf("hB", [128, NCH, TW], BF16)
    u_t = p.sbuf("uB", [128, NJ, TW], BF16)
    sq_t = u_t
    rstd_t = p.sbuf("rstdB", [128, TW], F32)
    tmp_t = [p.sbuf("tmpB%d" % i, [128, TW], F32) for i in range(2)]
    sg_t = [p.sbuf("sgB%d" % i, [128, TW], F32) for i in range(2)]
    wo_t = [p.sbuf("woB%d" % i, [128, NCH, 128], BF16) for i in range(2)]
    wg_t = [p.sbuf("wgB%d" % i, [128, 2, NCH, 128], BF16) for i in range(3)]
    wd_t = [p.sbuf("wdB%d" % i, [128, NJ, 128], BF16) for i in range(2)]
    ko = kg = kd = 0

    for ti, (c0, W, mc) in enumerate(tiles):
        x1 = xr_t[0]
        mx = mx_t[0]
        if "ld_x" in d:
            d["ld_x"](x1, c0, W, mc)
        else:
            p.dma("sp", x1[:, :, :W], d["xres"][:, :, c0:c0 + W], reads=[d["xres"]], writes=[x1])
        if "ld_mix" in d:
            d["ld_mix"](mx, c0, W, mc)
        else:
            p.dma("sp", mx[:, :, :W], d["mix"][:, :, c0:c0 + W], reads=[d["mix"]], writes=[mx])
        for o in range(NCH):
            w = wo_t[ko % 2]
            ko += 1
            p.dma("act", w[:], wout_b[o], reads=[wout_b], writes=[w])
            ps = cx.bank()
            for c in range(NCH):
                mm(p, ps, ps[:, :W], w, w[:, c, :], mx, mx[:, c, :W], c == 0, c == NCH - 1)
            p.op("dve", "scalar_tensor_tensor", out=x1[:, o, :W], in0=ps[:, :W], scalar=mod[:, 0, o, mc:mc + 1], in1=x1[:, o, :W],
                op0=ALU.mult, op1=ALU.add, reads=[ps, mod, x1], writes=[x1])
        rms_stats(p, cx, x1, x1[:, :, :W], W, sq_t, rstd_t)
        norm_mod(p, cx, x1, lambda c: x1[:, c, :W], W, rstd_t, tmp_t,
                 A2, lambda c: A2[:, c, mc:mc + 1], mod, lambda c: mod[:, 1, c, mc:mc + 1],
                 h_t, lambda c: h_t[:, c, :W])
        for j in range(NJ):
            w = wg_t[kg % 3]
            kg += 1
            p.dma("sp", w[:], wgu_b[j], reads=[wgu_b], writes=[w])
            pg = cx.bank()
            pu = cx.bank()
            for c in range(NCH):
                mm(p, pg, pg[:, :W], w, w[:, 0, c, :], h_t, h_t[:, c, :W], c == 0, c == NCH - 1)
            for c in range(NCH):
                mm(p, pu, pu[:, :W], w, w[:, 1, c, :], h_t, h_t[:, c, :W], c == 0, c == NCH - 1)
            sg = sg_t[j % 2]
            p.op("act", "activation", out=sg[:, :W], in_=pg[:, :W], func=AF.Silu,
                 reads=[pg], writes=[sg])
            p.op("dve", "tensor_tensor", out=u_t[:, j, :W], in0=pu[:, :W], in1=sg[:, :W],
                                                                  op=ALU.mult, reads=[pu, sg], writes=[u_t])
        for o in range(NCH):
            w = wd_t[kd % 2]
            kd += 1
            p.dma("act", w[:], wdn_b[o], reads=[wdn_b], writes=[w])
            ps = cx.bank()
            for j in range(NJ):
                mm(p, ps, ps[:, :W], w, w[:, j, :], u_t, u_t[:, j, :W], j == 0, j == NJ - 1)
            p.op("dve", "scalar_tensor_tensor", out=x1[:, o, :W], in0=ps[:, :W], scalar=mod[:, 3, o, mc:mc + 1], in1=x1[:, o, :W],
                op0=ALU.mult, op1=ALU.add, reads=[ps, mod, x1], writes=[x1])
        if last:
            rms_stats(p, cx, x1, x1[:, :, :W], W, sq_t, rstd_t)
            for c in range(NCH):
                p.op("dve", "scalar_tensor_tensor", out=x1[:, c, :W], in0=x1[:, c, :W], scalar=fw[:, c:c + 1], in1=rstd_t[:, :W],
                    op0=ALU.mult, op1=ALU.mult, reads=[x1, fw, rstd_t], writes=[x1])
        if "st_x" in d:
            d["st_x"](x1, c0, W, mc)
        else:
            p.dma("pool", d["xout"][:, :, c0:c0 + W], x1[:, :, :W], reads=[x1], writes=[d["xout"]])


def make_B(NT, tiles, last):
    nc = bass.Bass("TRN2", target_bir_lowering=False)
    p = Prog(nc)
    cx = Ctx(p)
    d = {
        "xres": p.dram("xres", [128, NCH, NT], F32, kind="ExternalInput"),
        "mix": p.dram("mix", [128, NCH, NT], BF16, kind="ExternalInput"),
        "svec": p.dram("svec", [128, NCH, 2], F32, kind="ExternalInput"),
        "modw": p.dram("modw", [4, NCH, 128, NCH, 128], F32, kind="ExternalInput"),
        "modb": p.dram("modb", [128, 4, NCH], F32, kind="ExternalInput"),
        "normw": p.dram("normw", [128, NCH], F32, kind="ExternalInput"),
        "wout": p.dram("wout", [NCH, 128, NCH, 128], F32, kind="ExternalInput"),
        "wgu": p.dram("wgu", [NJ, 128, 2, NCH, 128], F32, kind="ExternalInput"),
        "wdn": p.dram("wdn", [NCH, 128, NJ, 128], F32, kind="ExternalInput"),
        "xout": p.dram("xout", [128, NCH, NT], F32, kind="ExternalOutput"),
    }
    if last:
        d["finalw"] = p.dram("finalw", [128, NCH], F32, kind="ExternalInput")
    build_B(p, cx, d, tiles, last)
    p.finish([d["xout"]])
    return nc, p


def fm(v):
    return np.ascontiguousarray(v.reshape(-1, 128).T)


def fm_act(xT):
    F, N = xT.shape
    return np.ascontiguousarray(xT.reshape(F // 128, 128, N).transpose(1, 0, 2))


def blk_lhsT(w, ncols_blk=128):
    K, N = w.shape
    return np.ascontiguousarray(w.reshape(K // 128, 128, N // 128, 128).transpose(2, 1, 0, 3))


def host_B_inputs(l, c_b, c_ctx, mod_w, mod_b, norm_ffn_w, w_out, wg, wu, wd, final_w=None):
    comps = [2, 3, 4, 5]
    modw = np.stack([blk_lhsT(mod_w[l][:, k * D:(k + 1) * D]) for k in comps])
    modb = np.stack([fm(mod_b[l][k * D:(k + 1) * D]) for k in comps], axis=1)
    svec = np.stack([fm(c_b), fm(c_ctx)], axis=2)
    g = blk_lhsT(wg[l])
    u = blk_lhsT(wu[l])
    wgu = np.ascontiguousarray(np.stack([g, u], axis=2))
    out = {
        "svec": np.ascontiguousarray(svec), "modw": modw, "modb": np.ascontiguousarray(modb),
        "normw": fm(norm_ffn_w[l]), "wout": blk_lhsT(w_out), "wgu": wgu, "wdn": blk_lhsT(wd[l]),
    }
    if final_w is not None:
        out["finalw"] = fm(final_w)
    return out


def st_mix(p, d, ch, src_t, src_ap, c0, W, mc):
    if "st_mix" in d:
        d["st_mix"](ch, src_t, src_ap, c0, W, mc)
    else:
        p.dma("pool", d["mixo"][ch, :, c0:c0 + W], src_ap, reads=[src_t], writes=[d["mixo"]])


def load_modA(p, cx, d):
    mod = p.sbuf("modA", [128, 2, NCH, 2], F32)
    compute_mod(p, cx, d["svec"], d["modw"], d["modb"], 2, mod)
    nw = p.sbuf("nwA", [128, NCH], F32)
    p.dma("sp", nw[:], d["normw"][:, :], writes=[nw])
    A1 = p.sbuf("A1", [128, NCH, 2], F32)
    for col in range(2):
        p.op("dve", "scalar_tensor_tensor", out=A1[:, :, col], in0=mod[:, 1, :, col], scalar=1.0, in1=nw[:],
             op0=ALU.add, op1=ALU.mult, reads=[mod, nw], writes=[A1])
    return A1, mod


def gelu_tanh_evac(p, ps, W, out_t, out_ap, t1, t2):
    p.op("act", "activation", out=t1[:, :W], in_=ps[:, :W], func=AF.Square, reads=[ps], writes=[t1])
    p.op("dve", "tensor_scalar", out=t1[:, :W], in0=t1[:, :W], scalar1=0.044715, scalar2=1.0, op0=ALU.mult,
         op1=ALU.add, reads=[t1], writes=[t1])
    p.op("dve", "tensor_tensor", out=t1[:, :W], in0=ps[:, :W], in1=t1[:, :W], op=ALU.mult, reads=[ps, t1],
         writes=[t1])
    p.op("act", "activation", out=t2[:, :W], in_=t1[:, :W], func=AF.Sigmoid, scale=1.5957691216057308,
         reads=[t1], writes=[t2])
    p.op("dve", "tensor_tensor", out=out_ap, in0=ps[:, :W], in1=t2[:, :W], op=ALU.mult, reads=[ps, t2],
         writes=[out_t])


def rope_evac(p, cx, ps, W, qf, rmat, cs_t, cs_ap, sn_ap, t1, t2, out_t, out_ap):
    p.op("act", "activation", out=qf[:, :W], in_=ps[:, :W], func=AF.Copy, reads=[ps], writes=[qf])
    p2 = cx.bank()
    p.op("pe", "matmul", out=p2[:, :W], lhsT=rmat[:], rhs=qf[:, :W], start=True, stop=True, reads=[rmat, qf],
         writes=[p2])
    p.op("dve", "tensor_tensor", out=t1[:, :W], in0=qf[:, :W], in1=cs_ap, op=ALU.mult, reads=[qf, cs_t], writes=[t1])
    p.op("dve", "tensor_tensor", out=t2[:, :W], in0=p2[:, :W], in1=sn_ap, op=ALU.mult, reads=[p2, cs_t], writes=[t2])
    p.op("pool", "tensor_tensor", out=out_ap, in0=t1[:, :W], in1=t2[:, :W], op=ALU.add, reads=[t1, t2],
         writes=[out_t])


def build_A0(p, cx, d, S, CT, lam_init):
    SL = S - CT
    tiles = [(0, CT, 1)] + [(CT + i * 512, 512, 0) for i in range(SL // 512)]
    NBLK = S // 128
    gg_d = p.dram(p.uniq("gg"), [2, 128, S], F32)
    xr_d = p.dram(p.uniq("xr"), [2, 128, S], F32)
    qk_d = p.dram(p.uniq("qk"), [4, 128, S], BF16)
    v_d = p.dram(p.uniq("vt"), [128, NBLK, 256], BF16)
    A1, mod = load_modA(p, cx, d)
    rmat = p.sbuf("rmat", [128, 128], F32)
    p.dma("sp", rmat[:], d["rmat"][:, :], writes=[rmat])
    oneb = p.sbuf("oneb", [128, 1], F32)
    p.op("dve", "memset", ap=oneb[:], constant=1.0, writes=[oneb])
    lqk = p.sbuf("lqk", [128, 4, 64], F32)
    p.dma("sp", lqk[:], d["lqk"][:, :, :], writes=[lqk])
    ltmp = p.sbuf("ltmp", [128, 2, 64], F32)
    lsum = p.sbuf("lsum", [128, 2], F32)
    neg_lam = p.sbuf("neglam", [128, 1], F32)
    for i in range(2):
        p.op("dve", "tensor_tensor", out=ltmp[:, i, :], in0=lqk[:, 2 * i, :], in1=lqk[:, 2 * i + 1, :], op=ALU.mult,
             reads=[lqk], writes=[ltmp])
        p.op("dve", "tensor_reduce", out=lsum[:, i:i + 1], in_=ltmp[:, i, :], axis=AX.X, op=ALU.add, reads=[ltmp],
             writes=[lsum])
    p.op("act", "activation", out=lsum[:], in_=lsum[:], func=AF.Exp, reads=[lsum], writes=[lsum])
    p.op("dve", "tensor_tensor", out=neg_lam[:], in0=lsum[:, 1:2], in1=lsum[:, 0:1], op=ALU.subtract, reads=[lsum],
         writes=[neg_lam])
    p.op("dve", "tensor_scalar", out=neg_lam[:], in0=neg_lam[:], scalar1=-lam_init, scalar2=None, op0=ALU.add,
         reads=[neg_lam], writes=[neg_lam])
    wsub = p.sbuf("wsub", [128, 1], F32)
    p.dma("sp", wsub[:], d["subw"][:, :], writes=[wsub])
    p.op("dve", "tensor_scalar", out=wsub[:], in0=wsub[:], scalar1=1.0 - lam_init, scalar2=None, op0=ALU.mult,
         reads=[wsub], writes=[wsub])
    lba = p.sbuf("lba", [128, 2, 2], F32)
    lbx = p.sbuf("lbx", [128, 2, 2], F32)
    llam = p.sbuf("llam", [128, 2, 2], F32)
    nsl = p.sbuf("nsl", [128, 2, 2], F32)
    cw = p.sbuf("convw", [128, 2, 4], F32)
    cb = p.sbuf("convb", [128, 2], F32)
    p.dma("sp", lba[:], d["lba"][:, :, :], writes=[lba])
    p.dma("sp", lbx[:], d["lbx"][:, :, :], writes=[lbx])
    p.dma("sp", llam[:], d["llam"][:, :, :], writes=[llam])
    p.dma("sp", cw[:], d["convw"][:, :, :], writes=[cw])
    p.dma("sp", cb[:], d["convb"][:, :], writes=[cb])
    ee = p.sbuf("ee", [128, 2, 2], F32)
    pp = p.sbuf("pp", [128, 2, 2], F32)
    p.op("act", "activation", out=ee[:], in_=llam[:], func=AF.Exp, scale=-1.0, reads=[llam], writes=[ee])
    p.op("dve", "tensor_scalar", out=pp[:], in0=ee[:], scalar1=-0.25, scalar2=1.0 / 3.0, op0=ALU.mult, op1=ALU.add,
         reads=[ee], writes=[pp])
    p.op("dve", "tensor_tensor", out=pp[:], in0=pp[:], in1=ee[:], op=ALU.mult, reads=[pp, ee], writes=[pp])
    p.op("dve", "tensor_scalar", out=pp[:], in0=pp[:], scalar1=-0.5, scalar2=None, op0=ALU.add, reads=[pp], writes=[pp])
    p.op("dve", "tensor_tensor", out=pp[:], in0=pp[:], in1=ee[:], op=ALU.mult, reads=[pp, ee], writes=[pp])
    p.op("dve", "tensor_scalar", out=pp[:], in0=pp[:], scalar1=1.0, scalar2=None, op0=ALU.add, reads=[pp], writes=[pp])
    p.op("dve", "tensor_tensor", out=pp[:], in0=pp[:], in1=ee[:], op=ALU.mult, reads=[pp, ee], writes=[pp])
    p.op("dve", "tensor_scalar", out=nsl[:], in0=pp[:], scalar1=-8.0, scalar2=None, op0=ALU.mult, reads=[pp],
         writes=[nsl])
    wg = p.sbuf("lruw", [128, 2, 2, 2, 128], BF16)
    p.dma("pool", wg[:], d["lruw"][:, :, :, :, :], writes=[wg])

    p.open_scope()
    win = p.sbuf("win", [128, NCH, 1280], BF16)
    for c in range(0, NCH, 4):
        p.dma("pool", win[:, c:c + 4, :], d["win"][:, c:c + 4, :], writes=[win])
    xt = p.sbuf("xtA", [128, NCH, 512], F32)
    sq_t = p.sbuf("sqA", [128, NCH, 512], BF16)
    h_t = p.sbuf("hA", [128, NCH, 512], BF16)
    rstd_t = p.sbuf("rstdA", [128, 512], F32)
    tmp_t = [p.sbuf("tmpA%d" % i, [128, 512], F32) for i in range(2)]
    cs_t = p.sbuf("csA", [128, 2, 512], F32)
    stg_f = p.sbuf("stgf", [128, 4, 512], F32)
    stg_b = p.sbuf("stgb", [128, 4, 512], BF16)
    stg_v = p.sbuf("stgv", [128, 4, 256], BF16)
    qf = [p.sbuf("qf%d" % i, [128, 512], F32) for i in range(2)]
    t1 = [p.sbuf("t1A%d" % i, [128, 512], F32) for i in range(2)]
    t2 = [p.sbuf("t2A%d" % i, [128, 512], F32) for i in range(2)]
    for ti, (c0, W, mc) in enumerate(tiles):
        lat = (mc == 0)
        if "ld_xa" in d:
            d["ld_xa"](xt, c0, W, mc)
        else:
            p.dma("sp", xt[:, :, :W], d["xa"][:, :, c0:c0 + W], reads=[d["xa"]], writes=[xt])
        if lat:
            p.dma("sp", cs_t[:, 0, :W], d["cos"][:, c0 - CT:c0 - CT + W], writes=[cs_t])
            p.dma("sp", cs_t[:, 1, :W], d["sin"][:, c0 - CT:c0 - CT + W], writes=[cs_t])
        rms_stats(p, cx, xt, xt[:, :, :W], W, sq_t, rstd_t)
        norm_mod(p, cx, xt, lambda c: xt[:, c, :W], W, rstd_t, tmp_t,
                 A1, lambda c: A1[:, c, mc:mc + 1], mod, lambda c: mod[:, 0, c, mc:mc + 1],
                 h_t, lambda c: h_t[:, c, :W])
        for oc in range(8):
            ps = cx.bank()
            for c in range(NCH):
                mm(p, ps, ps[:, :W], win, win[:, c, oc * 128:(oc + 1) * 128], h_t, h_t[:, c, :W], c == 0, c == NCH - 1)
            if oc < 2:
                gelu_tanh_evac(p, ps, W, stg_f, stg_f[:, oc, :W], t1[oc % 2], t2[oc % 2])
            elif oc < 4:
                p.op("act", "activation", out=stg_f[:, oc, :W], in_=ps[:, :W], func=AF.Copy, reads=[ps],
                     writes=[stg_f])
            elif lat:
                rope_evac(p, cx, ps, W, qf[oc % 2], rmat, cs_t, cs_t[:, 0, :W], cs_t[:, 1, :W], t1[oc % 2],
                          t2[oc % 2], stg_b, stg_b[:, oc - 4, :W])
            else:
                p.op("act", "activation", out=stg_b[:, oc - 4, :W], in_=ps[:, :W], func=AF.Copy, reads=[ps],
                     writes=[stg_b])
        for tb in range(W // 128):
            ps = cx.bank()
            for c in range(NCH):
                mm(p, ps, ps[:, :256], h_t, h_t[:, c, tb * 128:(tb + 1) * 128], win, win[:, c, 1024:1280],
                   c == 0, c == NCH - 1)
            p.op("act", "activation", out=stg_v[:, tb, :], in_=ps[:, :256], func=AF.Copy, reads=[ps], writes=[stg_v])
        p.dma("pool", gg_d[:, :, c0:c0 + W].rearrange("a p w -> p a w"), stg_f[:, 0:2, :W], reads=[stg_f],
              writes=[gg_d])
        p.dma("pool", xr_d[:, :, c0:c0 + W].rearrange("a p w -> p a w"), stg_f[:, 2:4, :W], reads=[stg_f],
              writes=[xr_d])
        p.dma("pool", qk_d[:, :, c0:c0 + W].rearrange("a p w -> p a w"), stg_b[:, :, :W], reads=[stg_b],
              writes=[qk_d])
        p.dma("pool", v_d[:, c0 // 128:(c0 + W) // 128, :], stg_v[:, :W // 128, :], reads=[stg_v], writes=[v_d])
    p.close_scope()

    p.open_scope()
    KT = p.sbuf("KT", [128, S], BF16)
    V = p.sbuf("Vtok", [128, NBLK, 128], BF16)
    QT = [p.sbuf("QT%d" % i, [128, 512], BF16) for i in range(2)]
    E = [p.sbuf("E%d" % i, [128, 512], BF16) for i in range(4)]
    rz = [p.sbuf("rz%d" % i, [128, 512], F32) for i in range(2)]
    oo = [p.sbuf("oo%d" % i, [128, 512], F32) for i in range(2)]
    dd = p.sbuf("dd", [128, 512], F32)
    dsq = p.sbuf("dsq", [128, 512], BF16)
    drs = p.sbuf("drs", [128, 512], F32)
    dout = [p.sbuf("dout%d" % i, [128, 512], BF16) for i in range(2)]
    srot = [cx.banks[4], cx.banks[5], cx.banks[6], cx.pmisc]
    si = 0
    ei = 0
    qi = 0
    for hh in range(2):
        p.dma("sp", KT[:], qk_d[2 + hh], reads=[qk_d], writes=[KT])
        for b0 in range(0, NBLK, 32):
            b1 = min(NBLK, b0 + 32)
            p.dma("sp", V[:, b0:b1, :], v_d[:, b0:b1, hh * 128:(hh + 1) * 128], reads=[v_d], writes=[V])
        for ti, (c0, W, mc) in enumerate(tiles):
            kblocks = list(range(CT // 128)) if mc == 1 else list(range(NBLK))
            Q = QT[qi % 2]
            qi += 1
            p.dma("sp", Q[:, :W], qk_d[hh, :, c0:c0 + W], reads=[qk_d], writes=[Q])
            for m in range(2):
                O = cx.banks[2 * m]
                Z = cx.banks[2 * m + 1]
                pend = []
                for ki, kb in enumerate(kblocks):
                    sp_ = srot[si % 4]
                    si += 1
                    mm(p, sp_, sp_[:, :W], KT, KT[64 * m:64 * m + 64, kb * 128:(kb + 1) * 128], Q,
                       Q[64 * m:64 * m + 64, :W], True, True)
                    Eb = E[ei % 4]
                    ei += 1
                    p.op("act", "activation", out=Eb[:, :W], in_=sp_[:, :W], func=AF.Exp, scale=0.125, reads=[sp_],
                         writes=[Eb])
                    pend.append((kb, Eb, ki == 0, ki == len(kblocks) - 1))
                    while len(pend) > (0 if ki == len(kblocks) - 1 else LAG):
                        kb_, Eb_, first, lastk = pend.pop(0)
                        mm(p, O, O[:, :W], V, V[:, kb_, :], Eb_, Eb_[:, :W], first, lastk)
                        mm(p, Z, Z[:, :W], cx.ones_bf, cx.ones_bf[:], Eb_, Eb_[:, :W], first, lastk)
                p.op("dve", "reciprocal", out=rz[m][:, :W], in_=Z[:, :W], reads=[Z], writes=[rz[m]])
                p.op("dve", "tensor_tensor", out=oo[m][:, :W], in0=O[:, :W], in1=rz[m][:, :W], op=ALU.mult,
                     reads=[O, rz[m]], writes=[oo[m]])
            p.op("dve", "scalar_tensor_tensor", out=dd[:, :W], in0=oo[1][:, :W], scalar=neg_lam[:], in1=oo[0][:, :W],
                 op0=ALU.mult, op1=ALU.add, reads=[oo[0], oo[1], neg_lam], writes=[dd])
            p.op("act", "activation", out=dsq[:, :W], in_=dd[:, :W], func=AF.Square, reads=[dd], writes=[dsq])
            ps = cx.banks[2]
            mm(p, ps, ps[:, :W], cx.ones_bf, cx.ones_bf[:], dsq, dsq[:, :W], True, True)
            p.op("act", "activation", out=drs[:, :W], in_=ps[:, :W], func=AF.Sqrt, scale=1.0 / 128, bias=cx.epsb[:],
                 reads=[ps, cx.epsb], writes=[drs])
            p.op("dve", "reciprocal", out=drs[:, :W], in_=drs[:, :W], reads=[drs], writes=[drs])
            do = dout[ti % 2]
            p.op("dve", "scalar_tensor_tensor", out=do[:, :W], in0=dd[:, :W], scalar=wsub[:], in1=drs[:, :W],
                 op0=ALU.mult, op1=ALU.mult, reads=[dd, wsub, drs], writes=[do])
            st_mix(p, d, 2 + hh, do, do[:, :W], c0, W, mc)
    p.close_scope()

    for bi in range(2):
        p.open_scope()
        xc = p.sbuf("xc", [128, S], F32)
        xcb = p.sbuf("xcb", [128, S], BF16)
        p.open_scope()
        xp = p.sbuf("xp", [128, S + 6], F32)
        p.op("pool", "memset", ap=xp[:], constant=0.0, writes=[xp])
        p.dma("sp", xp[:, 1:1 + CT], xr_d[bi, :, 0:CT], reads=[xr_d], writes=[xp])
        p.dma("sp", xp[:, CT + 4:CT + 4 + SL], xr_d[bi, :, CT:S], reads=[xr_d], writes=[xp])
        for (o0, po, n) in ((0, 1, CT), (CT, CT + 4, SL)):
            for n0 in range(0, n, 4096):
                n1 = min(n, n0 + 4096)
                L = n1 - n0
                p.op("dve", "tensor_scalar", out=xc[:, o0 + n0:o0 + n1], in0=xp[:, po + n0 - 1:po + n0 - 1 + L],
                     scalar1=cw[:, bi, 0:1], scalar2=cb[:, bi:bi + 1], op0=ALU.mult, op1=ALU.add,
                     reads=[xp, cw, cb], writes=[xc])
                for k in range(1, 4):
                    p.op("dve", "scalar_tensor_tensor", out=xc[:, o0 + n0:o0 + n1],
                         in0=xp[:, po + n0 - 1 + k:po + n0 - 1 + k + L], scalar=cw[:, bi, k:k + 1],
                         in1=xc[:, o0 + n0:o0 + n1], op0=ALU.mult, op1=ALU.add, reads=[xp, cw, xc], writes=[xc])
        p.close_scope()
        for n0 in range(0, S, 4096):
            n1 = min(S, n0 + 4096)
            p.op("act", "activation", out=xcb[:, n0:n1], in_=xc[:, n0:n1], func=AF.Copy, reads=[xc], writes=[xcb])
        p.open_scope()
        racc = p.sbuf("racc", [128, S], F32)
        gr = p.sbuf("gr", [128, 512], F32)
        ga = p.sbuf("ga", [128, 512], F32)
        gi = p.sbuf("gi", [128, 512], F32)
        gs = p.sbuf("gs", [128, 512], F32)
        gb = p.sbuf("gb", [128, 512], F32)
        hb = [p.sbuf("hb%d" % i, [128, 512], F32) for i in range(2)]
        ggt = p.sbuf("ggt", [128, 512], F32)
        hsum = p.sbuf("hsum", [128, 512], F32)
        ao = [p.sbuf("ao%d" % i, [128, 512], BF16) for i in range(2)]
        for dr in range(2):
            order = list(range(len(tiles)))
            if dr == 1:
                order = [0] + order[:0:-1]
            prev = None
            for n, ti in enumerate(order):
                c0, W, mc = tiles[ti]
                pa = cx.bank()
                px = cx.bank()
                mm(p, pa, pa[:, :W], wg, wg[:, 0, dr, bi, :], xcb, xcb[:, c0:c0 + W], True, True)
                mm(p, px, px[:, :W], wg, wg[:, 1, dr, bi, :], xcb, xcb[:, c0:c0 + W], True, True)
                p.op("act", "activation", out=gr[:, :W], in_=pa[:, :W], func=AF.Sigmoid, bias=lba[:, dr, bi:bi + 1],
                     reads=[pa, lba], writes=[gr])
                p.op("act", "activation", out=gi[:, :W], in_=px[:, :W], func=AF.Sigmoid, bias=lbx[:, dr, bi:bi + 1],
                     reads=[px, lbx], writes=[gi])
                p.op("act", "activation", out=ga[:, :W], in_=gr[:, :W], func=AF.Exp, scale=nsl[:, dr, bi:bi + 1],
                     reads=[gr, nsl], writes=[ga])
                p.op("dve", "tensor_scalar", out=ga[:, :W], in0=ga[:, :W], scalar1=1.0, scalar2=None, op0=ALU.min,
                     reads=[ga], writes=[ga])
                p.op("dve", "tensor_tensor", out=gs[:, :W], in0=ga[:, :W], in1=ga[:, :W], op=ALU.mult, reads=[ga],
                     writes=[gs])
                p.op("act", "activation", out=gs[:, :W], in_=gs[:, :W], func=AF.Sqrt, scale=-1.0, bias=oneb[:],
                     reads=[gs, oneb], writes=[gs])
                p.op("dve", "tensor_tensor", out=gb[:, :W], in0=gs[:, :W], in1=gi[:, :W], op=ALU.mult, reads=[gs, gi],
                     writes=[gb])
                p.op("dve", "tensor_tensor", out=gb[:, :W], in0=gb[:, :W], in1=xc[:, c0:c0 + W], op=ALU.mult,
                     reads=[gb, xc], writes=[gb])
                if dr == 0:
                    init = 0.0 if n == 0 else racc[:, c0 - 1:c0]
                    p.op("dve", "tensor_tensor_scan", out=racc[:, c0:c0 + W], data0=ga[:, :W], data1=gb[:, :W],
                         initial=init, op0=ALU.mult, op1=ALU.add, reads=[ga, gb, racc], writes=[racc])
                else:
                    hcur = hb[n % 2]
                    if n == 0:
                        init = 0.0
                        rd = [ga, gb]
                    else:
                        init = prev[0][:, 0:1]
                        rd = [ga, gb, prev[0]]
                    p.op("dve", "tensor_tensor_scan", out=hcur[:, 0:W][:, ::-1],
                         data0=ga[:, 0:W][:, ::-1], data1=gb[:, 0:W][:, ::-1], initial=init, op0=ALU.mult,
                         op1=ALU.add, reads=rd, writes=[hcur])
                    prev = (hcur, W)
                    p.dma("sp", ggt[:, :W], gg_d[bi, :, c0:c0 + W], reads=[gg_d], writes=[ggt])
                    a_o = ao[n % 2]
                    p.op("pool", "tensor_tensor", out=hsum[:, :W], in0=hcur[:, :W], in1=racc[:, c0:c0 + W], op=ALU.add,
                         reads=[hcur, racc], writes=[hsum])
                    p.op("pool", "tensor_tensor", out=a_o[:, :W], in0=hsum[:, :W], in1=ggt[:, :W], op=ALU.mult,
                         reads=[hsum, ggt], writes=[a_o])
                    st_mix(p, d, bi, a_o, a_o[:, :W], c0, W, mc)
        p.close_scope()
        p.close_scope()


def a0_dram(p, S, SL, pre="", fused=False):
    dd = {
        "xa": p.dram(pre + "xa", [128, NCH, S], F32, kind="ExternalInput"),
        "svec": p.dram(pre + "svec", [128, NCH, 2], F32, kind="ExternalInput"),
        "modw": p.dram(pre + "modw", [2, NCH, 128, NCH, 128], F32, kind="ExternalInput"),
        "modb": p.dram(pre + "modb", [128, 2, NCH], F32, kind="ExternalInput"),
        "normw": p.dram(pre + "normw", [128, NCH], F32, kind="ExternalInput"),
        "win": p.dram(pre + "win", [128, NCH, 1280], F32, kind="ExternalInput"),
        "rmat": p.dram(pre + "rmat", [128, 128], F32, kind="ExternalInput"),
        "lqk": p.dram(pre + "lqk", [128, 4, 64], F32, kind="ExternalInput"),
        "subw": p.dram(pre + "subw", [128, 1], F32, kind="ExternalInput"),
        "lba": p.dram(pre + "lba", [128, 2, 2], F32, kind="ExternalInput"),
        "lbx": p.dram(pre + "lbx", [128, 2, 2], F32, kind="ExternalInput"),
        "llam": p.dram(pre + "llam", [128, 2, 2], F32, kind="ExternalInput"),
        "convw": p.dram(pre + "convw", [128, 2, 4], F32, kind="ExternalInput"),
        "convb": p.dram(pre + "convb", [128, 2], F32, kind="ExternalInput"),
        "lruw": p.dram(pre + "lruw", [128, 2, 2, 2, 128], F32, kind="ExternalInput"),
        "cos": p.dram(pre + "cos", [128, SL], F32, kind="ExternalInput"),
        "sin": p.dram(pre + "sin", [128, SL], F32, kind="ExternalInput"),
    }
    dd["mixo"] = p.dram(pre + "mixo", [4, 128, S], BF16, kind="Internal" if fused else "ExternalOutput")
    return dd


def make_A0(S, CT):
    nc = bass.Bass("TRN2", target_bir_lowering=False)
    p = Prog(nc)
    cx = Ctx(p)
    d = a0_dram(p, S, S - CT)
    build_A0(p, cx, d, S, CT, 0.8 - 0.6 * math.exp(0.0))
    p.finish([d["mixo"]])
    return nc, p


def rope_tables(SL, head_dim, grid_w=64, theta=10000.0):
    rows = SL // grid_w
    row = np.repeat(np.arange(rows, dtype=np.float32), grid_w)
    col = np.tile(np.arange(grid_w, dtype=np.float32), rows)
    n_freq = head_dim // 4
    inv = (np.float32(theta) ** (-np.arange(n_freq, dtype=np.float32) / np.float32(n_freq))).astype(np.float32)
    ang = np.concatenate([row[:, None] * inv, col[:, None] * inv], axis=-1).astype(np.float32)
    cos = np.cos(ang).astype(np.float32).T
    sin = np.sin(ang).astype(np.float32).T
    reps = 128 // (head_dim // 2)
    return np.ascontiguousarray(np.tile(cos, (reps, 1))), np.ascontiguousarray(np.tile(sin, (reps, 1)))


def rot_matrix(head_dim):
    R = np.zeros((128, 128), np.float32)
    hh = head_dim // 2
    for m in range(128):
        if (m % head_dim) < hh:
            R[m + hh, m] = -1.0
        else:
            R[m - hh, m] = 1.0
    return R


def host_A0_inputs(j, c_b, c_ctx, mod_w, mod_b, norm_mix_w, ab_w_in, conv_w, conv_b, wa, ba, wx, bx, lam,
                   lq1, lk1, lq2, lk2, subw, SL):
    l = 0
    comps = [0, 1]
    modw = np.stack([blk_lhsT(mod_w[l][:, k * D:(k + 1) * D]) for k in comps])
    modb = np.stack([fm(mod_b[l][k * D:(k + 1) * D]) for k in comps], axis=1)
    svec = np.stack([fm(c_b), fm(c_ctx)], axis=2)
    W = ab_w_in[0]
    cols = np.concatenate([np.arange(256 * j, 256 * j + 256), 1024 + np.arange(256 * j, 256 * j + 256),
                           2048 + np.arange(256 * j, 256 * j + 256), 3072 + np.arange(256 * j, 256 * j + 256),
                           4096 + np.arange(256 * j, 256 * j + 256)])
    win = np.ascontiguousarray(W[:, cols].reshape(NCH, 128, 1280).transpose(1, 0, 2))
    blks = [2 * j, 2 * j + 1]

    def pvec(v):
        return np.ascontiguousarray(np.stack([v[:, b * 128:(b + 1) * 128] for b in blks], axis=2).transpose(1, 0, 2))
    lruw = np.stack([np.stack([np.stack([w[0][dr, b] for b in blks], 0) for dr in range(2)], 0) for w in (wa, wx)], 0)
    lruw = np.ascontiguousarray(lruw.transpose(3, 0, 1, 2, 4))
    cos, sin = rope_tables(SL, 64)
    return {
        "svec": np.ascontiguousarray(svec), "modw": modw, "modb": np.ascontiguousarray(modb),
        "normw": fm(norm_mix_w[l]), "win": win, "rmat": rot_matrix(64),
        "lqk": np.ascontiguousarray(np.broadcast_to(np.stack([lq1[0], lk1[0], lq2[0], lk2[0]])[None], (128, 4, 64))),
        "subw": np.ascontiguousarray(subw[0].reshape(128, 1)),
        "lba": pvec(ba[0]), "lbx": pvec(bx[0]), "llam": pvec(lam[0]),
        "convw": np.ascontiguousarray(np.stack([conv_w[0][:, b * 128:(b + 1) * 128] for b in blks], 0).transpose(2, 0, 1)),
        "convb": np.ascontiguousarray(np.stack([conv_b[0][b * 128:(b + 1) * 128] for b in blks], 1)),
        "lruw": lruw, "cos": cos, "sin": sin,
    }


def build_A1(p, cx, d, S, CT, stage=9):
    SL = S - CT
    tiles = [(0, CT, 1)] + [(CT + i * 512, 512, 0) for i in range(SL // 512)]
    NBLK = S // 128
    NF = 1408
    qh_d = p.dram(p.uniq("qh"), [2, 128, S], F32)
    f_d = p.dram(p.uniq("fd"), [2, 2, 128, S], F32)
    sg_d = p.dram(p.uniq("sg"), [2, 128, S], F32)
    qk_d = p.dram(p.uniq("qk1"), [3, 128, S], BF16)
    vt_d = p.dram(p.uniq("vt1"), [128, NBLK, 384], BF16)
    o_d = p.dram(p.uniq("od"), [2, 128, S], F32)

    A1, mod = load_modA(p, cx, d)
    rmat = p.sbuf("rmat", [128, 128], F32)
    p.dma("sp", rmat[:], d["rmat"][:, :], writes=[rmat])
    ident = p.sbuf("ident", [128, 128], BF16)
    p.dma("pool", ident[:], d["ident"][:, :], writes=[ident])
    nws = p.sbuf("nws", [128, 3], F32)
    p.dma("sp", nws[:], d["nws"][:, :], writes=[nws])
    lbl = p.sbuf("lbl", [128, 2, 2, 2], F32)
    p.dma("sp", lbl[:], d["lbl"][:, :, :, :], writes=[lbl])
    lb = p.sbuf("lb", [128, 2, 2], F32)
    oml = p.sbuf("oml", [128, 2, 2], F32)
    p.op("dve", "tensor_tensor", out=lb[:], in0=lbl[:, :, 1, :], in1=lbl[:, :, 0, :], op=ALU.subtract, reads=[lbl],
         writes=[lb])
    p.op("act", "activation", out=lb[:], in_=lb[:], func=AF.Sigmoid, reads=[lb], writes=[lb])
    p.op("dve", "tensor_scalar", out=oml[:], in0=lb[:], scalar1=-1.0, scalar2=1.0, op0=ALU.mult, op1=ALU.add,
         reads=[lb], writes=[oml])
    maskF = p.sbuf("maskF", [128, 512], F32)
    maskB = p.sbuf("maskB", [128, 512], F32)
    p.op("pool", "memset", ap=maskF[:], constant=1.0, writes=[maskF])
    p.op("pool", "memset", ap=maskB[:], constant=1.0, writes=[maskB])
    for ci in range(8):
        p.op("pool", "memset", ap=maskF[:, ci * 64:ci * 64 + 1], constant=0.0, writes=[maskF])
        p.op("pool", "memset", ap=maskB[:, ci * 64 + 63:ci * 64 + 64], constant=0.0, writes=[maskB])
    tri = p.sbuf("tri", [64, 2, 64], F32)
    p.dma("sp", tri[:], d["tri"][:, :, :], writes=[tri])

    p.open_scope()
    win = p.sbuf("win1", [128, NCH, 1792], BF16)
    for c in range(0, NCH, 4):
        p.dma("pool", win[:, c:c + 4, :], d["win"][:, c:c + 4, :], writes=[win])
    xt = p.sbuf("xtA", [128, NCH, 512], F32)
    sq_t = p.sbuf("sqA", [128, NCH, 512], BF16)
    h_t = p.sbuf("hA", [128, NCH, 512], BF16)
    rstd_t = p.sbuf("rstdA", [128, 512], F32)
    tmp_t = [p.sbuf("tmpA%d" % i, [128, 512], F32) for i in range(2)]
    cs_t = p.sbuf("csA", [128, 2, 512], F32)
    stg_q = p.sbuf("stgq", [128, 2, 512], F32)
    stg_f = p.sbuf("stgf1", [128, 4, 512], F32)
    stg_g = p.sbuf("stgg", [128, 2, 512], F32)
    stg_b = p.sbuf("stgb1", [128, 3, 512], BF16)
    stg_v = p.sbuf("stgv1", [128, 4, 384], BF16)
    qf = [p.sbuf("qf%d" % i, [128, 512], F32) for i in range(2)]
    qn = [p.sbuf("qn%d" % i, [128, 512], F32) for i in range(2)]
    qsq = [p.sbuf("qsq%d" % i, [128, 512], BF16) for i in range(2)]
    qrs = [p.sbuf("qrs%d" % i, [128, 512], F32) for i in range(2)]
    t1 = [p.sbuf("t1A%d" % i, [128, 512], F32) for i in range(2)]
    t2 = [p.sbuf("t2A%d" % i, [128, 512], F32) for i in range(2)]
    for ti, (c0, W, mc) in enumerate(tiles):
        lat = (mc == 0)
        if "ld_xa" in d:
            d["ld_xa"](xt, c0, W, mc)
        else:
            p.dma("sp", xt[:, :, :W], d["xa"][:, :, c0:c0 + W], reads=[d["xa"]], writes=[xt])
        if lat:
            p.dma("sp", cs_t[:, 0, :W], d["cos"][:, c0 - CT:c0 - CT + W], writes=[cs_t])
            p.dma("sp", cs_t[:, 1, :W], d["sin"][:, c0 - CT:c0 - CT + W], writes=[cs_t])
        rms_stats(p, cx, xt, xt[:, :, :W], W, sq_t, rstd_t)
        norm_mod(p, cx, xt, lambda c: xt[:, c, :W], W, rstd_t, tmp_t,
                 A1, lambda c: A1[:, c, mc:mc + 1], mod, lambda c: mod[:, 0, c, mc:mc + 1],
                 h_t, lambda c: h_t[:, c, :W])
        for oc in range(11):
            if oc in (8, 9) and not lat:
                continue
            ps = cx.bank()
            for c in range(NCH):
                mm(p, ps, ps[:, :W], win, win[:, c, oc * 128:(oc + 1) * 128], h_t, h_t[:, c, :W], c == 0, c == NCH - 1)
            if oc < 2:
                p.op("act", "activation", out=stg_q[:, oc, :W], in_=ps[:, :W], func=AF.Silu, reads=[ps], writes=[stg_q])
            elif oc < 6:
                dr, hh = (oc - 2) // 2, (oc - 2) % 2
                tt = t1[oc % 2]
                p.op("act", "activation", out=tt[:, :W], in_=ps[:, :W], func=AF.Sigmoid, reads=[ps], writes=[tt])
                p.op("dve", "tensor_scalar", out=stg_f[:, oc - 2, :W], in0=tt[:, :W], scalar1=oml[:, dr, hh:hh + 1],
                     scalar2=lb[:, dr, hh:hh + 1], op0=ALU.mult, op1=ALU.add, reads=[tt, oml, lb], writes=[stg_f])
            elif oc < 8:
                p.op("act", "activation", out=stg_g[:, oc - 6, :W], in_=ps[:, :W], func=AF.Silu, reads=[ps],
                     writes=[stg_g])
            else:
                k = oc % 2
                wcol = 1 if oc < 10 else 2
                p.op("act", "activation", out=qf[k][:, :W], in_=ps[:, :W], func=AF.Copy, reads=[ps], writes=[qf[k]])
                p.op("act", "activation", out=qsq[k][:, :W], in_=ps[:, :W], func=AF.Square, reads=[ps], writes=[qsq[k]])
                p2 = cx.bank()
                mm(p, p2, p2[:, :W], cx.ones_bf, cx.ones_bf[:], qsq[k], qsq[k][:, :W], True, True)
                p.op("act", "activation", out=qrs[k][:, :W], in_=p2[:, :W], func=AF.Sqrt, scale=1.0 / 128,
                     bias=cx.epsb[:], reads=[p2, cx.epsb], writes=[qrs[k]])
                p.op("dve", "reciprocal", out=qrs[k][:, :W], in_=qrs[k][:, :W], reads=[qrs[k]], writes=[qrs[k]])
                if lat:
                    p.op("dve", "scalar_tensor_tensor", out=qn[k][:, :W], in0=qf[k][:, :W], scalar=nws[:, wcol:wcol + 1],
                         in1=qrs[k][:, :W], op0=ALU.mult, op1=ALU.mult, reads=[qf[k], nws, qrs[k]], writes=[qn[k]])
                    p3 = cx.bank()
                    p.op("pe", "matmul", out=p3[:, :W], lhsT=rmat[:], rhs=qn[k][:, :W], start=True, stop=True,
                         reads=[rmat, qn[k]], writes=[p3])
                    p.op("dve", "tensor_tensor", out=t1[k][:, :W], in0=qn[k][:, :W], in1=cs_t[:, 0, :W], op=ALU.mult,
                         reads=[qn[k], cs_t], writes=[t1[k]])
                    p.op("dve", "tensor_tensor", out=t2[k][:, :W], in0=p3[:, :W], in1=cs_t[:, 1, :W], op=ALU.mult,
                         reads=[p3, cs_t], writes=[t2[k]])
                    p.op("pool", "tensor_tensor", out=stg_b[:, oc - 8, :W], in0=t1[k][:, :W], in1=t2[k][:, :W],
                         op=ALU.add, reads=[t1[k], t2[k]], writes=[stg_b])
                else:
                    p.op("dve", "scalar_tensor_tensor", out=stg_b[:, oc - 8, :W], in0=qf[k][:, :W],
                         scalar=nws[:, wcol:wcol + 1], in1=qrs[k][:, :W], op0=ALU.mult, op1=ALU.mult,
                         reads=[qf[k], nws, qrs[k]], writes=[stg_b])
        for tb in range(W // 128):
            ps = cx.bank()
            for c in range(NCH):
                mm(p, ps, ps[:, :384], h_t, h_t[:, c, tb * 128:(tb + 1) * 128], win, win[:, c, NF:NF + 384],
                   c == 0, c == NCH - 1)
            p.op("act", "activation", out=stg_v[:, tb, :], in_=ps[:, :384], func=AF.Copy, reads=[ps], writes=[stg_v])
        p.dma("pool", qh_d[:, :, c0:c0 + W].rearrange("a p w -> p a w"), stg_q[:, :, :W], reads=[stg_q], writes=[qh_d])
        p.dma("pool", f_d[:, :, :, c0:c0 + W].rearrange("a b p w -> p (a b) w"), stg_f[:, :, :W], reads=[stg_f],
              writes=[f_d])
        p.dma("pool", sg_d[:, :, c0:c0 + W].rearrange("a p w -> p a w"), stg_g[:, :, :W], reads=[stg_g], writes=[sg_d])
        if lat:
            p.dma("pool", qk_d[:, :, c0:c0 + W].rearrange("a p w -> p a w"), stg_b[:, :, :W], reads=[stg_b],
                  writes=[qk_d])
        else:
            p.dma("pool", qk_d[2, :, c0:c0 + W], stg_b[:, 2, :W], reads=[stg_b], writes=[qk_d])
        p.dma("pool", vt_d[:, c0 // 128:(c0 + W) // 128, :], stg_v[:, :W // 128, :], reads=[stg_v], writes=[vt_d])
    p.close_scope()

    if stage < 2:
        return
    p.open_scope()
    KT = p.sbuf("KT1", [128, S], BF16)
    V = p.sbuf("Vtok1", [128, NBLK, 128], BF16)
    QT = [p.sbuf("QT1%d" % i, [128, 512], BF16) for i in range(2)]
    E = [p.sbuf("E1%d" % i, [128, 512], BF16) for i in range(4)]
    rz = p.sbuf("rz1", [128, 512], F32)
    ao = [p.sbuf("ao1%d" % i, [128, 512], BF16) for i in range(2)]
    srot = [cx.banks[4], cx.banks[5], cx.banks[6], cx.pmisc]
    si = ei = qi = 0
    p.dma("sp", KT[:], qk_d[2], reads=[qk_d], writes=[KT])
    for b0 in range(0, NBLK, 32):
        b1 = min(NBLK, b0 + 32)
        p.dma("sp", V[:, b0:b1, :], vt_d[:, b0:b1, 256:384], reads=[vt_d], writes=[V])
    sc = 128 ** -0.5
    for hh in range(2):
        for ti, (c0, W, mc) in enumerate(tiles):
            if mc == 1:
                continue
            Q = QT[qi % 2]
            O = cx.banks[2 * (qi % 2)]
            Z = cx.banks[2 * (qi % 2) + 1]
            qi += 1
            p.dma("sp", Q[:, :W], qk_d[hh, :, c0:c0 + W], reads=[qk_d], writes=[Q])
            pend = []
            for kb in range(NBLK):
                sp_ = srot[si % 4]
                si += 1
                mm(p, sp_, sp_[:, :W], KT, KT[:, kb * 128:(kb + 1) * 128], Q, Q[:, :W], True, True)
                Eb = E[ei % 4]
                ei += 1
                p.op("act", "activation", out=Eb[:, :W], in_=sp_[:, :W], func=AF.Exp, scale=sc, reads=[sp_],
                     writes=[Eb])
                pend.append((kb, Eb))
                while len(pend) > (0 if kb == NBLK - 1 else LAG):
                    kb_, Eb_ = pend.pop(0)
                    mm(p, O, O[:, :W], V, V[:, kb_, :], Eb_, Eb_[:, :W], kb_ == 0, kb_ == NBLK - 1)
                    mm(p, Z, Z[:, :W], cx.ones_bf, cx.ones_bf[:], Eb_, Eb_[:, :W], kb_ == 0, kb_ == NBLK - 1)
            p.op("dve", "reciprocal", out=rz[:, :W], in_=Z[:, :W], reads=[Z], writes=[rz])
            a_o = ao[qi % 2]
            p.op("dve", "tensor_tensor", out=a_o[:, :W], in0=O[:, :W], in1=rz[:, :W], op=ALU.mult, reads=[O, rz],
                 writes=[a_o])
            st_mix(p, d, 2 + hh, a_o, a_o[:, :W], c0, W, mc)
    p.close_scope()

    if stage < 3:
        return
    p.open_scope()
    ft = p.sbuf("ft", [128, 512], F32)
    qt = p.sbuf("qt", [128, 512], F32)
    lf = p.sbuf("lf", [128, 512], F32)
    lfm = p.sbuf("lfm", [128, 514], F32)
    kk = p.sbuf("kk", [128, 512], F32)
    bb = p.sbuf("bb", [128, 512], F32)
    cc = p.sbuf("cc", [128, 512], F32)
    eb = p.sbuf("eb", [128, 512], F32)
    enb = p.sbuf("enb", [128, 512], F32)
    dd_ = p.sbuf("ddh", [128, 512], F32)
    ed = p.sbuf("ed", [128, 512], F32)
    ec = p.sbuf("ec", [128, 512], F32)
    qe = p.sbuf("qe", [128, 512], BF16)
    ke = p.sbuf("ke", [128, 512], BF16)
    kd = p.sbuf("kd", [128, 512], BF16)
    kdT = p.sbuf("kdT", [64, 8, 128], BF16)
    Vc = p.sbuf("Vc", [64, 8, 128], BF16)
    scm = [[p.sbuf("scm%d_%d" % (dr_, i), [64, 64], BF16) for i in range(2)] for dr_ in range(2)]
    for dr_ in range(2):
        for i in range(2):
            p.op("pool", "memset", ap=scm[dr_][i][:], constant=0.0, writes=[scm[dr_][i]])
    S32 = p.sbuf("S32", [128, 128], F32)
    Sbf = p.sbuf("Sbf", [128, 128], BF16)
    ot = p.sbuf("ot", [128, 512], F32)
    of = p.sbuf("of", [128, 512], F32)
    osq = p.sbuf("osq", [128, 512], BF16)
    ors = p.sbuf("ors", [128, 512], F32)
    sgt = p.sbuf("sgt", [128, 512], F32)
    oy = p.sbuf("oy", [128, 512], F32)
    ob = [p.sbuf("ob%d" % i, [128, 512], BF16) for i in range(2)]
    p.op("pool", "memset", ap=lfm[:], constant=0.0, writes=[lfm])
    for hh in range(2):
        for dr in range(2):
            order = list(range(len(tiles)))
            if dr == 1:
                order = [0] + order[:0:-1]
            p.op("dve", "memset", ap=S32[:], constant=0.0, writes=[S32])
            p.op("dve", "memset", ap=Sbf[:], constant=0.0, writes=[Sbf])
            mF, mB = (maskF, maskB) if dr == 0 else (maskB, maskF)
            for n, ti in enumerate(order):
                c0, W, mc = tiles[ti]
                nch = W // 64
                p.dma("sp", ft[:, :W], f_d[dr, hh, :, c0:c0 + W], reads=[f_d], writes=[ft])
                p.dma("sp", qt[:, :W], qh_d[hh, :, c0:c0 + W], reads=[qh_d], writes=[qt])
                for half in range(2):
                    p.dma("sp", Vc[:, half:nch:2, :], vt_d[half * 64:(half + 1) * 64, c0 // 128:(c0 + W) // 128,
                                                           hh * 128:(hh + 1) * 128], reads=[vt_d], writes=[Vc])
                p.op("act", "activation", out=lf[:, :W], in_=ft[:, :W], func=AF.Ln, reads=[ft], writes=[lf])
                p.op("dve", "tensor_scalar", out=kk[:, :W], in0=ft[:, :W], scalar1=-1.0, scalar2=1.0, op0=ALU.mult,
                     op1=ALU.add, reads=[ft], writes=[kk])
                p.op("dve", "tensor_tensor", out=lfm[:, 1:W + 1], in0=lf[:, :W], in1=mF[:, :W], op=ALU.mult,
                     reads=[lf, mF], writes=[lfm])
                if dr == 0:
                    p.op("dve", "tensor_tensor_scan", out=bb[:, 0:W], data0=mF[:, 0:W], data1=lf[:, 0:W], initial=0.0,
                         op0=ALU.mult, op1=ALU.add, reads=[mF, lf], writes=[bb])
                    p.op("dve", "tensor_tensor_scan", out=cc[:, 0:W][:, ::-1], data0=mB[:, 0:W][:, ::-1],
                         data1=lfm[:, 2:W + 2][:, ::-1], initial=0.0, op0=ALU.mult, op1=ALU.add, reads=[mB, lfm],
                         writes=[cc])
                else:
                    p.op("dve", "tensor_tensor_scan", out=bb[:, 0:W][:, ::-1], data0=mF[:, 0:W][:, ::-1],
                         data1=lf[:, 0:W][:, ::-1], initial=0.0, op0=ALU.mult, op1=ALU.add, reads=[mF, lf], writes=[bb])
                    p.op("dve", "tensor_tensor_scan", out=cc[:, 0:W], data0=mB[:, 0:W], data1=lfm[:, 0:W], initial=0.0,
                         op0=ALU.mult, op1=ALU.add, reads=[mB, lfm], writes=[cc])
                p.op("act", "activation", out=eb[:, :W], in_=bb[:, :W], func=AF.Exp, reads=[bb], writes=[eb])
                for ci in range(nch):
                    apos = ci * 64 + (31 if dr == 0 else 32)
                    p.op("dve", "tensor_scalar", out=dd_[:, ci * 64:ci * 64 + 64], in0=bb[:, ci * 64:ci * 64 + 64],
                         scalar1=bb[:, apos:apos + 1], scalar2=None, op0=ALU.subtract, reads=[bb], writes=[dd_])
                p.op("act", "activation", out=ed[:, :W], in_=dd_[:, :W], func=AF.Exp, reads=[dd_], writes=[ed])
                p.op("act", "activation", out=enb[:, :W], in_=dd_[:, :W], func=AF.Exp, scale=-1.0, reads=[dd_],
                     writes=[enb])
                p.op("act", "activation", out=ec[:, :W], in_=cc[:, :W], func=AF.Exp, reads=[cc], writes=[ec])
                p.op("dve", "tensor_tensor", out=qe[:, :W], in0=qt[:, :W], in1=ed[:, :W], op=ALU.mult, reads=[qt, ed],
                     writes=[qe])
                p.op("pool", "tensor_tensor", out=ke[:, :W], in0=kk[:, :W], in1=enb[:, :W], op=ALU.mult, reads=[kk, enb],
                     writes=[ke])
                p.op("pool", "tensor_tensor", out=kd[:, :W], in0=kk[:, :W], in1=ec[:, :W], op=ALU.mult, reads=[kk, ec],
                     writes=[kd])
                for ci in range(nch):
                    pt = cx.bank()
                    mm(p, pt, pt[:64, :128], kd, kd[:, ci * 64:(ci + 1) * 64], ident, ident[:], True, True)
                    p.op("act", "activation", out=kdT[:, ci, :], in_=pt[:64, :128], func=AF.Copy, reads=[pt],
                         writes=[kdT])
                corder = list(range(nch)) if dr == 0 else list(range(nch - 1, -1, -1))
                if stage < 4:
                    continue
                for k_, ci in enumerate(corder):
                    cs = slice(ci * 64, ci * 64 + 64)
                    lastpos = ci * 64 + (63 if dr == 0 else 0)
                    f0 = 0 if dr == 0 else 32
                    s0 = 32 - f0
                    apos = ci * 64 + (31 if dr == 0 else 32)
                    ps1 = cx.bank()
                    mm(p, ps1, ps1[:64, s0:s0 + 32], ke, ke[:, cs], qe, qe[:, ci * 64 + s0:ci * 64 + s0 + 32], True, True)
                    pf = cx.bank()
                    mm(p, pf, pf[f0:f0 + 32, f0:f0 + 32], ke, ke[:, ci * 64 + f0:ci * 64 + f0 + 32], qe,
                       qe[:, ci * 64 + f0:ci * 64 + f0 + 32], True, True)
                    sm = scm[dr][k_ % 2]
                    p.op("dve", "tensor_tensor", out=sm[:, s0:s0 + 32], in0=ps1[:64, s0:s0 + 32], in1=tri[:, dr, s0:s0 + 32],
                         op=ALU.mult, reads=[ps1, tri], writes=[sm])
                    p.op("dve", "tensor_tensor", out=sm[f0:f0 + 32, f0:f0 + 32], in0=pf[f0:f0 + 32, f0:f0 + 32],
                         in1=tri[f0:f0 + 32, dr, f0:f0 + 32], op=ALU.mult, reads=[pf, tri], writes=[sm])
                    p.op("pool", "tensor_scalar", out=Sbf[:], in0=S32[:], scalar1=eb[:, apos:apos + 1], scalar2=None,
                         op0=ALU.mult, reads=[S32, eb], writes=[Sbf])
                    ps2 = cx.bank()
                    mm(p, ps2, ps2[:, :64], Sbf, Sbf[:], qe, qe[:, cs], True, False)
                    mm(p, ps2, ps2[:, :64], Vc, Vc[:, ci, :], sm, sm[:, :], False, True)
                    p.op("act", "activation", out=ot[:, cs], in_=ps2[:, :64], func=AF.Copy, reads=[ps2], writes=[ot])
                    ps3 = cx.bank()
                    mm(p, ps3, ps3[:, :128], kdT, kdT[:, ci, :], Vc, Vc[:, ci, :], True, True)
                    p.op("dve", "scalar_tensor_tensor", out=S32[:], in0=S32[:], scalar=eb[:, lastpos:lastpos + 1],
                         in1=ps3[:, :128], op0=ALU.mult, op1=ALU.add, reads=[S32, eb, ps3], writes=[S32])
                if dr == 0:
                    p.dma("pool", o_d[hh, :, c0:c0 + W], ot[:, :W], reads=[ot], writes=[o_d])
                else:
                    p.dma("sp", of[:, :W], o_d[hh, :, c0:c0 + W], reads=[o_d], writes=[of])
                    p.dma("sp", sgt[:, :W], sg_d[hh, :, c0:c0 + W], reads=[sg_d], writes=[sgt])
                    p.op("dve", "tensor_tensor", out=of[:, :W], in0=of[:, :W], in1=ot[:, :W], op=ALU.add, reads=[of, ot],
                         writes=[of])
                    p.op("act", "activation", out=osq[:, :W], in_=of[:, :W], func=AF.Square, reads=[of], writes=[osq])
                    ps4 = cx.bank()
                    mm(p, ps4, ps4[:, :W], cx.ones_bf, cx.ones_bf[:], osq, osq[:, :W], True, True)
                    p.op("act", "activation", out=ors[:, :W], in_=ps4[:, :W], func=AF.Sqrt, scale=1.0 / 128,
                         bias=cx.epsb[:], reads=[ps4, cx.epsb], writes=[ors])
                    p.op("dve", "reciprocal", out=ors[:, :W], in_=ors[:, :W], reads=[ors], writes=[ors])
                    p.op("dve", "scalar_tensor_tensor", out=oy[:, :W], in0=of[:, :W], scalar=nws[:, 0:1], in1=ors[:, :W],
                         op0=ALU.mult, op1=ALU.mult, reads=[of, nws, ors], writes=[oy])
                    o_b = ob[n % 2]
                    p.op("pool", "tensor_tensor", out=o_b[:, :W], in0=oy[:, :W], in1=sgt[:, :W], op=ALU.mult,
                         reads=[oy, sgt], writes=[o_b])
                    st_mix(p, d, hh, o_b, o_b[:, :W], c0, W, mc)
    p.close_scope()


def a1_dram(p, S, SL, pre="", fused=False):
    dd = {
        "svec": p.dram(pre + "svec", [128, NCH, 2], F32, kind="ExternalInput"),
        "modw": p.dram(pre + "modw", [2, NCH, 128, NCH, 128], F32, kind="ExternalInput"),
        "modb": p.dram(pre + "modb", [128, 2, NCH], F32, kind="ExternalInput"),
        "normw": p.dram(pre + "normw", [128, NCH], F32, kind="ExternalInput"),
        "win": p.dram(pre + "win", [128, NCH, 1792], F32, kind="ExternalInput"),
        "rmat": p.dram(pre + "rmat", [128, 128], F32, kind="ExternalInput"),
        "ident": p.dram(pre + "ident", [128, 128], F32, kind="ExternalInput"),
        "nws": p.dram(pre + "nws", [128, 3], F32, kind="ExternalInput"),
        "lbl": p.dram(pre + "lbl", [128, 2, 2, 2], F32, kind="ExternalInput"),
        "tri": p.dram(pre + "tri", [64, 2, 64], F32, kind="ExternalInput"),
        "cos": p.dram(pre + "cos", [128, SL], F32, kind="ExternalInput"),
        "sin": p.dram(pre + "sin", [128, SL], F32, kind="ExternalInput"),
    }
    dd["mixo"] = p.dram(pre + "mixo", [4, 128, S], BF16, kind="Internal" if fused else "ExternalOutput")
    if not fused:
        dd["xa"] = p.dram(pre + "xa", [128, NCH, S], F32, kind="ExternalInput")
    return dd


def make_A1(S, CT, stage=9):
    nc = bass.Bass("TRN2", target_bir_lowering=False)
    p = Prog(nc)
    cx = Ctx(p)
    d = a1_dram(p, S, S - CT)
    build_A1(p, cx, d, S, CT, stage)
    p.finish([d["mixo"]])
    return nc, p


def host_A1_inputs(j, c_b, c_ctx, mod_w, mod_b, norm_mix_w, cd_w_in, lb_logits, hgrn_norm_w, q_norm_w, k_norm_w, SL):
    l = 1
    comps = [0, 1]
    modw = np.stack([blk_lhsT(mod_w[l][:, k * D:(k + 1) * D]) for k in comps])
    modb = np.stack([fm(mod_b[l][k * D:(k + 1) * D]) for k in comps], axis=1)
    svec = np.stack([fm(c_b), fm(c_ctx)], axis=2)
    W = cd_w_in[0]
    r = np.arange(256 * j, 256 * j + 256)
    g = j // 2
    cols = np.concatenate([r, 1024 + r, 2048 + r, 4096 + r, 5120 + r, 6144 + np.arange(128 * g, 128 * g + 128),
                           3072 + r, 6400 + np.arange(128 * g, 128 * g + 128)])
    win = np.ascontiguousarray(W[:, cols].reshape(NCH, 128, 1792).transpose(1, 0, 2))
    heads = [2 * j, 2 * j + 1]
    lbl = np.stack([lb_logits[:, :, h * 128:(h + 1) * 128] for h in heads], axis=3)
    lbl = np.ascontiguousarray(lbl.transpose(2, 0, 1, 3))
    cos, sin = rope_tables(SL, 128)
    s_ = np.arange(64)[:, None]
    t_ = np.arange(64)[None, :]
    tri = np.stack([(s_ <= t_), (s_ >= t_)], axis=1).astype(np.float32)
    return {
        "svec": np.ascontiguousarray(svec), "modw": modw, "modb": np.ascontiguousarray(modb),
        "normw": fm(norm_mix_w[l]), "win": win, "rmat": rot_matrix(128), "ident": np.eye(128, dtype=np.float32),
        "nws": np.ascontiguousarray(np.stack([hgrn_norm_w[0], q_norm_w[0], k_norm_w[0]], axis=1)),
        "lbl": lbl, "tri": np.ascontiguousarray(tri), "cos": cos, "sin": sin,
    }


SEQ = 16384
CTX = 256
NCORE = 8
_CACHE = {}


def _prog(key, fn):
    if key not in _CACHE:
        _CACHE[key] = fn()
    return _CACHE[key]


def _dbg(tag, arrs):
    import os, sys
    if not os.environ.get("KDEBUG"):
        return
    for i, a in enumerate(arrs):
        a = np.asarray(a, dtype=np.float32)
        bad = ~np.isfinite(a)
        msg = "[kdebug] %s core %d nonfinite=%d rms=%.4g" % (tag, i, int(bad.sum()), float(np.sqrt(np.mean(np.where(bad, 0, a) ** 2))))
        if bad.any():
            idx = np.argwhere(bad)
            msg += " first=%s chunks=%s" % (idx[0].tolist(), sorted(set(idx[:, 0].tolist()))[:8])
        print(msg, file=sys.stderr, flush=True)


def _from_fm(a):
    P, C, N = a.shape
    return a.transpose(2, 1, 0).reshape(N, C * P)


def kernel_unfused(x, c, ctx, c_ctx, mod_w, mod_b, norm_mix_w, norm_ffn_w, ffn_w_gate, ffn_w_up, ffn_w_down,
           ab_w_in, ab_w_out, lru_conv_w, lru_conv_b, lru_wa, lru_ba, lru_wx, lru_bx, lru_lambda,
           diff_lq1, diff_lk1, diff_lq2, diff_lk2, diff_subln_w,
           cd_w_in, cd_w_out, hgrn_lb_logits, hgrn_norm_w, gqa_q_norm_w, gqa_k_norm_w, final_norm_w):
    f = lambda a: np.asarray(a, dtype=np.float32)
    (x, c, ctx, c_ctx, mod_w, mod_b, norm_mix_w, norm_ffn_w, ffn_w_gate, ffn_w_up, ffn_w_down, ab_w_in, ab_w_out,
     lru_conv_w, lru_conv_b, lru_wa, lru_ba, lru_wx, lru_bx, lru_lambda, diff_lq1, diff_lk1, diff_lq2, diff_lk2,
     diff_subln_w, cd_w_in, cd_w_out, hgrn_lb_logits, hgrn_norm_w, gqa_q_norm_w, gqa_k_norm_w, final_norm_w) = map(f, (
        x, c, ctx, c_ctx, mod_w, mod_b, norm_mix_w, norm_ffn_w, ffn_w_gate, ffn_w_up, ffn_w_down, ab_w_in, ab_w_out,
        lru_conv_w, lru_conv_b, lru_wa, lru_ba, lru_wx, lru_bx, lru_lambda, diff_lq1, diff_lk1, diff_lq2, diff_lk2,
        diff_subln_w, cd_w_in, cd_w_out, hgrn_lb_logits, hgrn_norm_w, gqa_q_norm_w, gqa_k_norm_w, final_norm_w))
    B = x.shape[0]
    SL = x.shape[1]
    CT = ctx.shape[1]
    S = CT + SL
    QL = SL // 4
    QC = CT // 4
    cores = list(range(NCORE))

    def mix_assemble(res, with_ctx):
        outs = []
        for core in cores:
            b, q = core // 4, core % 4
            full = np.empty((NCH, 128, S), dtype=ml_dtypes.bfloat16)
            for j in range(4):
                o = res[b * 4 + j]["mixo"]
                full[2 * j] = o[0]
                full[2 * j + 1] = o[1]
                full[8 + 2 * j] = o[2]
                full[8 + 2 * j + 1] = o[3]
            parts = [full[:, :, CT + q * QL:CT + (q + 1) * QL]]
            if with_ctx:
                parts.append(full[:, :, q * QC:(q + 1) * QC])
            outs.append(np.ascontiguousarray(np.concatenate(parts, axis=2).transpose(1, 0, 2)))
        return outs

    nc, _ = _prog(("A0", S, CT), lambda: make_A0(S, CT))
    xa = [fm_act(np.concatenate([ctx[b], x[b]], 0).T) for b in range(B)]
    maps = []
    for core in cores:
        b, j = core // 4, core % 4
        h = host_A0_inputs(j, c[b], c_ctx, mod_w, mod_b, norm_mix_w, ab_w_in, lru_conv_w, lru_conv_b, lru_wa, lru_ba,
                           lru_wx, lru_bx, lru_lambda, diff_lq1, diff_lk1, diff_lq2, diff_lk2, diff_subln_w, SL)
        h["xa"] = xa[b]
        maps.append(h)
    resA0 = run_bass_kernel_spmd(nc, maps, core_ids=cores).results
    del maps
    _dbg("A0 mixo", [r["mixo"] for r in resA0])
    NT0 = QL + QC
    tiles0 = [(i * 512, 512, 0) for i in range(QL // 512)] + [(QL, QC, 1)]
    nc, _ = _prog(("B", NT0, 0), lambda: make_B(NT0, tiles0, False))
    mixs = mix_assemble(resA0, True)
    del resA0
    wB = [host_B_inputs(0, c[b], c_ctx, mod_w, mod_b, norm_ffn_w, ab_w_out[0], ffn_w_gate, ffn_w_up, ffn_w_down)
          for b in range(B)]
    maps = []
    for core in cores:
        b, q = core // 4, core % 4
        h = dict(wB[b])
        xr = np.concatenate([x[b][q * QL:(q + 1) * QL], ctx[b][q * QC:(q + 1) * QC]], 0)
        h["xres"] = fm_act(xr.T)
        h["mix"] = mixs[core]
        maps.append(h)
    resB0 = run_bass_kernel_spmd(nc, maps, core_ids=cores).results
    del maps, wB, mixs
    _dbg("B0 xout", [r["xout"] for r in resB0])
    nc, _ = _prog(("A1", S, CT), lambda: make_A1(S, CT))
    x1q = [resB0[core]["xout"] for core in cores]
    del resB0
    xa = []
    for b in range(B):
        lat = np.concatenate([x1q[b * 4 + q][:, :, :QL] for q in range(4)], axis=2)
        cc = np.concatenate([x1q[b * 4 + q][:, :, QL:] for q in range(4)], axis=2)
        xa.append(np.ascontiguousarray(np.concatenate([cc, lat], axis=2)))
    maps = []
    for core in cores:
        b, j = core // 4, core % 4
        h = host_A1_inputs(j, c[b], c_ctx, mod_w, mod_b, norm_mix_w, cd_w_in, hgrn_lb_logits, hgrn_norm_w,
                           gqa_q_norm_w, gqa_k_norm_w, SL)
        h["xa"] = xa[b]
        maps.append(h)
    resA1 = run_bass_kernel_spmd(nc, maps, core_ids=cores).results
    del maps, xa
    _dbg("A1 mixo", [r["mixo"][:, :, CT:] for r in resA1])
    tiles1 = [(i * 512, 512, 0) for i in range(QL // 512)]
    nc, _ = _prog(("B", QL, 1), lambda: make_B(QL, tiles1, True))
    mixs = mix_assemble(resA1, False)
    del resA1
    wB = [host_B_inputs(1, c[b], c_ctx, mod_w, mod_b, norm_ffn_w, cd_w_out[0], ffn_w_gate, ffn_w_up, ffn_w_down,
                        final_norm_w) for b in range(B)]
    maps = []
    for core in cores:
        b, q = core // 4, core % 4
        h = dict(wB[b])
        h["xres"] = np.ascontiguousarray(x1q[core][:, :, :QL])
        h["mix"] = mixs[core]
        maps.append(h)
    resB1 = run_bass_kernel_spmd(nc, maps, core_ids=cores).results
    out = np.empty((B, SL, D), dtype=np.float32)
    for core in cores:
        b, q = core // 4, core % 4
        out[b, q * QL:(q + 1) * QL] = _from_fm(resB1[core]["xout"])
    return out


GROUPS = [[0, 1, 2, 3], [4, 5, 6, 7]]


def b_dram(p, pre, last):
    d = {
        "svec": p.dram(pre + "svec", [128, NCH, 2], F32, kind="ExternalInput"),
        "modw": p.dram(pre + "modw", [4, NCH, 128, NCH, 128], F32, kind="ExternalInput"),
        "modb": p.dram(pre + "modb", [128, 4, NCH], F32, kind="ExternalInput"),
        "normw": p.dram(pre + "normw", [128, NCH], F32, kind="ExternalInput"),
        "wout": p.dram(pre + "wout", [NCH, 128, NCH, 128], F32, kind="ExternalInput"),
        "wgu": p.dram(pre + "wgu", [NJ, 128, 2, NCH, 128], F32, kind="ExternalInput"),
        "wdn": p.dram(pre + "wdn", [NCH, 128, NJ, 128], F32, kind="ExternalInput"),
    }
    if last:
        d["finalw"] = p.dram(pre + "finalw", [128, NCH], F32, kind="ExternalInput")
    return d


def fc_src(fc):
    if fc < 8:
        return fc % 2, fc // 2
    return 2 + (fc - 8) % 2, (fc - 8) // 2


def make_fused(S, CT):
    SL = S - CT
    QL = SL // 4
    QC = CT // 4
    HL = QL // 2
    nc = bass.Bass("TRN2", target_bir_lowering=False)
    p = Prog(nc)
    cx = Ctx(p)
    oh_d = p.dram("oh", [128, 4], F32, kind="ExternalInput")
    oh = p.sbuf("oh", [128, 4], F32)
    p.dma("sp", oh[:], oh_d[:, :], writes=[oh])

    def mix_loader(Gl, Gc):
        def ld(mx, c0, W, mc):
            FG = 2
            for grp in range(NCH // FG):
                buf = ld.buf
                for qq in range(4):
                    for f_ in range(FG):
                        ch, rk = fc_src(grp * FG + f_)
                        if mc == 0:
                            src = Gl[ch, qq, rk * 128:(rk + 1) * 128, c0:c0 + W]
                            p.dma("sp", buf[:, qq, f_, :W], src, reads=[Gl], writes=[buf])
                        else:
                            src = Gc[ch, rk * 128:(rk + 1) * 128, qq * QC:(qq + 1) * QC]
                            p.dma("sp", buf[:, qq, f_, :W], src, reads=[Gc], writes=[buf])
                dst = mx[:, grp * FG:(grp + 1) * FG, :W]
                p.op("dve", "tensor_scalar", out=dst, in0=buf[:, 0, :, :W], scalar1=oh[:, 0:1], scalar2=None,
                     op0=ALU.mult, reads=[buf, oh], writes=[mx])
                for qq in range(1, 4):
                    p.op("dve", "scalar_tensor_tensor", out=dst, in0=buf[:, qq, :, :W], scalar=oh[:, qq:qq + 1], in1=dst,
                         op0=ALU.mult, op1=ALU.add, reads=[buf, oh, mx], writes=[mx])
        return ld

    dB0 = b_dram(p, "b0_", False)
    dB1 = b_dram(p, "b1_", True)
    prep_B_weights(p, dB0)
    prep_B_weights(p, dB1)

    def mix_store(mixc, mixl):
        def st(ch, src_t, src_ap, c0, W, mc):
            if mc == 1:
                p.dma("pool", mixc[ch, :, c0:c0 + W], src_ap, reads=[src_t], writes=[mixc])
            else:
                col = c0 - CT
                p.dma("pool", mixl[ch, col // QL, :, col % QL:col % QL + W], src_ap, reads=[src_t], writes=[mixl])
        return st

    dA0 = a0_dram(p, S, SL, pre="a0_", fused=True)
    m0c = p.dram("m0c", [4, 128, CT], BF16)
    m0l = p.dram("m0l", [4, 4, 128, QL], BF16)
    dA0["st_mix"] = mix_store(m0c, m0l)
    p.open_scope()
    build_A0(p, cx, dA0, S, CT, 0.8 - 0.6 * math.exp(0.0))
    p.close_scope()
    G1l = p.dram("G1l", [4, 4, 512, QL], BF16)
    G1c = p.dram("G1c", [4, 512, CT], BF16)
    for ch in range(4):
        p.coll(m0c, m0c[ch].opt(), G1c, G1c[ch].opt(), GROUPS)
        for qq in range(4):
            p.coll(m0l, m0l[ch, qq].opt(), G1l, G1l[ch, qq].opt(), GROUPS)
    x1_d = p.dram("x1_d", [NCH, 2, 128, HL], F32)
    xc1_d = p.dram("xc1_d", [128, NCH, QC], F32)
    dB0["xres"] = p.dram("b0_xres", [128, NCH, QL + QC], F32, kind="ExternalInput")
    p.open_scope()
    ld0 = mix_loader(G1l, G1c)
    ld0.buf = p.sbuf("selbuf", [128, 4, 2, 512], BF16)
    dB0["ld_mix"] = ld0

    def st0(x1, c0, W, mc):
        if mc == 0:
            p.dma("pool", x1_d[:, c0 // HL, :, c0 % HL:c0 % HL + W].rearrange("c p w -> p c w"), x1[:, :, :W],
                  reads=[x1], writes=[x1_d])
        else:
            p.dma("pool", xc1_d[:, :, :], x1[:, :, :W], reads=[x1], writes=[xc1_d])
    dB0["st_x"] = st0
    tiles0 = [(i * 512, 512, 0) for i in range(QL // 512)] + [(QL, QC, 1)]
    build_B(p, cx, dB0, tiles0, False)
    p.close_scope()
    G2l = p.dram("G2l", [NCH, 2, 512, HL], F32)
    G2c = p.dram("G2c", [512, NCH * QC], F32)
    for c in range(NCH):
        for hf in range(2):
            p.coll(x1_d, x1_d[c, hf].opt(), G2l, G2l[c, hf].opt(), GROUPS)
    p.coll(xc1_d, xc1_d[:, :, :].rearrange("p c w -> p (c w)").opt(), G2c, G2c[:, :].opt(), GROUPS)
    dA1 = a1_dram(p, S, SL, pre="a1_", fused=True)

    def ld_xa1(xt, c0, W, mc):
        if mc == 1:
            for r in range(4):
                p.dma("sp", xt[:, :, r * QC:(r + 1) * QC],
                      G2c[r * 128:(r + 1) * 128, :].rearrange("p (c w) -> p c w", c=NCH), reads=[G2c], writes=[xt])
        else:
            col = c0 - CT
            r, within = col // QL, col % QL
            hf, off = within // HL, within % HL
            for c in range(NCH):
                p.dma("sp", xt[:, c, :W], G2l[c, hf, r * 128:(r + 1) * 128, off:off + W], reads=[G2l], writes=[xt])
    dA1["ld_xa"] = ld_xa1
    m1c = p.dram("m1c", [4, 128, CT], BF16)
    m1l = p.dram("m1l", [4, 4, 128, QL], BF16)
    dA1["st_mix"] = mix_store(m1c, m1l)
    p.open_scope()
    build_A1(p, cx, dA1, S, CT)
    p.close_scope()
    G3l = p.dram("G3l", [4, 4, 512, QL], BF16)
    for ch in range(4):
        for qq in range(4):
            p.coll(m1l, m1l[ch, qq].opt(), G3l, G3l[ch, qq].opt(), GROUPS)
    dB1["xout"] = p.dram("b1_xout", [128, NCH, QL], F32, kind="ExternalOutput")
    p.open_scope()
    ld1 = mix_loader(G3l, None)
    ld1.buf = p.sbuf("selbuf1", [128, 4, 2, 512], BF16)
    dB1["ld_mix"] = ld1

    def ldx1(x1, c0, W, mc):
        p.dma("sp", x1[:, :, :W], x1_d[:, c0 // HL, :, c0 % HL:c0 % HL + W].rearrange("c p w -> p c w"),
              reads=[x1_d], writes=[x1])
    dB1["ld_x"] = ldx1
    tiles1 = [(i * 512, 512, 0) for i in range(QL // 512)]
    build_B(p, cx, dB1, tiles1, True)
    p.close_scope()
    p.finish([dB1["xout"]])
    return nc, p


def kernel(x, c, ctx, c_ctx, mod_w, mod_b, norm_mix_w, norm_ffn_w, ffn_w_gate, ffn_w_up, ffn_w_down,
           ab_w_in, ab_w_out, lru_conv_w, lru_conv_b, lru_wa, lru_ba, lru_wx, lru_bx, lru_lambda,
           diff_lq1, diff_lk1, diff_lq2, diff_lk2, diff_subln_w,
           cd_w_in, cd_w_out, hgrn_lb_logits, hgrn_norm_w, gqa_q_norm_w, gqa_k_norm_w, final_norm_w):
    f = lambda a: np.asarray(a, dtype=np.float32)
    (x, c, ctx, c_ctx, mod_w, mod_b, norm_mix_w, norm_ffn_w, ffn_w_gate, ffn_w_up, ffn_w_down, ab_w_in, ab_w_out,
     lru_conv_w, lru_conv_b, lru_wa, lru_ba, lru_wx, lru_bx, lru_lambda, diff_lq1, diff_lk1, diff_lq2, diff_lk2,
     diff_subln_w, cd_w_in, cd_w_out, hgrn_lb_logits, hgrn_norm_w, gqa_q_norm_w, gqa_k_norm_w, final_norm_w) = map(f, (
        x, c, ctx, c_ctx, mod_w, mod_b, norm_mix_w, norm_ffn_w, ffn_w_gate, ffn_w_up, ffn_w_down, ab_w_in, ab_w_out,
        lru_conv_w, lru_conv_b, lru_wa, lru_ba, lru_wx, lru_bx, lru_lambda, diff_lq1, diff_lk1, diff_lq2, diff_lk2,
        diff_subln_w, cd_w_in, cd_w_out, hgrn_lb_logits, hgrn_norm_w, gqa_q_norm_w, gqa_k_norm_w, final_norm_w))
    B, SL = x.shape[0], x.shape[1]
    CT = ctx.shape[1]
    S = CT + SL
    QL, QC = SL // 4, CT // 4
    cores = list(range(NCORE))
    nc, _ = _prog(("fused", S, CT), lambda: make_fused(S, CT))
    xa = [fm_act(np.concatenate([ctx[b], x[b]], 0).T) for b in range(B)]
    wB0 = [host_B_inputs(0, c[b], c_ctx, mod_w, mod_b, norm_ffn_w, ab_w_out[0], ffn_w_gate, ffn_w_up, ffn_w_down)
           for b in range(B)]
    wB1 = [host_B_inputs(1, c[b], c_ctx, mod_w, mod_b, norm_ffn_w, cd_w_out[0], ffn_w_gate, ffn_w_up, ffn_w_down,
                         final_norm_w) for b in range(B)]
    maps = []
    for core in cores:
        b, j = core // 4, core % 4
        m = {}
        hA0 = host_A0_inputs(j, c[b], c_ctx, mod_w, mod_b, norm_mix_w, ab_w_in, lru_conv_w, lru_conv_b, lru_wa,
                             lru_ba, lru_wx, lru_bx, lru_lambda, diff_lq1, diff_lk1, diff_lq2, diff_lk2,
                             diff_subln_w, SL)
        hA0["xa"] = xa[b]
        for k, v in hA0.items():
            m["a0_" + k] = v
        for k, v in wB0[b].items():
            m["b0_" + k] = v
        xr = np.concatenate([x[b][j * QL:(j + 1) * QL], ctx[b][j * QC:(j + 1) * QC]], 0)
        m["b0_xres"] = fm_act(xr.T)
        hA1 = host_A1_inputs(j, c[b], c_ctx, mod_w, mod_b, norm_mix_w, cd_w_in, hgrn_lb_logits, hgrn_norm_w,
                             gqa_q_norm_w, gqa_k_norm_w, SL)
        for k, v in hA1.items():
            m["a1_" + k] = v
        for k, v in wB1[b].items():
            m["b1_" + k] = v
        ohv = np.zeros((128, 4), np.float32)
        ohv[:, j] = 1.0
        m["oh"] = ohv
        maps.append(m)
    res = run_bass_kernel_spmd(nc, maps, core_ids=cores).results
    out = np.empty((B, SL, D), dtype=np.float32)
    for core in cores:
        b, q = core // 4, core % 4
        out[b, q * QL:(q + 1) * QL] = _from_fm(res[core]["b1_xout"])
    return out
```

```python
import math
from contextlib import ExitStack
import numpy as np
import ml_dtypes
import concourse.bass as bass
import concourse.mybir as mybir
from concourse.bass_utils import run_bass_kernel_spmd

F32 = mybir.dt.float32
BF16 = mybir.dt.bfloat16
AF = mybir.ActivationFunctionType
ALU = mybir.AluOpType
AX = mybir.AxisListType

SEM_LIMIT = 30000
LAG = 2

D = 2048
NCH = 16
FF = 5632
NJ = 44
EPS = 1e-6


class T:
    __slots__ = ("t", "w", "r", "dsem", "dcnt", "name", "ap")

    def __init__(self, t, name, ap=None):
        self.t = t
        self.name = name
        self.w = None
        self.r = []
        self.dsem = None
        self.dcnt = 0
        self.ap = ap

    def __getitem__(self, idx):
        if self.ap is not None:
            return self.ap[idx]
        return self.t[idx]


class Prog:
    ENGS = ("pe", "act", "dve", "pool", "sp")

    def __init__(self, nc):
        self.nc = nc
        self.es = ExitStack()
        self.q = {e: [] for e in self.ENGS}
        self.cnt = {e: 0 for e in self.ENGS}
        self.sem = {}
        self.semobj = {}
        self.nsem = 0
        self.waited = {e: {} for e in self.ENGS}
        self.ninst = 0
        self.nm = 0
        self.scopes = []
        self.inherit = {}
        self.free_dsems = []
        for e in ("pe", "act", "dve", "pool"):
            self._new_eng_sem(e)

    def _alloc_sem(self, name):
        self.nsem += 1
        key = "%s_%d" % (name, self.nsem)
        s = self.es.enter_context(self.nc.semaphore(key))
        self.semobj[key] = s
        return key

    def _new_eng_sem(self, e):
        self.sem[e] = self._alloc_sem("s_" + e)
        self.cnt[e] = 0

    def uniq(self, name):
        self.nm += 1
        return "%s_%d" % (name, self.nm)

    def sbuf(self, name, shape, dt):
        name = self.uniq(name)
        st = self.scopes[-1][0] if self.scopes else self.es
        t = st.enter_context(self.nc.sbuf_tensor(name, list(shape), dt))
        tt = T(t, name)
        tt.r = list(self.inherit.items())
        if self.scopes:
            self.scopes[-1][1].append(tt)
        return tt

    def open_scope(self):
        self.scopes.append((ExitStack(), []))

    def close_scope(self):
        st, lst = self.scopes.pop()
        for t in lst:
            deps = list(t.r)
            if t.w is not None:
                deps.append(t.w)
            for k, v in deps:
                if self.inherit.get(k, 0) < v:
                    self.inherit[k] = v
            if t.dsem is not None:
                self.free_dsems.append((t.dsem, t.dcnt))
                t.dsem = None
        st.close()

    def psum(self, name, shape, dt=F32):
        name = self.uniq(name)
        t = self.es.enter_context(self.nc.psum_tensor(name, list(shape), dt))
        return T(t, name)

    def dram(self, name, shape, dt, kind="Internal"):
        t = self.nc.dram_tensor(name, list(shape), dt, kind=kind)
        tt = T(t, name, ap=t.ap())
        if self.scopes and kind == "Internal":
            self.scopes[-1][1].append(tt)
        return tt

    def _deps(self, reads, writes):
        need = {}
        for t in reads:
            if t.w is not None:
                s, v = t.w
                if need.get(s, 0) < v:
                    need[s] = v
        for t in writes:
            if t.w is not None:
                s, v = t.w
                if need.get(s, 0) < v:
                    need[s] = v
            for (s, v) in t.r:
                if need.get(s, 0) < v:
                    need[s] = v
        return need

    def _emit_waits(self, e, need):
        wd = self.waited[e]
        for s, v in need.items():
            if wd.get(s, 0) >= v:
                continue
            wd[s] = v
            so = self.semobj[s]
            self.q[e].append(lambda en, so=so, v=v: en.wait_ge(so, v))
            self.ninst += 1

    def _mark(self, dep, reads, writes):
        for t in writes:
            t.w = dep
            t.r = []
        for t in reads:
            t.r.append(dep)
            if len(t.r) > 32:
                m = {}
                for s, v in t.r:
                    if m.get(s, 0) < v:
                        m[s] = v
                t.r = list(m.items())

    def op(self, e, fn, reads=(), writes=(), **kw):
        need = self._deps(reads, writes)
        if e == "pe":
            need = {k: v for k, v in need.items() if not k.startswith("s_pe")}
        self._emit_waits(e, need)
        if self.cnt[e] >= SEM_LIMIT:
            self._new_eng_sem(e)
        self.cnt[e] += 1
        key = self.sem[e]
        so = self.semobj[key]
        self.q[e].append(lambda en, fn=fn, so=so, kw=kw: getattr(en, fn)(**kw).then_inc(so, 1))
        self.ninst += 1
        dep = (key, self.cnt[e])
        self._mark(dep, reads, writes)
        return dep

    def dma(self, e, out, in_, reads=(), writes=(), **kw):
        need = self._deps(reads, writes)
        self._emit_waits(e, need)
        d = writes[0]
        if d.dsem is None and self.free_dsems:
            d.dsem, d.dcnt = self.free_dsems.pop()
        if d.dsem is None or d.dcnt >= SEM_LIMIT * 16:
            d.dsem = self._alloc_sem("d_" + d.name)
            d.dcnt = 0
        d.dcnt += 16
        so = self.semobj[d.dsem]
        self.q[e].append(
            lambda en, so=so, out=out, in_=in_, kw=kw: en.dma_start(out=out, in_=in_, **kw).then_inc(so, 16))
        self.ninst += 1
        dep = (d.dsem, d.dcnt)
        self._mark(dep, reads, writes)
        return dep

    def coll(self, src_t, src_ap, dst_t, dst_ap, groups):
        need = self._deps([src_t], [dst_t])
        self._emit_waits("pool", need)
        d = dst_t
        if d.dsem is None:
            d.dsem = self._alloc_sem("c_" + d.name)
            d.dcnt = 0
        d.dcnt += 1
        so = self.semobj[d.dsem]
        self.q["pool"].append(lambda en, so=so, a=src_ap, b=dst_ap, g=groups: en.collective_compute(
            "AllGather", ALU.bypass, replica_groups=g, ins=[a], outs=[b]).then_inc(so))
        self.ninst += 1
        dep = (d.dsem, d.dcnt)
        self._mark(dep, [src_t], [dst_t])
        return dep

    def finish(self, out_tiles):
        need = {}
        for t in out_tiles:
            if t.w is not None:
                s, v = t.w
                if need.get(s, 0) < v:
                    need[s] = v
        self._emit_waits("sp", need)
        q = self.q
        with self.nc.Block() as block:
            @block.sync
            def _(en):
                for f in q["sp"]:
                    f(en)

            @block.tensor
            def _(en):
                for f in q["pe"]:
                    f(en)

            @block.scalar
            def _(en):
                for f in q["act"]:
                    f(en)

            @block.vector
            def _(en):
                for f in q["dve"]:
                    f(en)

            @block.gpsimd
            def _(en):
                for f in q["pool"]:
                    f(en)
        self.es.close()


class Ctx:
    def __init__(self, p):
        self.p = p
        self.banks = [p.psum("bank%d" % i, [128, 512]) for i in range(6)]
        self.bank_i = 0
        self.pmisc = p.psum("pmisc", [128, 512])
        self.banks.append(p.psum("bank6", [128, 512]))
        self.ones_bf = p.sbuf("ones_bf", [128, 128], BF16)
        p.op("dve", "memset", ap=self.ones_bf[:], constant=1.0, writes=[self.ones_bf])
        self.epsb = p.sbuf("epsb", [128, 1], F32)
        p.op("dve", "memset", ap=self.epsb[:], constant=EPS, writes=[self.epsb])

    def bank(self):
        b = self.banks[self.bank_i % len(self.banks)]
        self.bank_i += 1
        return b

    def bankbf(self):
        self.bf_i += 1
        return self.bft[self.bf_i % 2]


def mm(p, out_t, out_ap, lhsT_t, lhsT_ap, rhs_t, rhs_ap, start, stop):
    p.op("pe", "matmul", out=out_ap, lhsT=lhsT_ap, rhs=rhs_ap, start=start, stop=stop,
         reads=[lhsT_t, rhs_t], writes=[out_t])


def compute_mod(p, cx, svec_d, modw_d, modb_d, ncomp, mod_sb):
    sv = p.sbuf("sv", [128, NCH, 2], F32)
    sil = p.sbuf("sil", [128, NCH, 2], F32)
    mb = p.sbuf("modb", [128, ncomp, NCH], F32)
    p.dma("sp", sv[:], svec_d[:, :, :], writes=[sv])
    p.dma("sp", mb[:], modb_d[:, :, :], writes=[mb])
    p.op("act", "activation", out=sil[:], in_=sv[:], func=AF.Silu, reads=[sv], writes=[sil])
    wb = [p.sbuf("modwb%d" % i, [128, NCH, 128], F32) for i in range(2)]
    k = 0
    for comp in range(ncomp):
        for oc in range(NCH):
            w = wb[k % 2]
            k += 1
            p.dma("sp", w[:], modw_d[comp, oc], writes=[w])
            ps = cx.bank()
            for c in range(NCH):
                mm(p, ps, ps[:, 0:2], w, w[:, c, :], sil, sil[:, c, :], c == 0, c == NCH - 1)
            p.op("dve", "tensor_scalar", out=mod_sb[:, comp, oc, :], in0=ps[:, 0:2], scalar1=mb[:, comp, oc:oc + 1], scalar2=None,
                op0=ALU.add, reads=[ps, mb], writes=[mod_sb])


def rms_stats(p, cx, x_t, x_ap3, W, sq_t, rstd_t):
    p.op("act", "activation", out=sq_t[:, 0:NCH, :W], in_=x_ap3, func=AF.Square, reads=[x_t], writes=[sq_t])
    ps = cx.bank()
    for c in range(NCH):
        mm(p, ps, ps[:, :W], cx.ones_bf, cx.ones_bf[:], sq_t, sq_t[:, c, :W], c == 0, c == NCH - 1)
    p.op("act", "activation", out=rstd_t[:, :W], in_=ps[:, :W], func=AF.Sqrt, scale=1.0 / D,
                                       bias=cx.epsb[:], reads=[ps, cx.epsb], writes=[rstd_t])
    p.op("dve", "reciprocal", out=rstd_t[:, :W], in_=rstd_t[:, :W], reads=[rstd_t], writes=[rstd_t])


def norm_mod(p, cx, x_t, x3, W, rstd_t, tmp_t, A_t, A_ap, B_t, B_ap, h_t, h3):
    for c in range(NCH):
        tt = tmp_t[c % 2]
        p.op("dve", "tensor_tensor", out=tt[:, :W], in0=x3(c), in1=rstd_t[:, :W], op=ALU.mult,
             reads=[x_t, rstd_t], writes=[tt])
        p.op("act", "activation", out=h3(c), in_=tt[:, :W], func=AF.Identity,
                                                       scale=A_ap(c), bias=B_ap(c),
             reads=[tt, A_t, B_t], writes=[h_t])


def prep_B_weights(p, d, q="pool"):
    wout_b = p.dram(p.uniq("wout_b"), [NCH, 128, NCH, 128], BF16)
    wgu_b = p.dram(p.uniq("wgu_b"), [NJ, 128, 2, NCH, 128], BF16)
    wdn_b = p.dram(p.uniq("wdn_b"), [NCH, 128, NJ, 128], BF16)
    for o in range(0, NCH, 4):
        p.dma(q, wout_b[o:o + 4], d["wout"][o:o + 4], reads=[d["wout"]], writes=[wout_b])
    for j in range(0, NJ, 4):
        p.dma(q, wgu_b[j:j + 4], d["wgu"][j:j + 4], reads=[d["wgu"]], writes=[wgu_b])
    for o in range(0, NCH, 2):
        p.dma(q, wdn_b[o:o + 2], d["wdn"][o:o + 2], reads=[d["wdn"]], writes=[wdn_b])
    d["wcast"] = (wout_b, wgu_b, wdn_b)


def build_B(p, cx, d, tiles, last):
    if "wcast" not in d:
        prep_B_weights(p, d)
    wout_b, wgu_b, wdn_b = d["wcast"]

    mod = p.sbuf("modB", [128, 4, NCH, 2], F32)
    compute_mod(p, cx, d["svec"], d["modw"], d["modb"], 4, mod)
    nw = p.sbuf("nwB", [128, NCH], F32)
    p.dma("sp", nw[:], d["normw"][:, :], writes=[nw])
    A2 = p.sbuf("A2", [128, NCH, 2], F32)
    for col in range(2):
        p.op("dve", "scalar_tensor_tensor", out=A2[:, :, col], in0=mod[:, 2, :, col], scalar=1.0, in1=nw[:], op0=ALU.add, op1=ALU.mult,
            reads=[mod, nw], writes=[A2])
    if last:
        fw = p.sbuf("fwB", [128, NCH], F32)
        p.dma("sp", fw[:], d["finalw"][:, :], writes=[fw])

    TW = 512
    xr_t = [p.sbuf("xresB%d" % i, [128, NCH, TW], F32) for i in range(1)]
    mx_t = [p.sbuf("mixB%d" % i, [128, NCH, TW], BF16) for i in range(1)]
    h_t = p.sbuf("hB", [128, NCH, TW], BF16)
    u_t = p.sbuf("uB", [128, NJ, TW], BF16)
    sq_t = u_t
    rstd_t = p.sbuf("rstdB", [128, TW], F32)
    tmp_t = [p.sbuf("tmpB%d" % i, [128, TW], F32) for i in range(2)]
    sg_t = [p.sbuf("sgB%d" % i, [128, TW], F32) for i in range(2)]
    wo_t = [p.sbuf("woB%d" % i, [128, NCH, 128], BF16) for i in range(2)]
    wg_t = [p.sbuf("wgB%d" % i, [128, 2, NCH, 128], BF16) for i in range(3)]
    wd_t = [p.sbuf("wdB%d" % i, [128, NJ, 128], BF16) for i in range(2)]
    ko = kg = kd = 0

    for ti, (c0, W, mc) in enumerate(tiles):
        x1 = xr_t[0]
        mx = mx_t[0]
        if "ld_x" in d:
            d["ld_x"](x1, c0, W, mc)
        else:
            p.dma("sp", x1[:, :, :W], d["xres"][:, :, c0:c0 + W], reads=[d["xres"]], writes=[x1])
        if "ld_mix" in d:
            d["ld_mix"](mx, c0, W, mc)
        else:
            p.dma("sp", mx[:, :, :W], d["mix"][:, :, c0:c0 + W], reads=[d["mix"]], writes=[mx])
        for o in range(NCH):
            w = wo_t[ko % 2]
            ko += 1
            p.dma("act", w[:], wout_b[o], reads=[wout_b], writes=[w])
            ps = cx.bank()
            for c in range(NCH):
                mm(p, ps, ps[:, :W], w, w[:, c, :], mx, mx[:, c, :W], c == 0, c == NCH - 1)
            p.op("dve", "scalar_tensor_tensor", out=x1[:, o, :W], in0=ps[:, :W], scalar=mod[:, 0, o, mc:mc + 1], in1=x1[:, o, :W],
                op0=ALU.mult, op1=ALU.add, reads=[ps, mod, x1], writes=[x1])
        rms_stats(p, cx, x1, x1[:, :, :W], W, sq_t, rstd_t)
        norm_mod(p, cx, x1, lambda c: x1[:, c, :W], W, rstd_t, tmp_t,
                 A2, lambda c: A2[:, c, mc:mc + 1], mod, lambda c: mod[:, 1, c, mc:mc + 1],
                 h_t, lambda c: h_t[:, c, :W])
        for j in range(NJ):
            w = wg_t[kg % 3]
            kg += 1
            p.dma("sp", w[:], wgu_b[j], reads=[wgu_b], writes=[w])
            pg = cx.bank()
            pu = cx.bank()
            for c in range(NCH):
                mm(p, pg, pg[:, :W], w, w[:, 0, c, :], h_t, h_t[:, c, :W], c == 0, c == NCH - 1)
            for c in range(NCH):
                mm(p, pu, pu[:, :W], w, w[:, 1, c, :], h_t, h_t[:, c, :W], c == 0, c == NCH - 1)
            sg = sg_t[j % 2]
            p.op("act", "activation", out=sg[:, :W], in_=pg[:, :W], func=AF.Silu,
                 reads=[pg], writes=[sg])
            p.op("dve", "tensor_tensor", out=u_t[:, j, :W], in0=pu[:, :W], in1=sg[:, :W],
                                                                  op=ALU.mult, reads=[pu, sg], writes=[u_t])
        for o in range(NCH):
            w = wd_t[kd % 2]
            kd += 1
            p.dma("act", w[:], wdn_b[o], reads=[wdn_b], writes=[w])
            ps = cx.bank()
            for j in range(NJ):
                mm(p, ps, ps[:, :W], w, w[:, j, :], u_t, u_t[:, j, :W], j == 0, j == NJ - 1)
            p.op("dve", "scalar_tensor_tensor", out=x1[:, o, :W], in0=ps[:, :W], scalar=mod[:, 3, o, mc:mc + 1], in1=x1[:, o, :W],
                op0=ALU.mult, op1=ALU.add, reads=[ps, mod, x1], writes=[x1])
        if last:
            rms_stats(p, cx, x1, x1[:, :, :W], W, sq_t, rstd_t)
            for c in range(NCH):
                p.op("dve", "scalar_tensor_tensor", out=x1[:, c, :W], in0=x1[:, c, :W], scalar=fw[:, c:c + 1], in1=rstd_t[:, :W],
                    op0=ALU.mult, op1=ALU.mult, reads=[x1, fw, rstd_t], writes=[x1])
        if "st_x" in d:
            d["st_x"](x1, c0, W, mc)
        else:
            p.dma("pool", d["xout"][:, :, c0:c0 + W], x1[:, :, :W], reads=[x1], writes=[d["xout"]])


def make_B(NT, tiles, last):
    nc = bass.Bass("TRN2", target_bir_lowering=False)
    p = Prog(nc)
    cx = Ctx(p)
    d = {
        "xres": p.dram("xres", [128, NCH, NT], F32, kind="ExternalInput"),
        "mix": p.dram("mix", [128, NCH, NT], BF16, kind="ExternalInput"),
        "svec": p.dram("svec", [128, NCH, 2], F32, kind="ExternalInput"),
        "modw": p.dram("modw", [4, NCH, 128, NCH, 128], F32, kind="ExternalInput"),
        "modb": p.dram("modb", [128, 4, NCH], F32, kind="ExternalInput"),
        "normw": p.dram("normw", [128, NCH], F32, kind="ExternalInput"),
        "wout": p.dram("wout", [NCH, 128, NCH, 128], F32, kind="ExternalInput"),
        "wgu": p.dram("wgu", [NJ, 128, 2, NCH, 128], F32, kind="ExternalInput"),
        "wdn": p.dram("wdn", [NCH, 128, NJ, 128], F32, kind="ExternalInput"),
        "xout": p.dram("xout", [128, NCH, NT], F32, kind="ExternalOutput"),
    }
    if last:
        d["finalw"] = p.dram("finalw", [128, NCH], F32, kind="ExternalInput")
    build_B(p, cx, d, tiles, last)
    p.finish([d["xout"]])
    return nc, p


def fm(v):
    return np.ascontiguousarray(v.reshape(-1, 128).T)


def fm_act(xT):
    F, N = xT.shape
    return np.ascontiguousarray(xT.reshape(F // 128, 128, N).transpose(1, 0, 2))


def blk_lhsT(w, ncols_blk=128):
    K, N = w.shape
    return np.ascontiguousarray(w.reshape(K // 128, 128, N // 128, 128).transpose(2, 1, 0, 3))


def host_B_inputs(l, c_b, c_ctx, mod_w, mod_b, norm_ffn_w, w_out, wg, wu, wd, final_w=None):
    comps = [2, 3, 4, 5]
    modw = np.stack([blk_lhsT(mod_w[l][:, k * D:(k + 1) * D]) for k in comps])
    modb = np.stack([fm(mod_b[l][k * D:(k + 1) * D]) for k in comps], axis=1)
    svec = np.stack([fm(c_b), fm(c_ctx)], axis=2)
    g = blk_lhsT(wg[l])
    u = blk_lhsT(wu[l])
    wgu = np.ascontiguousarray(np.stack([g, u], axis=2))
    out = {
        "svec": np.ascontiguousarray(svec), "modw": modw, "modb": np.ascontiguousarray(modb),
        "normw": fm(norm_ffn_w[l]), "wout": blk_lhsT(w_out), "wgu": wgu, "wdn": blk_lhsT(wd[l]),
    }
    if final_w is not None:
        out["finalw"] = fm(final_w)
    return out


def st_mix(p, d, ch, src_t, src_ap, c0, W, mc):
    if "st_mix" in d:
        d["st_mix"](ch, src_t, src_ap, c0, W, mc)
    else:
        p.dma("pool", d["mixo"][ch, :, c0:c0 + W], src_ap, reads=[src_t], writes=[d["mixo"]])


def load_modA(p, cx, d):
    mod = p.sbuf("modA", [128, 2, NCH, 2], F32)
    compute_mod(p, cx, d["svec"], d["modw"], d["modb"], 2, mod)
    nw = p.sbuf("nwA", [128, NCH], F32)
    p.dma("sp", nw[:], d["normw"][:, :], writes=[nw])
    A1 = p.sbuf("A1", [128, NCH, 2], F32)
    for col in range(2):
        p.op("dve", "scalar_tensor_tensor", out=A1[:, :, col], in0=mod[:, 1, :, col], scalar=1.0, in1=nw[:],
             op0=ALU.add, op1=ALU.mult, reads=[mod, nw], writes=[A1])
    return A1, mod


def gelu_tanh_evac(p, ps, W, out_t, out_ap, t1, t2):
    p.op("act", "activation", out=t1[:, :W], in_=ps[:, :W], func=AF.Square, reads=[ps], writes=[t1])
    p.op("dve", "tensor_scalar", out=t1[:, :W], in0=t1[:, :W], scalar1=0.044715, scalar2=1.0, op0=ALU.mult,
         op1=ALU.add, reads=[t1], writes=[t1])
    p.op("dve", "tensor_tensor", out=t1[:, :W], in0=ps[:, :W], in1=t1[:, :W], op=ALU.mult, reads=[ps, t1],
         writes=[t1])
    p.op("act", "activation", out=t2[:, :W], in_=t1[:, :W], func=AF.Sigmoid, scale=1.5957691216057308,
         reads=[t1], writes=[t2])
    p.op("dve", "tensor_tensor", out=out_ap, in0=ps[:, :W], in1=t2[:, :W], op=ALU.mult, reads=[ps, t2],
         writes=[out_t])


def rope_evac(p, cx, ps, W, qf, rmat, cs_t, cs_ap, sn_ap, t1, t2, out_t, out_ap):
    p.op("act", "activation", out=qf[:, :W], in_=ps[:, :W], func=AF.Copy, reads=[ps], writes=[qf])
    p2 = cx.bank()
    p.op("pe", "matmul", out=p2[:, :W], lhsT=rmat[:], rhs=qf[:, :W], start=True, stop=True, reads=[rmat, qf],
         writes=[p2])
    p.op("dve", "tensor_tensor", out=t1[:, :W], in0=qf[:, :W], in1=cs_ap, op=ALU.mult, reads=[qf, cs_t], writes=[t1])
    p.op("dve", "tensor_tensor", out=t2[:, :W], in0=p2[:, :W], in1=sn_ap, op=ALU.mult, reads=[p2, cs_t], writes=[t2])
    p.op("pool", "tensor_tensor", out=out_ap, in0=t1[:, :W], in1=t2[:, :W], op=ALU.add, reads=[t1, t2],
         writes=[out_t])


def build_A0(p, cx, d, S, CT, lam_init):
    SL = S - CT
    tiles = [(0, CT, 1)] + [(CT + i * 512, 512, 0) for i in range(SL // 512)]
    NBLK = S // 128
    gg_d = p.dram(p.uniq("gg"), [2, 128, S], F32)
    xr_d = p.dram(p.uniq("xr"), [2, 128, S], F32)
    qk_d = p.dram(p.uniq("qk"), [4, 128, S], BF16)
    v_d = p.dram(p.uniq("vt"), [128, NBLK, 256], BF16)
    A1, mod = load_modA(p, cx, d)
    rmat = p.sbuf("rmat", [128, 128], F32)
    p.dma("sp", rmat[:], d["rmat"][:, :], writes=[rmat])
    oneb = p.sbuf("oneb", [128, 1], F32)
    p.op("dve", "memset", ap=oneb[:], constant=1.0, writes=[oneb])
    lqk = p.sbuf("lqk", [128, 4, 64], F32)
    p.dma("sp", lqk[:], d["lqk"][:, :, :], writes=[lqk])
    ltmp = p.sbuf("ltmp", [128, 2, 64], F32)
    lsum = p.sbuf("lsum", [128, 2], F32)
    neg_lam = p.sbuf("neglam", [128, 1], F32)
    for i in range(2):
        p.op("dve", "tensor_tensor", out=ltmp[:, i, :], in0=lqk[:, 2 * i, :], in1=lqk[:, 2 * i + 1, :], op=ALU.mult,
             reads=[lqk], writes=[ltmp])
        p.op("dve", "tensor_reduce", out=lsum[:, i:i + 1], in_=ltmp[:, i, :], axis=AX.X, op=ALU.add, reads=[ltmp],
             writes=[lsum])
    p.op("act", "activation", out=lsum[:], in_=lsum[:], func=AF.Exp, reads=[lsum], writes=[lsum])
    p.op("dve", "tensor_tensor", out=neg_lam[:], in0=lsum[:, 1:2], in1=lsum[:, 0:1], op=ALU.subtract, reads=[lsum],
         writes=[neg_lam])
    p.op("dve", "tensor_scalar", out=neg_lam[:], in0=neg_lam[:], scalar1=-lam_init, scalar2=None, op0=ALU.add,
         reads=[neg_lam], writes=[neg_lam])
    wsub = p.sbuf("wsub", [128, 1], F32)
    p.dma("sp", wsub[:], d["subw"][:, :], writes=[wsub])
    p.op("dve", "tensor_scalar", out=wsub[:], in0=wsub[:], scalar1=1.0 - lam_init, scalar2=None, op0=ALU.mult,
         reads=[wsub], writes=[wsub])
    lba = p.sbuf("lba", [128, 2, 2], F32)
    lbx = p.sbuf("lbx", [128, 2, 2], F32)
    llam = p.sbuf("llam", [128, 2, 2], F32)
    nsl = p.sbuf("nsl", [128, 2, 2], F32)
    cw = p.sbuf("convw", [128, 2, 4], F32)
    cb = p.sbuf("convb", [128, 2], F32)
    p.dma("sp", lba[:], d["lba"][:, :, :], writes=[lba])
    p.dma("sp", lbx[:], d["lbx"][:, :, :], writes=[lbx])
    p.dma("sp", llam[:], d["llam"][:, :, :], writes=[llam])
    p.dma("sp", cw[:], d["convw"][:, :, :], writes=[cw])
    p.dma("sp", cb[:], d["convb"][:, :], writes=[cb])
    ee = p.sbuf("ee", [128, 2, 2], F32)
    pp = p.sbuf("pp", [128, 2, 2], F32)
    p.op("act", "activation", out=ee[:], in_=llam[:], func=AF.Exp, scale=-1.0, reads=[llam], writes=[ee])
    p.op("dve", "tensor_scalar", out=pp[:], in0=ee[:], scalar1=-0.25, scalar2=1.0 / 3.0, op0=ALU.mult, op1=ALU.add,
         reads=[ee], writes=[pp])
    p.op("dve", "tensor_tensor", out=pp[:], in0=pp[:], in1=ee[:], op=ALU.mult, reads=[pp, ee], writes=[pp])
    p.op("dve", "tensor_scalar", out=pp[:], in0=pp[:], scalar1=-0.5, scalar2=None, op0=ALU.add, reads=[pp], writes=[pp])
    p.op("dve", "tensor_tensor", out=pp[:], in0=pp[:], in1=ee[:], op=ALU.mult, reads=[pp, ee], writes=[pp])
    p.op("dve", "tensor_scalar", out=pp[:], in0=pp[:], scalar1=1.0, scalar2=None, op0=ALU.add, reads=[pp], writes=[pp])
    p.op("dve", "tensor_tensor", out=pp[:], in0=pp[:], in1=ee[:], op=ALU.mult, reads=[pp, ee], writes=[pp])
    p.op("dve", "tensor_scalar", out=nsl[:], in0=pp[:], scalar1=-8.0, scalar2=None, op0=ALU.mult, reads=[pp],
         writes=[nsl])
    wg = p.sbuf("lruw", [128, 2, 2, 2, 128], BF16)
    p.dma("pool", wg[:], d["lruw"][:, :, :, :, :], writes=[wg])

    p.open_scope()
    win = p.sbuf("win", [128, NCH, 1280], BF16)
    for c in range(0, NCH, 4):
        p.dma("pool", win[:, c:c + 4, :], d["win"][:, c:c + 4, :], writes=[win])
    xt = p.sbuf("xtA", [128, NCH, 512], F32)
    sq_t = p.sbuf("sqA", [128, NCH, 512], BF16)
    h_t = p.sbuf("hA", [128, NCH, 512], BF16)
    rstd_t = p.sbuf("rstdA", [128, 512], F32)
    tmp_t = [p.sbuf("tmpA%d" % i, [128, 512], F32) for i in range(2)]
    cs_t = p.sbuf("csA", [128, 2, 512], F32)
    stg_f = p.sbuf("stgf", [128, 4, 512], F32)
    stg_b = p.sbuf("stgb", [128, 4, 512], BF16)
    stg_v = p.sbuf("stgv", [128, 4, 256], BF16)
    qf = [p.sbuf("qf%d" % i, [128, 512], F32) for i in range(2)]
    t1 = [p.sbuf("t1A%d" % i, [128, 512], F32) for i in range(2)]
    t2 = [p.sbuf("t2A%d" % i, [128, 512], F32) for i in range(2)]
    for ti, (c0, W, mc) in enumerate(tiles):
        lat = (mc == 0)
        if "ld_xa" in d:
            d["ld_xa"](xt, c0, W, mc)
        else:
            p.dma("sp", xt[:, :, :W], d["xa"][:, :, c0:c0 + W], reads=[d["xa"]], writes=[xt])
        if lat:
            p.dma("sp", cs_t[:, 0, :W], d["cos"][:, c0 - CT:c0 - CT + W], writes=[cs_t])
            p.dma("sp", cs_t[:, 1, :W], d["sin"][:, c0 - CT:c0 - CT + W], writes=[cs_t])
        rms_stats(p, cx, xt, xt[:, :, :W], W, sq_t, rstd_t)
        norm_mod(p, cx, xt, lambda c: xt[:, c, :W], W, rstd_t, tmp_t,
                 A1, lambda c: A1[:, c, mc:mc + 1], mod, lambda c: mod[:, 0, c, mc:mc + 1],
                 h_t, lambda c: h_t[:, c, :W])
        for oc in range(8):
            ps = cx.bank()
            for c in range(NCH):
                mm(p, ps, ps[:, :W], win, win[:, c, oc * 128:(oc + 1) * 128], h_t, h_t[:, c, :W], c == 0, c == NCH - 1)
            if oc < 2:
                gelu_tanh_evac(p, ps, W, stg_f, stg_f[:, oc, :W], t1[oc % 2], t2[oc % 2])
            elif oc < 4:
                p.op("act", "activation", out=stg_f[:, oc, :W], in_=ps[:, :W], func=AF.Copy, reads=[ps],
                     writes=[stg_f])
            elif lat:
                rope_evac(p, cx, ps, W, qf[oc % 2], rmat, cs_t, cs_t[:, 0, :W], cs_t[:, 1, :W], t1[oc % 2],
                          t2[oc % 2], stg_b, stg_b[:, oc - 4, :W])
            else:
                p.op("act", "activation", out=stg_b[:, oc - 4, :W], in_=ps[:, :W], func=AF.Copy, reads=[ps],
                     writes=[stg_b])
        for tb in range(W // 128):
            ps = cx.bank()
            for c in range(NCH):
                mm(p, ps, ps[:, :256], h_t, h_t[:, c, tb * 128:(tb + 1) * 128], win, win[:, c, 1024:1280],
                   c == 0, c == NCH - 1)
            p.op("act", "activation", out=stg_v[:, tb, :], in_=ps[:, :256], func=AF.Copy, reads=[ps], writes=[stg_v])
        p.dma("pool", gg_d[:, :, c0:c0 + W].rearrange("a p w -> p a w"), stg_f[:, 0:2, :W], reads=[stg_f],
              writes=[gg_d])
        p.dma("pool", xr_d[:, :, c0:c0 + W].rearrange("a p w -> p a w"), stg_f[:, 2:4, :W], reads=[stg_f],
              writes=[xr_d])
        p.dma("pool", qk_d[:, :, c0:c0 + W].rearrange("a p w -> p a w"), stg_b[:, :, :W], reads=[stg_b],
              writes=[qk_d])
        p.dma("pool", v_d[:, c0 // 128:(c0 + W) // 128, :], stg_v[:, :W // 128, :], reads=[stg_v], writes=[v_d])
    p.close_scope()

    p.open_scope()
    KT = p.sbuf("KT", [128, S], BF16)
    V = p.sbuf("Vtok", [128, NBLK, 128], BF16)
    QT = [p.sbuf("QT%d" % i, [128, 512], BF16) for i in range(2)]
    E = [p.sbuf("E%d" % i, [128, 512], BF16) for i in range(4)]
    rz = [p.sbuf("rz%d" % i, [128, 512], F32) for i in range(2)]
    oo = [p.sbuf("oo%d" % i, [128, 512], F32) for i in range(2)]
    dd = p.sbuf("dd", [128, 512], F32)
    dsq = p.sbuf("dsq", [128, 512], BF16)
    drs = p.sbuf("drs", [128, 512], F32)
    dout = [p.sbuf("dout%d" % i, [128, 512], BF16) for i in range(2)]
    srot = [cx.banks[4], cx.banks[5], cx.banks[6], cx.pmisc]
    si = 0
    ei = 0
    qi = 0
    for hh in range(2):
        p.dma("sp", KT[:], qk_d[2 + hh], reads=[qk_d], writes=[KT])
        for b0 in range(0, NBLK, 32):
            b1 = min(NBLK, b0 + 32)
            p.dma("sp", V[:, b0:b1, :], v_d[:, b0:b1, hh * 128:(hh + 1) * 128], reads=[v_d], writes=[V])
        for ti, (c0, W, mc) in enumerate(tiles):
            kblocks = list(range(CT // 128)) if mc == 1 else list(range(NBLK))
            Q = QT[qi % 2]
            qi += 1
            p.dma("sp", Q[:, :W], qk_d[hh, :, c0:c0 + W], reads=[qk_d], writes=[Q])
            for m in range(2):
                O = cx.banks[2 * m]
                Z = cx.banks[2 * m + 1]
                pend = []
                for ki, kb in enumerate(kblocks):
                    sp_ = srot[si % 4]
                    si += 1
                    mm(p, sp_, sp_[:, :W], KT, KT[64 * m:64 * m + 64, kb * 128:(kb + 1) * 128], Q,
                       Q[64 * m:64 * m + 64, :W], True, True)
                    Eb = E[ei % 4]
                    ei += 1
                    p.op("act", "activation", out=Eb[:, :W], in_=sp_[:, :W], func=AF.Exp, scale=0.125, reads=[sp_],
                         writes=[Eb])
                    pend.append((kb, Eb, ki == 0, ki == len(kblocks) - 1))
                    while len(pend) > (0 if ki == len(kblocks) - 1 else LAG):
                        kb_, Eb_, first, lastk = pend.pop(0)
                        mm(p, O, O[:, :W], V, V[:, kb_, :], Eb_, Eb_[:, :W], first, lastk)
                        mm(p, Z, Z[:, :W], cx.ones_bf, cx.ones_bf[:], Eb_, Eb_[:, :W], first, lastk)
                p.op("dve", "reciprocal", out=rz[m][:, :W], in_=Z[:, :W], reads=[Z], writes=[rz[m]])
                p.op("dve", "tensor_tensor", out=oo[m][:, :W], in0=O[:, :W], in1=rz[m][:, :W], op=ALU.mult,
                     reads=[O, rz[m]], writes=[oo[m]])
            p.op("dve", "scalar_tensor_tensor", out=dd[:, :W], in0=oo[1][:, :W], scalar=neg_lam[:], in1=oo[0][:, :W],
                 op0=ALU.mult, op1=ALU.add, reads=[oo[0], oo[1], neg_lam], writes=[dd])
            p.op("act", "activation", out=dsq[:, :W], in_=dd[:, :W], func=AF.Square, reads=[dd], writes=[dsq])
            ps = cx.banks[2]
            mm(p, ps, ps[:, :W], cx.ones_bf, cx.ones_bf[:], dsq, dsq[:, :W], True, True)
            p.op("act", "activation", out=drs[:, :W], in_=ps[:, :W], func=AF.Sqrt, scale=1.0 / 128, bias=cx.epsb[:],
                 reads=[ps, cx.epsb], writes=[drs])
            p.op("dve", "reciprocal", out=drs[:, :W], in_=drs[:, :W], reads=[drs], writes=[drs])
            do = dout[ti % 2]
            p.op("dve", "scalar_tensor_tensor", out=do[:, :W], in0=dd[:, :W], scalar=wsub[:], in1=drs[:, :W],
                 op0=ALU.mult, op1=ALU.mult, reads=[dd, wsub, drs], writes=[do])
            st_mix(p, d, 2 + hh, do, do[:, :W], c0, W, mc)
    p.close_scope()

    for bi in range(2):
        p.open_scope()
        xc = p.sbuf("xc", [128, S], F32)
        xcb = p.sbuf("xcb", [128, S], BF16)
        p.open_scope()
        xp = p.sbuf("xp", [128, S + 6], F32)
        p.op("pool", "memset", ap=xp[:], constant=0.0, writes=[xp])
        p.dma("sp", xp[:, 1:1 + CT], xr_d[bi, :, 0:CT], reads=[xr_d], writes=[xp])
        p.dma("sp", xp[:, CT + 4:CT + 4 + SL], xr_d[bi, :, CT:S], reads=[xr_d], writes=[xp])
        for (o0, po, n) in ((0, 1, CT), (CT, CT + 4, SL)):
            for n0 in range(0, n, 4096):
                n1 = min(n, n0 + 4096)
                L = n1 - n0
                p.op("dve", "tensor_scalar", out=xc[:, o0 + n0:o0 + n1], in0=xp[:, po + n0 - 1:po + n0 - 1 + L],
                     scalar1=cw[:, bi, 0:1], scalar2=cb[:, bi:bi + 1], op0=ALU.mult, op1=ALU.add,
                     reads=[xp, cw, cb], writes=[xc])
                for k in range(1, 4):
                    p.op("dve", "scalar_tensor_tensor", out=xc[:, o0 + n0:o0 + n1],
                         in0=xp[:, po + n0 - 1 + k:po + n0 - 1 + k + L], scalar=cw[:, bi, k:k + 1],
                         in1=xc[:, o0 + n0:o0 + n1], op0=ALU.mult, op1=ALU.add, reads=[xp, cw, xc], writes=[xc])
        p.close_scope()
        for n0 in range(0, S, 4096):
            n1 = min(S, n0 + 4096)
            p.op("act", "activation", out=xcb[:, n0:n1], in_=xc[:, n0:n1], func=AF.Copy, reads=[xc], writes=[xcb])
        p.open_scope()
        racc = p.sbuf("racc", [128, S], F32)
        gr = p.sbuf("gr", [128, 512], F32)
        ga = p.sbuf("ga", [128, 512], F32)
        gi = p.sbuf("gi", [128, 512], F32)
        gs = p.sbuf("gs", [128, 512], F32)
        gb = p.sbuf("gb", [128, 512], F32)
        hb = [p.sbuf("hb%d" % i, [128, 512], F32) for i in range(2)]
        ggt = p.sbuf("ggt", [128, 512], F32)
        hsum = p.sbuf("hsum", [128, 512], F32)
        ao = [p.sbuf("ao%d" % i, [128, 512], BF16) for i in range(2)]
        for dr in range(2):
            order = list(range(len(tiles)))
            if dr == 1:
                order = [0] + order[:0:-1]
            prev = None
            for n, ti in enumerate(order):
                c0, W, mc = tiles[ti]
                pa = cx.bank()
                px = cx.bank()
                mm(p, pa, pa[:, :W], wg, wg[:, 0, dr, bi, :], xcb, xcb[:, c0:c0 + W], True, True)
                mm(p, px, px[:, :W], wg, wg[:, 1, dr, bi, :], xcb, xcb[:, c0:c0 + W], True, True)
                p.op("act", "activation", out=gr[:, :W], in_=pa[:, :W], func=AF.Sigmoid, bias=lba[:, dr, bi:bi + 1],
                     reads=[pa, lba], writes=[gr])
                p.op("act", "activation", out=gi[:, :W], in_=px[:, :W], func=AF.Sigmoid, bias=lbx[:, dr, bi:bi + 1],
                     reads=[px, lbx], writes=[gi])
                p.op("act", "activation", out=ga[:, :W], in_=gr[:, :W], func=AF.Exp, scale=nsl[:, dr, bi:bi + 1],
                     reads=[gr, nsl], writes=[ga])
                p.op("dve", "tensor_scalar", out=ga[:, :W], in0=ga[:, :W], scalar1=1.0, scalar2=None, op0=ALU.min,
                     reads=[ga], writes=[ga])
                p.op("dve", "tensor_tensor", out=gs[:, :W], in0=ga[:, :W], in1=ga[:, :W], op=ALU.mult, reads=[ga],
                     writes=[gs])
                p.op("act", "activation", out=gs[:, :W], in_=gs[:, :W], func=AF.Sqrt, scale=-1.0, bias=oneb[:],
                     reads=[gs, oneb], writes=[gs])
                p.op("dve", "tensor_tensor", out=gb[:, :W], in0=gs[:, :W], in1=gi[:, :W], op=ALU.mult, reads=[gs, gi],
                     writes=[gb])
                p.op("dve", "tensor_tensor", out=gb[:, :W], in0=gb[:, :W], in1=xc[:, c0:c0 + W], op=ALU.mult,
                     reads=[gb, xc], writes=[gb])
                if dr == 0:
                    init = 0.0 if n == 0 else racc[:, c0 - 1:c0]
                    p.op("dve", "tensor_tensor_scan", out=racc[:, c0:c0 + W], data0=ga[:, :W], data1=gb[:, :W],
                         initial=init, op0=ALU.mult, op1=ALU.add, reads=[ga, gb, racc], writes=[racc])
                else:
                    hcur = hb[n % 2]
                    if n == 0:
                        init = 0.0
                        rd = [ga, gb]
                    else:
                        init = prev[0][:, 0:1]
                        rd = [ga, gb, prev[0]]
                    p.op("dve", "tensor_tensor_scan", out=hcur[:, 0:W][:, ::-1],
                         data0=ga[:, 0:W][:, ::-1], data1=gb[:, 0:W][:, ::-1], initial=init, op0=ALU.mult,
                         op1=ALU.add, reads=rd, writes=[hcur])
                    prev = (hcur, W)
                    p.dma("sp", ggt[:, :W], gg_d[bi, :, c0:c0 + W], reads=[gg_d], writes=[ggt])
                    a_o = ao[n % 2]
                    p.op("pool", "tensor_tensor", out=hsum[:, :W], in0=hcur[:, :W], in1=racc[:, c0:c0 + W], op=ALU.add,
                         reads=[hcur, racc], writes=[hsum])
                    p.op("pool", "tensor_tensor", out=a_o[:, :W], in0=hsum[:, :W], in1=ggt[:, :W], op=ALU.mult,
                         reads=[hsum, ggt], writes=[a_o])
                    st_mix(p, d, bi, a_o, a_o[:, :W], c0, W, mc)
        p.close_scope()
        p.close_scope()


def a0_dram(p, S, SL, pre="", fused=False):
    dd = {
        "xa": p.dram(pre + "xa", [128, NCH, S], F32, kind="ExternalInput"),
        "svec": p.dram(pre + "svec", [128, NCH, 2], F32, kind="ExternalInput"),
        "modw": p.dram(pre + "modw", [2, NCH, 128, NCH, 128], F32, kind="ExternalInput"),
        "modb": p.dram(pre + "modb", [128, 2, NCH], F32, kind="ExternalInput"),
        "normw": p.dram(pre + "normw", [128, NCH], F32, kind="ExternalInput"),
        "win": p.dram(pre + "win", [128, NCH, 1280], F32, kind="ExternalInput"),
        "rmat": p.dram(pre + "rmat", [128, 128], F32, kind="ExternalInput"),
        "lqk": p.dram(pre + "lqk", [128, 4, 64], F32, kind="ExternalInput"),
        "subw": p.dram(pre + "subw", [128, 1], F32, kind="ExternalInput"),
        "lba": p.dram(pre + "lba", [128, 2, 2], F32, kind="ExternalInput"),
        "lbx": p.dram(pre + "lbx", [128, 2, 2], F32, kind="ExternalInput"),
        "llam": p.dram(pre + "llam", [128, 2, 2], F32, kind="ExternalInput"),
        "convw": p.dram(pre + "convw", [128, 2, 4], F32, kind="ExternalInput"),
        "convb": p.dram(pre + "convb", [128, 2], F32, kind="ExternalInput"),
        "lruw": p.dram(pre + "lruw", [128, 2, 2, 2, 128], F32, kind="ExternalInput"),
        "cos": p.dram(pre + "cos", [128, SL], F32, kind="ExternalInput"),
        "sin": p.dram(pre + "sin", [128, SL], F32, kind="ExternalInput"),
    }
    dd["mixo"] = p.dram(pre + "mixo", [4, 128, S], BF16, kind="Internal" if fused else "ExternalOutput")
    return dd


def make_A0(S, CT):
    nc = bass.Bass("TRN2", target_bir_lowering=False)
    p = Prog(nc)
    cx = Ctx(p)
    d = a0_dram(p, S, S - CT)
    build_A0(p, cx, d, S, CT, 0.8 - 0.6 * math.exp(0.0))
    p.finish([d["mixo"]])
    return nc, p


def rope_tables(SL, head_dim, grid_w=64, theta=10000.0):
    rows = SL // grid_w
    row = np.repeat(np.arange(rows, dtype=np.float32), grid_w)
    col = np.tile(np.arange(grid_w, dtype=np.float32), rows)
    n_freq = head_dim // 4
    inv = (np.float32(theta) ** (-np.arange(n_freq, dtype=np.float32) / np.float32(n_freq))).astype(np.float32)
    ang = np.concatenate([row[:, None] * inv, col[:, None] * inv], axis=-1).astype(np.float32)
    cos = np.cos(ang).astype(np.float32).T
    sin = np.sin(ang).astype(np.float32).T
    reps = 128 // (head_dim // 2)
    return np.ascontiguousarray(np.tile(cos, (reps, 1))), np.ascontiguousarray(np.tile(sin, (reps, 1)))


def rot_matrix(head_dim):
    R = np.zeros((128, 128), np.float32)
    hh = head_dim // 2
    for m in range(128):
        if (m % head_dim) < hh:
            R[m + hh, m] = -1.0
        else:
            R[m - hh, m] = 1.0
    return R


def host_A0_inputs(j, c_b, c_ctx, mod_w, mod_b, norm_mix_w, ab_w_in, conv_w, conv_b, wa, ba, wx, bx, lam,
                   lq1, lk1, lq2, lk2, subw, SL):
    l = 0
    comps = [0, 1]
    modw = np.stack([blk_lhsT(mod_w[l][:, k * D:(k + 1) * D]) for k in comps])
    modb = np.stack([fm(mod_b[l][k * D:(k + 1) * D]) for k in comps], axis=1)
    svec = np.stack([fm(c_b), fm(c_ctx)], axis=2)
    W = ab_w_in[0]
    cols = np.concatenate([np.arange(256 * j, 256 * j + 256), 1024 + np.arange(256 * j, 256 * j + 256),
                           2048 + np.arange(256 * j, 256 * j + 256), 3072 + np.arange(256 * j, 256 * j + 256),
                           4096 + np.arange(256 * j, 256 * j + 256)])
    win = np.ascontiguousarray(W[:, cols].reshape(NCH, 128, 1280).transpose(1, 0, 2))
    blks = [2 * j, 2 * j + 1]

    def pvec(v):
        return np.ascontiguousarray(np.stack([v[:, b * 128:(b + 1) * 128] for b in blks], axis=2).transpose(1, 0, 2))
    lruw = np.stack([np.stack([np.stack([w[0][dr, b] for b in blks], 0) for dr in range(2)], 0) for w in (wa, wx)], 0)
    lruw = np.ascontiguousarray(lruw.transpose(3, 0, 1, 2, 4))
    cos, sin = rope_tables(SL, 64)
    return {
        "svec": np.ascontiguousarray(svec), "modw": modw, "modb": np.ascontiguousarray(modb),
        "normw": fm(norm_mix_w[l]), "win": win, "rmat": rot_matrix(64),
        "lqk": np.ascontiguousarray(np.broadcast_to(np.stack([lq1[0], lk1[0], lq2[0], lk2[0]])[None], (128, 4, 64))),
        "subw": np.ascontiguousarray(subw[0].reshape(128, 1)),
        "lba": pvec(ba[0]), "lbx": pvec(bx[0]), "llam": pvec(lam[0]),
        "convw": np.ascontiguousarray(np.stack([conv_w[0][:, b * 128:(b + 1) * 128] for b in blks], 0).transpose(2, 0, 1)),
        "convb": np.ascontiguousarray(np.stack([conv_b[0][b * 128:(b + 1) * 128] for b in blks], 1)),
        "lruw": lruw, "cos": cos, "sin": sin,
    }


def build_A1(p, cx, d, S, CT, stage=9):
    SL = S - CT
    tiles = [(0, CT, 1)] + [(CT + i * 512, 512, 0) for i in range(SL // 512)]
    NBLK = S // 128
    NF = 1408
    qh_d = p.dram(p.uniq("qh"), [2, 128, S], F32)
    f_d = p.dram(p.uniq("fd"), [2, 2, 128, S], F32)
    sg_d = p.dram(p.uniq("sg"), [2, 128, S], F32)
    qk_d = p.dram(p.uniq("qk1"), [3, 128, S], BF16)
    vt_d = p.dram(p.uniq("vt1"), [128, NBLK, 384], BF16)
    o_d = p.dram(p.uniq("od"), [2, 128, S], F32)

    A1, mod = load_modA(p, cx, d)
    rmat = p.sbuf("rmat", [128, 128], F32)
    p.dma("sp", rmat[:], d["rmat"][:, :], writes=[rmat])
    ident = p.sbuf("ident", [128, 128], BF16)
    p.dma("pool", ident[:], d["ident"][:, :], writes=[ident])
    nws = p.sbuf("nws", [128, 3], F32)
    p.dma("sp", nws[:], d["nws"][:, :], writes=[nws])
    lbl = p.sbuf("lbl", [128, 2, 2, 2], F32)
    p.dma("sp", lbl[:], d["lbl"][:, :, :, :], writes=[lbl])
    lb = p.sbuf("lb", [128, 2, 2], F32)
    oml = p.sbuf("oml", [128, 2, 2], F32)
    p.op("dve", "tensor_tensor", out=lb[:], in0=lbl[:, :, 1, :], in1=lbl[:, :, 0, :], op=ALU.subtract, reads=[lbl],
         writes=[lb])
    p.op("act", "activation", out=lb[:], in_=lb[:], func=AF.Sigmoid, reads=[lb], writes=[lb])
    p.op("dve", "tensor_scalar", out=oml[:], in0=lb[:], scalar1=-1.0, scalar2=1.0, op0=ALU.mult, op1=ALU.add,
         reads=[lb], writes=[oml])
    maskF = p.sbuf("maskF", [128, 512], F32)
    maskB = p.sbuf("maskB", [128, 512], F32)
    p.op("pool", "memset", ap=maskF[:], constant=1.0, writes=[maskF])
    p.op("pool", "memset", ap=maskB[:], constant=1.0, writes=[maskB])
    for ci in range(8):
        p.op("pool", "memset", ap=maskF[:, ci * 64:ci * 64 + 1], constant=0.0, writes=[maskF])
        p.op("pool", "memset", ap=maskB[:, ci * 64 + 63:ci * 64 + 64], constant=0.0, writes=[maskB])
    tri = p.sbuf("tri", [64, 2, 64], F32)
    p.dma("sp", tri[:], d["tri"][:, :, :], writes=[tri])

    p.open_scope()
    win = p.sbuf("win1", [128, NCH, 1792], BF16)
    for c in range(0, NCH, 4):
        p.dma("pool", win[:, c:c + 4, :], d["win"][:, c:c + 4, :], writes=[win])
    xt = p.sbuf("xtA", [128, NCH, 512], F32)
    sq_t = p.sbuf("sqA", [128, NCH, 512], BF16)
    h_t = p.sbuf("hA", [128, NCH, 512], BF16)
    rstd_t = p.sbuf("rstdA", [128, 512], F32)
    tmp_t = [p.sbuf("tmpA%d" % i, [128, 512], F32) for i in range(2)]
    cs_t = p.sbuf("csA", [128, 2, 512], F32)
    stg_q = p.sbuf("stgq", [128, 2, 512], F32)
    stg_f = p.sbuf("stgf1", [128, 4, 512], F32)
    stg_g = p.sbuf("stgg", [128, 2, 512], F32)
    stg_b = p.sbuf("stgb1", [128, 3, 512], BF16)
    stg_v = p.sbuf("stgv1", [128, 4, 384], BF16)
    qf = [p.sbuf("qf%d" % i, [128, 512], F32) for i in range(2)]
    qn = [p.sbuf("qn%d" % i, [128, 512], F32) for i in range(2)]
    qsq = [p.sbuf("qsq%d" % i, [128, 512], BF16) for i in range(2)]
    qrs = [p.sbuf("qrs%d" % i, [128, 512], F32) for i in range(2)]
    t1 = [p.sbuf("t1A%d" % i, [128, 512], F32) for i in range(2)]
    t2 = [p.sbuf("t2A%d" % i, [128, 512], F32) for i in range(2)]
    for ti, (c0, W, mc) in enumerate(tiles):
        lat = (mc == 0)
        if "ld_xa" in d:
            d["ld_xa"](xt, c0, W, mc)
        else:
            p.dma("sp", xt[:, :, :W], d["xa"][:, :, c0:c0 + W], reads=[d["xa"]], writes=[xt])
        if lat:
            p.dma("sp", cs_t[:, 0, :W], d["cos"][:, c0 - CT:c0 - CT + W], writes=[cs_t])
            p.dma("sp", cs_t[:, 1, :W], d["sin"][:, c0 - CT:c0 - CT + W], writes=[cs_t])
        rms_stats(p, cx, xt, xt[:, :, :W], W, sq_t, rstd_t)
        norm_mod(p, cx, xt, lambda c: xt[:, c, :W], W, rstd_t, tmp_t,
                 A1, lambda c: A1[:, c, mc:mc + 1], mod, lambda c: mod[:, 0, c, mc:mc + 1],
                 h_t, lambda c: h_t[:, c, :W])
        for oc in range(11):
            if oc in (8, 9) and not lat:
                continue
            ps = cx.bank()
            for c in range(NCH):
                mm(p, ps, ps[:, :W], win, win[:, c, oc * 128:(oc + 1) * 128], h_t, h_t[:, c, :W], c == 0, c == NCH - 1)
            if oc < 2:
                p.op("act", "activation", out=stg_q[:, oc, :W], in_=ps[:, :W], func=AF.Silu, reads=[ps], writes=[stg_q])
            elif oc < 6:
                dr, hh = (oc - 2) // 2, (oc - 2) % 2
                tt = t1[oc % 2]
                p.op("act", "activation", out=tt[:, :W], in_=ps[:, :W], func=AF.Sigmoid, reads=[ps], writes=[tt])
                p.op("dve", "tensor_scalar", out=stg_f[:, oc - 2, :W], in0=tt[:, :W], scalar1=oml[:, dr, hh:hh + 1],
                     scalar2=lb[:, dr, hh:hh + 1], op0=ALU.mult, op1=ALU.add, reads=[tt, oml, lb], writes=[stg_f])
            elif oc < 8:
                p.op("act", "activation", out=stg_g[:, oc - 6, :W], in_=ps[:, :W], func=AF.Silu, reads=[ps],
                     writes=[stg_g])
            else:
                k = oc % 2
                wcol = 1 if oc < 10 else 2
                p.op("act", "activation", out=qf[k][:, :W], in_=ps[:, :W], func=AF.Copy, reads=[ps], writes=[qf[k]])
                p.op("act", "activation", out=qsq[k][:, :W], in_=ps[:, :W], func=AF.Square, reads=[ps], writes=[qsq[k]])
                p2 = cx.bank()
                mm(p, p2, p2[:, :W], cx.ones_bf, cx.ones_bf[:], qsq[k], qsq[k][:, :W], True, True)
                p.op("act", "activation", out=qrs[k][:, :W], in_=p2[:, :W], func=AF.Sqrt, scale=1.0 / 128,
                     bias=cx.epsb[:], reads=[p2, cx.epsb], writes=[qrs[k]])
                p.op("dve", "reciprocal", out=qrs[k][:, :W], in_=qrs[k][:, :W], reads=[qrs[k]], writes=[qrs[k]])
                if lat:
                    p.op("dve", "scalar_tensor_tensor", out=qn[k][:, :W], in0=qf[k][:, :W], scalar=nws[:, wcol:wcol + 1],
                         in1=qrs[k][:, :W], op0=ALU.mult, op1=ALU.mult, reads=[qf[k], nws, qrs[k]], writes=[qn[k]])
                    p3 = cx.bank()
                    p.op("pe", "matmul", out=p3[:, :W], lhsT=rmat[:], rhs=qn[k][:, :W], start=True, stop=True,
                         reads=[rmat, qn[k]], writes=[p3])
                    p.op("dve", "tensor_tensor", out=t1[k][:, :W], in0=qn[k][:, :W], in1=cs_t[:, 0, :W], op=ALU.mult,
                         reads=[qn[k], cs_t], writes=[t1[k]])
                    p.op("dve", "tensor_tensor", out=t2[k][:, :W], in0=p3[:, :W], in1=cs_t[:, 1, :W], op=ALU.mult,
                         reads=[p3, cs_t], writes=[t2[k]])
                    p.op("pool", "tensor_tensor", out=stg_b[:, oc - 8, :W], in0=t1[k][:, :W], in1=t2[k][:, :W],
                         op=ALU.add, reads=[t1[k], t2[k]], writes=[stg_b])
                else:
                    p.op("dve", "scalar_tensor_tensor", out=stg_b[:, oc - 8, :W], in0=qf[k][:, :W],
                         scalar=nws[:, wcol:wcol + 1], in1=qrs[k][:, :W], op0=ALU.mult, op1=ALU.mult,
                         reads=[qf[k], nws, qrs[k]], writes=[stg_b])
        for tb in range(W // 128):
            ps = cx.bank()
            for c in range(NCH):
                mm(p, ps, ps[:, :384], h_t, h_t[:, c, tb * 128:(tb + 1) * 128], win, win[:, c, NF:NF + 384],
                   c == 0, c == NCH - 1)
            p.op("act", "activation", out=stg_v[:, tb, :], in_=ps[:, :384], func=AF.Copy, reads=[ps], writes=[stg_v])
        p.dma("pool", qh_d[:, :, c0:c0 + W].rearrange("a p w -> p a w"), stg_q[:, :, :W], reads=[stg_q], writes=[qh_d])
        p.dma("pool", f_d[:, :, :, c0:c0 + W].rearrange("a b p w -> p (a b) w"), stg_f[:, :, :W], reads=[stg_f],
              writes=[f_d])
        p.dma("pool", sg_d[:, :, c0:c0 + W].rearrange("a p w -> p a w"), stg_g[:, :, :W], reads=[stg_g], writes=[sg_d])
        if lat:
            p.dma("pool", qk_d[:, :, c0:c0 + W].rearrange("a p w -> p a w"), stg_b[:, :, :W], reads=[stg_b],
                  writes=[qk_d])
        else:
            p.dma("pool", qk_d[2, :, c0:c0 + W], stg_b[:, 2, :W], reads=[stg_b], writes=[qk_d])
        p.dma("pool", vt_d[:, c0 // 128:(c0 + W) // 128, :], stg_v[:, :W // 128, :], reads=[stg_v], writes=[vt_d])
    p.close_scope()

    if stage < 2:
        return
    p.open_scope()
    KT = p.sbuf("KT1", [128, S], BF16)
    V = p.sbuf("Vtok1", [128, NBLK, 128], BF16)
    QT = [p.sbuf("QT1%d" % i, [128, 512], BF16) for i in range(2)]
    E = [p.sbuf("E1%d" % i, [128, 512], BF16) for i in range(4)]
    rz = p.sbuf("rz1", [128, 512], F32)
    ao = [p.sbuf("ao1%d" % i, [128, 512], BF16) for i in range(2)]
    srot = [cx.banks[4], cx.banks[5], cx.banks[6], cx.pmisc]
    si = ei = qi = 0
    p.dma("sp", KT[:], qk_d[2], reads=[qk_d], writes=[KT])
    for b0 in range(0, NBLK, 32):
        b1 = min(NBLK, b0 + 32)
        p.dma("sp", V[:, b0:b1, :], vt_d[:, b0:b1, 256:384], reads=[vt_d], writes=[V])
    sc = 128 ** -0.5
    for hh in range(2):
        for ti, (c0, W, mc) in enumerate(tiles):
            if mc == 1:
                continue
            Q = QT[qi % 2]
            O = cx.banks[2 * (qi % 2)]
            Z = cx.banks[2 * (qi % 2) + 1]
            qi += 1
            p.dma("sp", Q[:, :W], qk_d[hh, :, c0:c0 + W], reads=[qk_d], writes=[Q])
            pend = []
            for kb in range(NBLK):
                sp_ = srot[si % 4]
                si += 1
                mm(p, sp_, sp_[:, :W], KT, KT[:, kb * 128:(kb + 1) * 128], Q, Q[:, :W], True, True)
                Eb = E[ei % 4]
                ei += 1
                p.op("act", "activation", out=Eb[:, :W], in_=sp_[:, :W], func=AF.Exp, scale=sc, reads=[sp_],
                     writes=[Eb])
                pend.append((kb, Eb))
                while len(pend) > (0 if kb == NBLK - 1 else LAG):
                    kb_, Eb_ = pend.pop(0)
                    mm(p, O, O[:, :W], V, V[:, kb_, :], Eb_, Eb_[:, :W], kb_ == 0, kb_ == NBLK - 1)
                    mm(p, Z, Z[:, :W], cx.ones_bf, cx.ones_bf[:], Eb_, Eb_[:, :W], kb_ == 0, kb_ == NBLK - 1)
            p.op("dve", "reciprocal", out=rz[:, :W], in_=Z[:, :W], reads=[Z], writes=[rz])
            a_o = ao[qi % 2]
            p.op("dve", "tensor_tensor", out=a_o[:, :W], in0=O[:, :W], in1=rz[:, :W], op=ALU.mult, reads=[O, rz],
                 writes=[a_o])
            st_mix(p, d, 2 + hh, a_o, a_o[:, :W], c0, W, mc)
    p.close_scope()

    if stage < 3:
        return
    p.open_scope()
    ft = p.sbuf("ft", [128, 512], F32)
    qt = p.sbuf("qt", [128, 512], F32)
    lf = p.sbuf("lf", [128, 512], F32)
    lfm = p.sbuf("lfm", [128, 514], F32)
    kk = p.sbuf("kk", [128, 512], F32)
    bb = p.sbuf("bb", [128, 512], F32)
    cc = p.sbuf("cc", [128, 512], F32)
    eb = p.sbuf("eb", [128, 512], F32)
    enb = p.sbuf("enb", [128, 512], F32)
    dd_ = p.sbuf("ddh", [128, 512], F32)
    ed = p.sbuf("ed", [128, 512], F32)
    ec = p.sbuf("ec", [128, 512], F32)
    qe = p.sbuf("qe", [128, 512], BF16)
    ke = p.sbuf("ke", [128, 512], BF16)
    kd = p.sbuf("kd", [128, 512], BF16)
    kdT = p.sbuf("kdT", [64, 8, 128], BF16)
    Vc = p.sbuf("Vc", [64, 8, 128], BF16)
    scm = [[p.sbuf("scm%d_%d" % (dr_, i), [64, 64], BF16) for i in range(2)] for dr_ in range(2)]
    for dr_ in range(2):
        for i in range(2):
            p.op("pool", "memset", ap=scm[dr_][i][:], constant=0.0, writes=[scm[dr_][i]])
    S32 = p.sbuf("S32", [128, 128], F32)
    Sbf = p.sbuf("Sbf", [128, 128], BF16)
    ot = p.sbuf("ot", [128, 512], F32)
    of = p.sbuf("of", [128, 512], F32)
    osq = p.sbuf("osq", [128, 512], BF16)
    ors = p.sbuf("ors", [128, 512], F32)
    sgt = p.sbuf("sgt", [128, 512], F32)
    oy = p.sbuf("oy", [128, 512], F32)
    ob = [p.sbuf("ob%d" % i, [128, 512], BF16) for i in range(2)]
    p.op("pool", "memset", ap=lfm[:], constant=0.0, writes=[lfm])
    for hh in range(2):
        for dr in range(2):
            order = list(range(len(tiles)))
            if dr == 1:
                order = [0] + order[:0:-1]
            p.op("dve", "memset", ap=S32[:], constant=0.0, writes=[S32])
            p.op("dve", "memset", ap=Sbf[:], constant=0.0, writes=[Sbf])
            mF, mB = (maskF, maskB) if dr == 0 else (maskB, maskF)
            for n, ti in enumerate(order):
                c0, W, mc = tiles[ti]
                nch = W // 64
                p.dma("sp", ft[:, :W], f_d[dr, hh, :, c0:c0 + W], reads=[f_d], writes=[ft])
                p.dma("sp", qt[:, :W], qh_d[hh, :, c0:c0 + W], reads=[qh_d], writes=[qt])
                for half in range(2):
                    p.dma("sp", Vc[:, half:nch:2, :], vt_d[half * 64:(half + 1) * 64, c0 // 128:(c0 + W) // 128,
                                                           hh * 128:(hh + 1) * 128], reads=[vt_d], writes=[Vc])
                p.op("act", "activation", out=lf[:, :W], in_=ft[:, :W], func=AF.Ln, reads=[ft], writes=[lf])
                p.op("dve", "tensor_scalar", out=kk[:, :W], in0=ft[:, :W], scalar1=-1.0, scalar2=1.0, op0=ALU.mult,
                     op1=ALU.add, reads=[ft], writes=[kk])
                p.op("dve", "tensor_tensor", out=lfm[:, 1:W + 1], in0=lf[:, :W], in1=mF[:, :W], op=ALU.mult,
                     reads=[lf, mF], writes=[lfm])
                if dr == 0:
                    p.op("dve", "tensor_tensor_scan", out=bb[:, 0:W], data0=mF[:, 0:W], data1=lf[:, 0:W], initial=0.0,
                         op0=ALU.mult, op1=ALU.add, reads=[mF, lf], writes=[bb])
                    p.op("dve", "tensor_tensor_scan", out=cc[:, 0:W][:, ::-1], data0=mB[:, 0:W][:, ::-1],
                         data1=lfm[:, 2:W + 2][:, ::-1], initial=0.0, op0=ALU.mult, op1=ALU.add, reads=[mB, lfm],
                         writes=[cc])
                else:
                    p.op("dve", "tensor_tensor_scan", out=bb[:, 0:W][:, ::-1], data0=mF[:, 0:W][:, ::-1],
                         data1=lf[:, 0:W][:, ::-1], initial=0.0, op0=ALU.mult, op1=ALU.add, reads=[mF, lf], writes=[bb])
                    p.op("dve", "tensor_tensor_scan", out=cc[:, 0:W], data0=mB[:, 0:W], data1=lfm[:, 0:W], initial=0.0,
                         op0=ALU.mult, op1=ALU.add, reads=[mB, lfm], writes=[cc])
                p.op("act", "activation", out=eb[:, :W], in_=bb[:, :W], func=AF.Exp, reads=[bb], writes=[eb])
                for ci in range(nch):
                    apos = ci * 64 + (31 if dr == 0 else 32)
                    p.op("dve", "tensor_scalar", out=dd_[:, ci * 64:ci * 64 + 64], in0=bb[:, ci * 64:ci * 64 + 64],
                         scalar1=bb[:, apos:apos + 1], scalar2=None, op0=ALU.subtract, reads=[bb], writes=[dd_])
                p.op("act", "activation", out=ed[:, :W], in_=dd_[:, :W], func=AF.Exp, reads=[dd_], writes=[ed])
                p.op("act", "activation", out=enb[:, :W], in_=dd_[:, :W], func=AF.Exp, scale=-1.0, reads=[dd_],
                     writes=[enb])
                p.op("act", "activation", out=ec[:, :W], in_=cc[:, :W], func=AF.Exp, reads=[cc], writes=[ec])
                p.op("dve", "tensor_tensor", out=qe[:, :W], in0=qt[:, :W], in1=ed[:, :W], op=ALU.mult, reads=[qt, ed],
                     writes=[qe])
                p.op("pool", "tensor_tensor", out=ke[:, :W], in0=kk[:, :W], in1=enb[:, :W], op=ALU.mult, reads=[kk, enb],
                     writes=[ke])
                p.op("pool", "tensor_tensor", out=kd[:, :W], in0=kk[:, :W], in1=ec[:, :W], op=ALU.mult, reads=[kk, ec],
                     writes=[kd])
                for ci in range(nch):
                    pt = cx.bank()
                    mm(p, pt, pt[:64, :128], kd, kd[:, ci * 64:(ci + 1) * 64], ident, ident[:], True, True)
                    p.op("act", "activation", out=kdT[:, ci, :], in_=pt[:64, :128], func=AF.Copy, reads=[pt],
                         writes=[kdT])
                corder = list(range(nch)) if dr == 0 else list(range(nch - 1, -1, -1))
                if stage < 4:
                    continue
                for k_, ci in enumerate(corder):
                    cs = slice(ci * 64, ci * 64 + 64)
                    lastpos = ci * 64 + (63 if dr == 0 else 0)
                    f0 = 0 if dr == 0 else 32
                    s0 = 32 - f0
                    apos = ci * 64 + (31 if dr == 0 else 32)
                    ps1 = cx.bank()
                    mm(p, ps1, ps1[:64, s0:s0 + 32], ke, ke[:, cs], qe, qe[:, ci * 64 + s0:ci * 64 + s0 + 32], True, True)
                    pf = cx.bank()
                    mm(p, pf, pf[f0:f0 + 32, f0:f0 + 32], ke, ke[:, ci * 64 + f0:ci * 64 + f0 + 32], qe,
                       qe[:, ci * 64 + f0:ci * 64 + f0 + 32], True, True)
                    sm = scm[dr][k_ % 2]
                    p.op("dve", "tensor_tensor", out=sm[:, s0:s0 + 32], in0=ps1[:64, s0:s0 + 32], in1=tri[:, dr, s0:s0 + 32],
                         op=ALU.mult, reads=[ps1, tri], writes=[sm])
                    p.op("dve", "tensor_tensor", out=sm[f0:f0 + 32, f0:f0 + 32], in0=pf[f0:f0 + 32, f0:f0 + 32],
                         in1=tri[f0:f0 + 32, dr, f0:f0 + 32], op=ALU.mult, reads=[pf, tri], writes=[sm])
                    p.op("dve", "tensor_scalar", out=Sbf[:], in0=S32[:], scalar1=eb[:, apos:apos + 1], scalar2=None,
                         op0=ALU.mult, reads=[S32, eb], writes=[Sbf])
                    ps2 = cx.bank()
                    mm(p, ps2, ps2[:, :64], Sbf, Sbf[:], qe, qe[:, cs], True, False)
                    mm(p, ps2, ps2[:, :64], Vc, Vc[:, ci, :], sm, sm[:, :], False, True)
                    p.op("act", "activation", out=ot[:, cs], in_=ps2[:, :64], func=AF.Copy, reads=[ps2], writes=[ot])
                    ps3 = cx.bank()
                    mm(p, ps3, ps3[:, :128], kdT, kdT[:, ci, :], Vc, Vc[:, ci, :], True, True)
                    p.op("dve", "scalar_tensor_tensor", out=S32[:], in0=S32[:], scalar=eb[:, lastpos:lastpos + 1],
                         in1=ps3[:, :128], op0=ALU.mult, op1=ALU.add, reads=[S32, eb, ps3], writes=[S32])
                if dr == 0:
                    p.dma("pool", o_d[hh, :, c0:c0 + W], ot[:, :W], reads=[ot], writes=[o_d])
                else:
                    p.dma("sp", of[:, :W], o_d[hh, :, c0:c0 + W], reads=[o_d], writes=[of])
                    p.dma("sp", sgt[:, :W], sg_d[hh, :, c0:c0 + W], reads=[sg_d], writes=[sgt])
                    p.op("dve", "tensor_tensor", out=of[:, :W], in0=of[:, :W], in1=ot[:, :W], op=ALU.add, reads=[of, ot],
                         writes=[of])
                    p.op("act", "activation", out=osq[:, :W], in_=of[:, :W], func=AF.Square, reads=[of], writes=[osq])
                    ps4 = cx.bank()
                    mm(p, ps4, ps4[:, :W], cx.ones_bf, cx.ones_bf[:], osq, osq[:, :W], True, True)
                    p.op("act", "activation", out=ors[:, :W], in_=ps4[:, :W], func=AF.Sqrt, scale=1.0 / 128,
                         bias=cx.epsb[:], reads=[ps4, cx.epsb], writes=[ors])
                    p.op("dve", "reciprocal", out=ors[:, :W], in_=ors[:, :W], reads=[ors], writes=[ors])
                    p.op("dve", "scalar_tensor_tensor", out=oy[:, :W], in0=of[:, :W], scalar=nws[:, 0:1], in1=ors[:, :W],
                         op0=ALU.mult, op1=ALU.mult, reads=[of, nws, ors], writes=[oy])
                    o_b = ob[n % 2]
                    p.op("pool", "tensor_tensor", out=o_b[:, :W], in0=oy[:, :W], in1=sgt[:, :W], op=ALU.mult,
                         reads=[oy, sgt], writes=[o_b])
                    st_mix(p, d, hh, o_b, o_b[:, :W], c0, W, mc)
    p.close_scope()


def a1_dram(p, S, SL, pre="", fused=False):
    dd = {
        "svec": p.dram(pre + "svec", [128, NCH, 2], F32, kind="ExternalInput"),
        "modw": p.dram(pre + "modw", [2, NCH, 128, NCH, 128], F32, kind="ExternalInput"),
        "modb": p.dram(pre + "modb", [128, 2, NCH], F32, kind="ExternalInput"),
        "normw": p.dram(pre + "normw", [128, NCH], F32, kind="ExternalInput"),
        "win": p.dram(pre + "win", [128, NCH, 1792], F32, kind="ExternalInput"),
        "rmat": p.dram(pre + "rmat", [128, 128], F32, kind="ExternalInput"),
        "ident": p.dram(pre + "ident", [128, 128], F32, kind="ExternalInput"),
        "nws": p.dram(pre + "nws", [128, 3], F32, kind="ExternalInput"),
        "lbl": p.dram(pre + "lbl", [128, 2, 2, 2], F32, kind="ExternalInput"),
        "tri": p.dram(pre + "tri", [64, 2, 64], F32, kind="ExternalInput"),
        "cos": p.dram(pre + "cos", [128, SL], F32, kind="ExternalInput"),
        "sin": p.dram(pre + "sin", [128, SL], F32, kind="ExternalInput"),
    }
    dd["mixo"] = p.dram(pre + "mixo", [4, 128, S], BF16, kind="Internal" if fused else "ExternalOutput")
    if not fused:
        dd["xa"] = p.dram(pre + "xa", [128, NCH, S], F32, kind="ExternalInput")
    return dd


def make_A1(S, CT, stage=9):
    nc = bass.Bass("TRN2", target_bir_lowering=False)
    p = Prog(nc)
    cx = Ctx(p)
    d = a1_dram(p, S, S - CT)
    build_A1(p, cx, d, S, CT, stage)
    p.finish([d["mixo"]])
    return nc, p


def host_A1_inputs(j, c_b, c_ctx, mod_w, mod_b, norm_mix_w, cd_w_in, lb_logits, hgrn_norm_w, q_norm_w, k_norm_w, SL):
    l = 1
    comps = [0, 1]
    modw = np.stack([blk_lhsT(mod_w[l][:, k * D:(k + 1) * D]) for k in comps])
    modb = np.stack([fm(mod_b[l][k * D:(k + 1) * D]) for k in comps], axis=1)
    svec = np.stack([fm(c_b), fm(c_ctx)], axis=2)
    W = cd_w_in[0]
    r = np.arange(256 * j, 256 * j + 256)
    g = j // 2
    cols = np.concatenate([r, 1024 + r, 2048 + r, 4096 + r, 5120 + r, 6144 + np.arange(128 * g, 128 * g + 128),
                           3072 + r, 6400 + np.arange(128 * g, 128 * g + 128)])
    win = np.ascontiguousarray(W[:, cols].reshape(NCH, 128, 1792).transpose(1, 0, 2))
    heads = [2 * j, 2 * j + 1]
    lbl = np.stack([lb_logits[:, :, h * 128:(h + 1) * 128] for h in heads], axis=3)
    lbl = np.ascontiguousarray(lbl.transpose(2, 0, 1, 3))
    cos, sin = rope_tables(SL, 128)
    s_ = np.arange(64)[:, None]
    t_ = np.arange(64)[None, :]
    tri = np.stack([(s_ <= t_), (s_ >= t_)], axis=1).astype(np.float32)
    return {
        "svec": np.ascontiguousarray(svec), "modw": modw, "modb": np.ascontiguousarray(modb),
        "normw": fm(norm_mix_w[l]), "win": win, "rmat": rot_matrix(128), "ident": np.eye(128, dtype=np.float32),
        "nws": np.ascontiguousarray(np.stack([hgrn_norm_w[0], q_norm_w[0], k_norm_w[0]], axis=1)),
        "lbl": lbl, "tri": np.ascontiguousarray(tri), "cos": cos, "sin": sin,
    }


SEQ = 16384
CTX = 256
NCORE = 8
_CACHE = {}


def _prog(key, fn):
    if key not in _CACHE:
        _CACHE[key] = fn()
    return _CACHE[key]


def _dbg(tag, arrs):
    import os, sys
    if not os.environ.get("KDEBUG"):
        return
    for i, a in enumerate(arrs):
        a = np.asarray(a, dtype=np.float32)
        bad = ~np.isfinite(a)
        msg = "[kdebug] %s core %d nonfinite=%d rms=%.4g" % (tag, i, int(bad.sum()), float(np.sqrt(np.mean(np.where(bad, 0, a) ** 2))))
        if bad.any():
            idx = np.argwhere(bad)
            msg += " first=%s chunks=%s" % (idx[0].tolist(), sorted(set(idx[:, 0].tolist()))[:8])
        print(msg, file=sys.stderr, flush=True)


def _from_fm(a):
    P, C, N = a.shape
    return a.transpose(2, 1, 0).reshape(N, C * P)


def kernel_unfused(x, c, ctx, c_ctx, mod_w, mod_b, norm_mix_w, norm_ffn_w, ffn_w_gate, ffn_w_up, ffn_w_down,
           ab_w_in, ab_w_out, lru_conv_w, lru_conv_b, lru_wa, lru_ba, lru_wx, lru_bx, lru_lambda,
           diff_lq1, diff_lk1, diff_lq2, diff_lk2, diff_subln_w,
           cd_w_in, cd_w_out, hgrn_lb_logits, hgrn_norm_w, gqa_q_norm_w, gqa_k_norm_w, final_norm_w):
    f = lambda a: np.asarray(a, dtype=np.float32)
    (x, c, ctx, c_ctx, mod_w, mod_b, norm_mix_w, norm_ffn_w, ffn_w_gate, ffn_w_up, ffn_w_down, ab_w_in, ab_w_out,
     lru_conv_w, lru_conv_b, lru_wa, lru_ba, lru_wx, lru_bx, lru_lambda, diff_lq1, diff_lk1, diff_lq2, diff_lk2,
     diff_subln_w, cd_w_in, cd_w_out, hgrn_lb_logits, hgrn_norm_w, gqa_q_norm_w, gqa_k_norm_w, final_norm_w) = map(f, (
        x, c, ctx, c_ctx, mod_w, mod_b, norm_mix_w, norm_ffn_w, ffn_w_gate, ffn_w_up, ffn_w_down, ab_w_in, ab_w_out,
        lru_conv_w, lru_conv_b, lru_wa, lru_ba, lru_wx, lru_bx, lru_lambda, diff_lq1, diff_lk1, diff_lq2, diff_lk2,
        diff_subln_w, cd_w_in, cd_w_out, hgrn_lb_logits, hgrn_norm_w, gqa_q_norm_w, gqa_k_norm_w, final_norm_w))
    B = x.shape[0]
    SL = x.shape[1]
    CT = ctx.shape[1]
    S = CT + SL
    QL = SL // 4
    QC = CT // 4
    cores = list(range(NCORE))

    def mix_assemble(res, with_ctx):
        outs = []
        for core in cores:
            b, q = core // 4, core % 4
            full = np.empty((NCH, 128, S), dtype=ml_dtypes.bfloat16)
            for j in range(4):
                o = res[b * 4 + j]["mixo"]
                full[2 * j] = o[0]
                full[2 * j + 1] = o[1]
                full[8 + 2 * j] = o[2]
                full[8 + 2 * j + 1] = o[3]
            parts = [full[:, :, CT + q * QL:CT + (q + 1) * QL]]
            if with_ctx:
                parts.append(full[:, :, q * QC:(q + 1) * QC])
            outs.append(np.ascontiguousarray(np.concatenate(parts, axis=2).transpose(1, 0, 2)))
        return outs

    nc, _ = _prog(("A0", S, CT), lambda: make_A0(S, CT))
    xa = [fm_act(np.concatenate([ctx[b], x[b]], 0).T) for b in range(B)]
    maps = []
    for core in cores:
        b, j = core // 4, core % 4
        h = host_A0_inputs(j, c[b], c_ctx, mod_w, mod_b, norm_mix_w, ab_w_in, lru_conv_w, lru_conv_b, lru_wa, lru_ba,
                           lru_wx, lru_bx, lru_lambda, diff_lq1, diff_lk1, diff_lq2, diff_lk2, diff_subln_w, SL)
        h["xa"] = xa[b]
        maps.append(h)
    resA0 = run_bass_kernel_spmd(nc, maps, core_ids=cores).results
    del maps
    _dbg("A0 mixo", [r["mixo"] for r in resA0])
    NT0 = QL + QC
    tiles0 = [(i * 512, 512, 0) for i in range(QL // 512)] + [(QL, QC, 1)]
    nc, _ = _prog(("B", NT0, 0), lambda: make_B(NT0, tiles0, False))
    mixs = mix_assemble(resA0, True)
    del resA0
    wB = [host_B_inputs(0, c[b], c_ctx, mod_w, mod_b, norm_ffn_w, ab_w_out[0], ffn_w_gate, ffn_w_up, ffn_w_down)
          for b in range(B)]
    maps = []
    for core in cores:
        b, q = core // 4, core % 4
        h = dict(wB[b])
        xr = np.concatenate([x[b][q * QL:(q + 1) * QL], ctx[b][q * QC:(q + 1) * QC]], 0)
        h["xres"] = fm_act(xr.T)
        h["mix"] = mixs[core]
        maps.append(h)
    resB0 = run_bass_kernel_spmd(nc, maps, core_ids=cores).results
    del maps, wB, mixs
    _dbg("B0 xout", [r["xout"] for r in resB0])
    nc, _ = _prog(("A1", S, CT), lambda: make_A1(S, CT))
    x1q = [resB0[core]["xout"] for core in cores]
    del resB0
    xa = []
    for b in range(B):
        lat = np.concatenate([x1q[b * 4 + q][:, :, :QL] for q in range(4)], axis=2)
        cc = np.concatenate([x1q[b * 4 + q][:, :, QL:] for q in range(4)], axis=2)
        xa.append(np.ascontiguousarray(np.concatenate([cc, lat], axis=2)))
    maps = []
    for core in cores:
        b, j = core // 4, core % 4
        h = host_A1_inputs(j, c[b], c_ctx, mod_w, mod_b, norm_mix_w, cd_w_in, hgrn_lb_logits, hgrn_norm_w,
                           gqa_q_norm_w, gqa_k_norm_w, SL)
        h["xa"] = xa[b]
        maps.append(h)
    resA1 = run_bass_kernel_spmd(nc, maps, core_ids=cores).results
    del maps, xa
    _dbg("A1 mixo", [r["mixo"][:, :, CT:] for r in resA1])
    tiles1 = [(i * 512, 512, 0) for i in range(QL // 512)]
    nc, _ = _prog(("B", QL, 1), lambda: make_B(QL, tiles1, True))
    mixs = mix_assemble(resA1, False)
    del resA1
    wB = [host_B_inputs(1, c[b], c_ctx, mod_w, mod_b, norm_ffn_w, cd_w_out[0], ffn_w_gate, ffn_w_up, ffn_w_down,
                        final_norm_w) for b in range(B)]
    maps = []
    for core in cores:
        b, q = core // 4, core % 4
        h = dict(wB[b])
        h["xres"] = np.ascontiguousarray(x1q[core][:, :, :QL])
        h["mix"] = mixs[core]
        maps.append(h)
    resB1 = run_bass_kernel_spmd(nc, maps, core_ids=cores).results
    out = np.empty((B, SL, D), dtype=np.float32)
    for core in cores:
        b, q = core // 4, core % 4
        out[b, q * QL:(q + 1) * QL] = _from_fm(resB1[core]["xout"])
    return out


GROUPS = [[0, 1, 2, 3], [4, 5, 6, 7]]


def b_dram(p, pre, last):
    d = {
        "svec": p.dram(pre + "svec", [128, NCH, 2], F32, kind="ExternalInput"),
        "modw": p.dram(pre + "modw", [4, NCH, 128, NCH, 128], F32, kind="ExternalInput"),
        "modb": p.dram(pre + "modb", [128, 4, NCH], F32, kind="ExternalInput"),
        "normw": p.dram(pre + "normw", [128, NCH], F32, kind="ExternalInput"),
        "wout": p.dram(pre + "wout", [NCH, 128, NCH, 128], F32, kind="ExternalInput"),
        "wgu": p.dram(pre + "wgu", [NJ, 128, 2, NCH, 128], F32, kind="ExternalInput"),
        "wdn": p.dram(pre + "wdn", [NCH, 128, NJ, 128], F32, kind="ExternalInput"),
    }
    if last:
        d["finalw"] = p.dram(pre + "finalw", [128, NCH], F32, kind="ExternalInput")
    return d


def fc_src(fc):
    if fc < 8:
        return fc % 2, fc // 2
    return 2 + (fc - 8) % 2, (fc - 8) // 2


def make_fused(S, CT):
    SL = S - CT
    QL = SL // 4
    QC = CT // 4
    HL = QL // 2
    nc = bass.Bass("TRN2", target_bir_lowering=False)
    p = Prog(nc)
    cx = Ctx(p)
    oh_d = p.dram("oh", [128, 4], F32, kind="ExternalInput")
    oh = p.sbuf("oh", [128, 4], F32)
    p.dma("sp", oh[:], oh_d[:, :], writes=[oh])

    def mix_loader(Gl, Gc):
        def ld(mx, c0, W, mc):
            FG = 2
            for grp in range(NCH // FG):
                buf = ld.buf
                for qq in range(4):
                    for f_ in range(FG):
                        ch, rk = fc_src(grp * FG + f_)
                        if mc == 0:
                            src = Gl[ch, qq, rk * 128:(rk + 1) * 128, c0:c0 + W]
                            p.dma("sp", buf[:, qq, f_, :W], src, reads=[Gl], writes=[buf])
                        else:
                            src = Gc[ch, rk * 128:(rk + 1) * 128, qq * QC:(qq + 1) * QC]
                            p.dma("sp", buf[:, qq, f_, :W], src, reads=[Gc], writes=[buf])
                dst = mx[:, grp * FG:(grp + 1) * FG, :W]
                p.op("dve", "tensor_scalar", out=dst, in0=buf[:, 0, :, :W], scalar1=oh[:, 0:1], scalar2=None,
                     op0=ALU.mult, reads=[buf, oh], writes=[mx])
                for qq in range(1, 4):
                    p.op("dve", "scalar_tensor_tensor", out=dst, in0=buf[:, qq, :, :W], scalar=oh[:, qq:qq + 1], in1=dst,
                         op0=ALU.mult, op1=ALU.add, reads=[buf, oh, mx], writes=[mx])
        return ld

    dB0 = b_dram(p, "b0_", False)
    dB1 = b_dram(p, "b1_", True)
    prep_B_weights(p, dB0)
    prep_B_weights(p, dB1)

    def mix_store(mixc, mixl):
        def st(ch, src_t, src_ap, c0, W, mc):
            if mc == 1:
                p.dma("pool", mixc[ch, :, c0:c0 + W], src_ap, reads=[src_t], writes=[mixc])
            else:
                col = c0 - CT
                p.dma("pool", mixl[ch, col // QL, :, col % QL:col % QL + W], src_ap, reads=[src_t], writes=[mixl])
        return st

    dA0 = a0_dram(p, S, SL, pre="a0_", fused=True)
    m0c = p.dram("m0c", [4, 128, CT], BF16)
    m0l = p.dram("m0l", [4, 4, 128, QL], BF16)
    dA0["st_mix"] = mix_store(m0c, m0l)
    p.open_scope()
    build_A0(p, cx, dA0, S, CT, 0.8 - 0.6 * math.exp(0.0))
    p.close_scope()
    G1l = p.dram("G1l", [4, 4, 512, QL], BF16)
    G1c = p.dram("G1c", [4, 512, CT], BF16)
    for ch in range(4):
        p.coll(m0c, m0c[ch].opt(), G1c, G1c[ch].opt(), GROUPS)
        for qq in range(4):
            p.coll(m0l, m0l[ch, qq].opt(), G1l, G1l[ch, qq].opt(), GROUPS)
    x1_d = p.dram("x1_d", [NCH, 2, 128, HL], F32)
    xc1_d = p.dram("xc1_d", [128, NCH, QC], F32)
    dB0["xres"] = p.dram("b0_xres", [128, NCH, QL + QC], F32, kind="ExternalInput")
    p.open_scope()
    ld0 = mix_loader(G1l, G1c)
    ld0.buf = p.sbuf("selbuf", [128, 4, 2, 512], BF16)
    dB0["ld_mix"] = ld0

    def st0(x1, c0, W, mc):
        if mc == 0:
            p.dma("pool", x1_d[:, c0 // HL, :, c0 % HL:c0 % HL + W].rearrange("c p w -> p c w"), x1[:, :, :W],
                  reads=[x1], writes=[x1_d])
        else:
            p.dma("pool", xc1_d[:, :, :], x1[:, :, :W], reads=[x1], writes=[xc1_d])
    dB0["st_x"] = st0
    tiles0 = [(i * 512, 512, 0) for i in range(QL // 512)] + [(QL, QC, 1)]
    build_B(p, cx, dB0, tiles0, False)
    p.close_scope()
    G2l = p.dram("G2l", [NCH, 2, 512, HL], F32)
    G2c = p.dram("G2c", [512, NCH * QC], F32)
    for c in range(NCH):
        for hf in range(2):
            p.coll(x1_d, x1_d[c, hf].opt(), G2l, G2l[c, hf].opt(), GROUPS)
    p.coll(xc1_d, xc1_d[:, :, :].rearrange("p c w -> p (c w)").opt(), G2c, G2c[:, :].opt(), GROUPS)
    dA1 = a1_dram(p, S, SL, pre="a1_", fused=True)

    def ld_xa1(xt, c0, W, mc):
        if mc == 1:
            for r in range(4):
                p.dma("sp", xt[:, :, r * QC:(r + 1) * QC],
                      G2c[r * 128:(r + 1) * 128, :].rearrange("p (c w) -> p c w", c=NCH), reads=[G2c], writes=[xt])
        else:
            col = c0 - CT
            r, within = col // QL, col % QL
            hf, off = within // HL, within % HL
            for c in range(NCH):
                p.dma("sp", xt[:, c, :W], G2l[c, hf, r * 128:(r + 1) * 128, off:off + W], reads=[G2l], writes=[xt])
    dA1["ld_xa"] = ld_xa1
    m1c = p.dram("m1c", [4, 128, CT], BF16)
    m1l = p.dram("m1l", [4, 4, 128, QL], BF16)
    dA1["st_mix"] = mix_store(m1c, m1l)
    p.open_scope()
    build_A1(p, cx, dA1, S, CT)
    p.close_scope()
    G3l = p.dram("G3l", [4, 4, 512, QL], BF16)
    for ch in range(4):
        for qq in range(4):
            p.coll(m1l, m1l[ch, qq].opt(), G3l, G3l[ch, qq].opt(), GROUPS)
    dB1["xout"] = p.dram("b1_xout", [128, NCH, QL], F32, kind="ExternalOutput")
    p.open_scope()
    ld1 = mix_loader(G3l, None)
    ld1.buf = p.sbuf("selbuf1", [128, 4, 2, 512], BF16)
    dB1["ld_mix"] = ld1

    def ldx1(x1, c0, W, mc):
        p.dma("sp", x1[:, :, :W], x1_d[:, c0 // HL, :, c0 % HL:c0 % HL + W].rearrange("c p w -> p c w"),
              reads=[x1_d], writes=[x1])
    dB1["ld_x"] = ldx1
    tiles1 = [(i * 512, 512, 0) for i in range(QL // 512)]
    build_B(p, cx, dB1, tiles1, True)
    p.close_scope()
    p.finish([dB1["xout"]])
    return nc, p


def kernel(x, c, ctx, c_ctx, mod_w, mod_b, norm_mix_w, norm_ffn_w, ffn_w_gate, ffn_w_up, ffn_w_down,
           ab_w_in, ab_w_out, lru_conv_w, lru_conv_b, lru_wa, lru_ba, lru_wx, lru_bx, lru_lambda,
           diff_lq1, diff_lk1, diff_lq2, diff_lk2, diff_subln_w,
           cd_w_in, cd_w_out, hgrn_lb_logits, hgrn_norm_w, gqa_q_norm_w, gqa_k_norm_w, final_norm_w):
    f = lambda a: np.asarray(a, dtype=np.float32)
    (x, c, ctx, c_ctx, mod_w, mod_b, norm_mix_w, norm_ffn_w, ffn_w_gate, ffn_w_up, ffn_w_down, ab_w_in, ab_w_out,
     lru_conv_w, lru_conv_b, lru_wa, lru_ba, lru_wx, lru_bx, lru_lambda, diff_lq1, diff_lk1, diff_lq2, diff_lk2,
     diff_subln_w, cd_w_in, cd_w_out, hgrn_lb_logits, hgrn_norm_w, gqa_q_norm_w, gqa_k_norm_w, final_norm_w) = map(f, (
        x, c, ctx, c_ctx, mod_w, mod_b, norm_mix_w, norm_ffn_w, ffn_w_gate, ffn_w_up, ffn_w_down, ab_w_in, ab_w_out,
        lru_conv_w, lru_conv_b, lru_wa, lru_ba, lru_wx, lru_bx, lru_lambda, diff_lq1, diff_lk1, diff_lq2, diff_lk2,
        diff_subln_w, cd_w_in, cd_w_out, hgrn_lb_logits, hgrn_norm_w, gqa_q_norm_w, gqa_k_norm_w, final_norm_w))
    B, SL = x.shape[0], x.shape[1]
    CT = ctx.shape[1]
    S = CT + SL
    QL, QC = SL // 4, CT // 4
    cores = list(range(NCORE))
    nc, _ = _prog(("fused", S, CT), lambda: make_fused(S, CT))
    xa = [fm_act(np.concatenate([ctx[b], x[b]], 0).T) for b in range(B)]
    wB0 = [host_B_inputs(0, c[b], c_ctx, mod_w, mod_b, norm_ffn_w, ab_w_out[0], ffn_w_gate, ffn_w_up, ffn_w_down)
           for b in range(B)]
    wB1 = [host_B_inputs(1, c[b], c_ctx, mod_w, mod_b, norm_ffn_w, cd_w_out[0], ffn_w_gate, ffn_w_up, ffn_w_down,
                         final_norm_w) for b in range(B)]
    maps = []
    for core in cores:
        b, j = core // 4, core % 4
        m = {}
        hA0 = host_A0_inputs(j, c[b], c_ctx, mod_w, mod_b, norm_mix_w, ab_w_in, lru_conv_w, lru_conv_b, lru_wa,
                             lru_ba, lru_wx, lru_bx, lru_lambda, diff_lq1, diff_lk1, diff_lq2, diff_lk2,
                             diff_subln_w, SL)
        hA0["xa"] = xa[b]
        for k, v in hA0.items():
            m["a0_" + k] = v
        for k, v in wB0[b].items():
            m["b0_" + k] = v
        xr = np.concatenate([x[b][j * QL:(j + 1) * QL], ctx[b][j * QC:(j + 1) * QC]], 0)
        m["b0_xres"] = fm_act(xr.T)
        hA1 = host_A1_inputs(j, c[b], c_ctx, mod_w, mod_b, norm_mix_w, cd_w_in, hgrn_lb_logits, hgrn_norm_w,
                             gqa_q_norm_w, gqa_k_norm_w, SL)
        for k, v in hA1.items():
            m["a1_" + k] = v
        for k, v in wB1[b].items():
            m["b1_" + k] = v
        ohv = np.zeros((128, 4), np.float32)
        ohv[:, j] = 1.0
        m["oh"] = ohv
        maps.append(m)
    res = run_bass_kernel_spmd(nc, maps, core_ids=cores).results
    out = np.empty((B, SL, D), dtype=np.float32)
    for core in cores:
        b, q = core // 4, core % 4
        out[b, q * QL:(q + 1) * QL] = _from_fm(res[core]["b1_xout"])
    return out
```
